# Optimizing a Trainium2 kernel written in Bass

```python
import math
import jax, jax.numpy as jnp
from jax import lax
import numpy as np

D_MODEL = 1024
BATCH = 8
SEQ = 8192
DEPTH = 1
DEC_BATCH = 2
DEC_SEQ = 16384
PAST_LEN = 128

CONV_WIDTH = D_MODEL // 2
CONV_K = 3
ATTN_HEADS = 4
ATTN_HEAD_DIM = 64
ATTN_V_DIM = 2 * ATTN_HEAD_DIM
ATTN_WIDTH = ATTN_HEADS * ATTN_V_DIM
QK_WIDTH = ATTN_HEADS * 2 * ATTN_HEAD_DIM
MIX_WIDTH = CONV_WIDTH + ATTN_WIDTH
IN_COLS = 3 * CONV_WIDTH + 2 * QK_WIDTH + ATTN_WIDTH
ROPE_DIM = ATTN_HEAD_DIM // 4
ROPE_THETA = 500000.0
Q_BLOCK = 128
MEM_TOKENS = 256
MEM_HEADS = 4
MEM_HEAD_DIM = 128
MEM_WIDTH = MEM_HEADS * MEM_HEAD_DIM
D_FF = 4 * D_MODEL
EPS = 1e-6

kernel_name = 'hybrid_conv_diffattn_encoder'


def rmsnorm(x, g):
    xf = x.astype(jnp.float32)
    y = xf * lax.rsqrt(jnp.mean(xf * xf, axis=-1, keepdims=True) + EPS) * g.astype(jnp.float32)
    return y.astype(x.dtype)


def rope_partial(t, cos, sin):
    half = ROPE_DIM // 2
    c = cos[None, :, None, None, :].astype(t.dtype)
    s = sin[None, :, None, None, :].astype(t.dtype)
    r1 = t[..., :half]
    r2 = t[..., half:ROPE_DIM]
    return jnp.concatenate([r1 * c - r2 * s, r2 * c + r1 * s, t[..., ROPE_DIM:]], axis=-1)


def short_conv_mixer(u_b, u_c, u_v, w, b):
    S = u_v.shape[1]
    z = u_c * u_v
    zp = jnp.pad(z, ((0, 0), (1, 1), (0, 0)))
    conv = zp[:, :S] * w[0] + zp[:, 1:S + 1] * w[1] + zp[:, 2:] * w[2] + b
    return u_b * conv


def diff_attention(q, k, v, lam):
    B, S, H, _, dh = q.shape
    nb = S // Q_BLOCK
    scale = dh ** -0.5
    qb = q.reshape(B, nb, Q_BLOCK, H, 2, dh).transpose(1, 0, 2, 3, 4, 5)

    def block(qi):
        s = jnp.einsum('bqhcd,bkhcd->bhcqk', qi, k).astype(jnp.float32) * scale
        p = jax.nn.softmax(s, axis=-1)
        a = p[:, :, 0] - lam * p[:, :, 1]
        return jnp.einsum('bhqk,bkhe->bqhe', a.astype(v.dtype), v)

    o = lax.map(block, qb)
    return o.transpose(1, 0, 2, 3, 4).reshape(B, S, H, v.shape[-1])


def memory_attention(h, m, w_q, w_kv, w_o, gq, gk):
    B, S, _ = h.shape
    M = m.shape[1]
    q = (h @ w_q).reshape(B, S, MEM_HEADS, MEM_HEAD_DIM)
    kv = (m @ w_kv).reshape(B, M, 2, MEM_HEADS, MEM_HEAD_DIM)
    k = rmsnorm(kv[:, :, 0], gk)
    v = kv[:, :, 1]
    q = rmsnorm(q, gq)
    s = jnp.einsum('bqhd,bkhd->bhqk', q, k).astype(jnp.float32) * (MEM_HEAD_DIM ** -0.5)
    p = jax.nn.softmax(s, axis=-1)
    o = jnp.einsum('bhqk,bkhd->bqhd', p.astype(v.dtype), v).reshape(B, S, MEM_WIDTH)
    return o @ w_o


def encoder_trunk(x, mem, g_mix, w_in, conv_w, conv_b, q_norm, k_norm, lambda_q1, lambda_k1,
                  lambda_q2, lambda_k2, g_subln, w_out, g_memq, g_memkv, wm_q, wm_kv,
                  q_norm_mem, k_norm_mem, wm_o, g_mlp, w_ff1, w_ff2):
    B, S, _ = x.shape
    pos = jnp.arange(S, dtype=jnp.float32)
    inv_freq = ROPE_THETA ** (-jnp.arange(0, ROPE_DIM, 2, dtype=jnp.float32) / ROPE_DIM)
    ang = pos[:, None] * inv_freq[None, :]
    cos, sin = jnp.cos(ang), jnp.sin(ang)
    splits = [CONV_WIDTH, 2 * CONV_WIDTH, 3 * CONV_WIDTH,
              3 * CONV_WIDTH + QK_WIDTH, 3 * CONV_WIDTH + 2 * QK_WIDTH]
    for l in range(DEPTH):
        h = rmsnorm(x, g_mix[l])
        u = h @ w_in[l]
        u_b, u_c, u_v, q, k, v = jnp.split(u, splits, axis=-1)
        y_conv = short_conv_mixer(u_b, u_c, u_v, conv_w[l], conv_b[l])

        q = q.reshape(B, S, ATTN_HEADS, 2, ATTN_HEAD_DIM)
        k = k.reshape(B, S, ATTN_HEADS, 2, ATTN_HEAD_DIM)
        v = v.reshape(B, S, ATTN_HEADS, ATTN_V_DIM)
        q = rope_partial(rmsnorm(q, q_norm[l]), cos, sin)
        k = rope_partial(rmsnorm(k, k_norm[l]), cos, sin)
        lam_init = 0.8 - 0.6 * math.exp(-0.3 * l)
        lam = (jnp.exp(jnp.sum(lambda_q1[l].astype(jnp.float32) * lambda_k1[l].astype(jnp.float32)))
               - jnp.exp(jnp.sum(lambda_q2[l].astype(jnp.float32) * lambda_k2[l].astype(jnp.float32)))
               + lam_init)
        o = diff_attention(q, k, v, lam)
        o = rmsnorm(o, g_subln[l]) * (1.0 - lam_init)
        mix = jnp.concatenate([y_conv, o.reshape(B, S, ATTN_WIDTH)], axis=-1)
        x = x + mix @ w_out[l]

        x = x + memory_attention(rmsnorm(x, g_memq[l]), rmsnorm(mem, g_memkv[l]),
                                 wm_q[l], wm_kv[l], wm_o[l], q_norm_mem[l], k_norm_mem[l])

        hf = rmsnorm(x, g_mlp[l]) @ w_ff1[l]
        x = x + jnp.square(jax.nn.relu(hf)) @ w_ff2[l]
    return x


def setup_inputs(seed: int = 0) -> dict:
    key = jax.random.key(seed)
    ks = jax.random.split(key, 32)
    f32 = jnp.float32

    def nrm(k, shape, scale):
        return jax.random.normal(k, shape, f32) * scale

    def gain(k, n):
        return 1.0 + 0.02 * jax.random.normal(k, (DEPTH, n), f32)

    return {
        'x_prompt': nrm(ks[0], (BATCH, SEQ, D_MODEL), 1.0),
        'x_sample': nrm(ks[1], (DEC_BATCH, DEC_SEQ, D_MODEL), 1.0),
        'mem_prompt': nrm(ks[2], (BATCH, MEM_TOKENS, D_MODEL), 1.0),
        'mem_sample': nrm(ks[3], (DEC_BATCH, MEM_TOKENS, D_MODEL), 1.0),
        'g_mix': gain(ks[4], D_MODEL),
        'w_in': nrm(ks[5], (DEPTH, D_MODEL, IN_COLS), D_MODEL ** -0.5),
        'conv_w': nrm(ks[6], (DEPTH, CONV_K, CONV_WIDTH), CONV_K ** -0.5),
        'conv_b': nrm(ks[7], (DEPTH, CONV_WIDTH), 0.02),
        'q_norm': gain(ks[8], ATTN_HEAD_DIM),
        'k_norm': gain(ks[9], ATTN_HEAD_DIM),
        'lambda_q1': nrm(ks[10], (DEPTH, ATTN_HEAD_DIM), 0.1),
        'lambda_k1': nrm(ks[11], (DEPTH, ATTN_HEAD_DIM), 0.1),
        'lambda_q2': nrm(ks[12], (DEPTH, ATTN_HEAD_DIM), 0.1),
        'lambda_k2': nrm(ks[13], (DEPTH, ATTN_HEAD_DIM), 0.1),
        'g_subln': gain(ks[14], ATTN_V_DIM),
        'w_out': nrm(ks[15], (DEPTH, MIX_WIDTH, D_MODEL), MIX_WIDTH ** -0.5),
        'g_memq': gain(ks[16], D_MODEL),
        'g_memkv': gain(ks[17], D_MODEL),
        'wm_q': nrm(ks[18], (DEPTH, D_MODEL, MEM_WIDTH), D_MODEL ** -0.5),
        'wm_kv': nrm(ks[19], (DEPTH, D_MODEL, 2 * MEM_WIDTH), D_MODEL ** -0.5),
        'q_norm_mem': gain(ks[20], MEM_HEAD_DIM),
        'k_norm_mem': gain(ks[21], MEM_HEAD_DIM),
        'wm_o': nrm(ks[22], (DEPTH, MEM_WIDTH, D_MODEL), MEM_WIDTH ** -0.5),
        'g_mlp': gain(ks[23], D_MODEL),
        'w_ff1': nrm(ks[24], (DEPTH, D_MODEL, D_FF), D_MODEL ** -0.5),
        'w_ff2': nrm(ks[25], (DEPTH, D_FF, D_MODEL), D_FF ** -0.5),
    }


def reference(x_prompt, x_sample, mem_prompt, mem_sample, g_mix, w_in, conv_w, conv_b, q_norm, k_norm,
              lambda_q1, lambda_k1, lambda_q2, lambda_k2, g_subln, w_out, g_memq, g_memkv, wm_q, wm_kv,
              q_norm_mem, k_norm_mem, wm_o, g_mlp, w_ff1, w_ff2):
    y_prompt = encoder_trunk(x_prompt, mem_prompt, g_mix, w_in, conv_w, conv_b, q_norm, k_norm,
                             lambda_q1, lambda_k1, lambda_q2, lambda_k2, g_subln, w_out, g_memq, g_memkv,
                             wm_q, wm_kv, q_norm_mem, k_norm_mem, wm_o, g_mlp, w_ff1, w_ff2)
    y_sample = encoder_trunk(x_sample, mem_sample, g_mix, w_in, conv_w, conv_b, q_norm, k_norm,
                             lambda_q1, lambda_k1, lambda_q2, lambda_k2, g_subln, w_out, g_memq, g_memkv,
                             wm_q, wm_kv, q_norm_mem, k_norm_mem, wm_o, g_mlp, w_ff1, w_ff2)
    return (y_prompt, y_sample)
```

```python
import math
import os
import heapq
import types
from contextlib import ExitStack

import numpy as np
import ml_dtypes
import concourse.bass as bass
import concourse.mybir as mybir
from concourse.alu_op_type import AluOpType as ALU
from concourse.bass_utils import run_bass_kernel_spmd

F32 = mybir.dt.float32
BF16 = mybir.dt.bfloat16
AF = mybir.ActivationFunctionType
AX = mybir.AxisListType

D = 1024
EPS = 1e-6
LAM_INIT = 0.8 - 0.6 * math.exp(-0.3 * 0)
MEMT = 256
KVCAP = 32768


class Buf:
    __slots__ = ("w", "r")

    def __init__(self):
        self.w = None
        self.r = []


class Lane:
    def __init__(self, idx, sem):
        self.idx = idx
        self.sem = sem
        self.count = 0


class Node:
    __slots__ = ("q", "fns", "deps", "cost", "lat", "prio", "succ", "nd", "ready", "start", "finish", "token", "dma", "noncontig")

    def __init__(self, q, fns, deps, cost, lat, prio, dma=None, noncontig=False):
        self.q, self.fns, self.deps, self.cost, self.lat, self.prio = q, fns, deps, cost, lat, prio
        self.succ = []
        self.nd = 0
        self.ready = 0.0
        self.start = self.finish = 0.0
        self.token = None
        self.dma = dma
        self.noncontig = noncontig


def _freeze(fn, depth=0):
    if not isinstance(fn, types.FunctionType) or fn.__closure__ is None or depth > 2:
        return fn
    cells = []
    for c in fn.__closure__:
        try:
            v = c.cell_contents
        except ValueError:
            cells.append(c)
            continue
        if isinstance(v, types.FunctionType) and v.__closure__ is not None:
            v = _freeze(v, depth + 1)
        cells.append(types.CellType(v))
    g = types.FunctionType(fn.__code__, fn.__globals__, fn.__name__, fn.__defaults__, tuple(cells))
    g.__kwdefaults__ = fn.__kwdefaults__
    return g


_FIX = {"pe": 30.0, "act": 240.0, "dve": 280.0, "pool": 300.0, "sp": 100.0}
_PER = {"act": 0.833, "dve": 1.05, "pool": 2.1}


class Sched:
    def __init__(self, nc, st, nlanes=12):
        self.nc = nc
        self.eng = {"pe": nc.tensor, "act": nc.scalar, "dve": nc.vector, "pool": nc.gpsimd, "sp": nc.sync}
        self.sem = {k: st.enter_context(nc.semaphore("s_" + k)) for k in self.eng}
        self.count = {k: 0 for k in self.eng}
        self.waited = {k: {} for k in self.eng}
        self.lanes = [Lane(i, st.enter_context(nc.semaphore("s_l%d" % i))) for i in range(nlanes)]
        self.next_lane = 0
        self.batch = []
        self.reorder = True
        self.slack = 500.0

    def _collect(self, reads, writes):
        deps = []
        for b in reads:
            if b.w is not None:
                deps.append(b.w)
        for b in writes:
            if b.w is not None:
                deps.append(b.w)
            deps.extend(b.r)
        return deps

    def _mark(self, node, reads, writes):
        for b in reads:
            b.r.append(node)
        for b in writes:
            b.w = node
            b.r = []

    def emit(self, q, fns, reads=(), writes=(), n=256, cost=None):
        if callable(fns):
            fns = [fns]
        if cost is None and getattr(fns, "cost", None):
            cost = fns.cost
        fns = [_freeze(f) for f in fns]
        if cost is None:
            if q == "pe":
                cost = _FIX["pe"] + 180.0 * len(fns)
            else:
                cost = _FIX[q] + _PER[q] * n
        node = Node(q, fns, self._collect(reads, writes), cost, cost, len(self.batch))
        self._mark(node, reads, writes)
        self.batch.append(node)
        return node

    def dma(self, q, out, in_, reads=(), writes=(), nbytes=262144, noncontig=False):
        lat = 2000.0 + nbytes / 120.0
        node = Node(q, None, self._collect(reads, writes), _FIX["sp"], lat, len(self.batch), dma=(out, in_), noncontig=noncontig)
        self._mark(node, reads, writes)
        self.batch.append(node)
        return node

    def _wait(self, q, tok):
        kind, name, val = tok
        if kind == "q" and name == q and q == "pe":
            return
        key = (kind, name)
        if self.waited[q].get(key, 0) >= val:
            return
        self.waited[q][key] = val
        sem = self.sem[name] if kind == "q" else self.lanes[name].sem
        self.eng[q].wait_ge(sem, val)

    def _emit_node(self, node):
        q = node.q
        need = {}
        for d in node.deps:
            kind, name, val = d.token
            if need.get((kind, name), 0) < val:
                need[(kind, name)] = val
        for (kind, name), val in need.items():
            self._wait(q, (kind, name, val))
        if node.dma is not None:
            lane = self.lanes[self.next_lane]
            self.next_lane = (self.next_lane + 1) % len(self.lanes)
            if lane.count:
                self._wait(q, ("l", lane.idx, lane.count))
            out, in_ = node.dma
            if node.noncontig:
                with self.nc.allow_non_contiguous_dma(reason="single-column scatter"):
                    ins = self.eng[q].dma_start(out=out, in_=in_)
            else:
                ins = self.eng[q].dma_start(out=out, in_=in_)
            lane.count += 16
            ins.then_inc(lane.sem, 16)
            node.token = ("l", lane.idx, lane.count)
        else:
            ins = None
            for f in node.fns:
                ins = f()
            self.count[q] += 1
            ins.then_inc(self.sem[q], 1)
            node.token = ("q", q, self.count[q])
        node.fns = None
        node.dma = None

    def flush(self):
        batch = self.batch
        self.batch = []
        if not batch:
            return
        if not self.reorder:
            for nd in batch:
                nd.deps = [d for d in nd.deps]
                self._emit_node(nd)
            return
        SYNC = self.slack
        for nd in batch:
            live = []
            seen = set()
            for d in nd.deps:
                if id(d) in seen:
                    continue
                seen.add(id(d))
                live.append(d)
                if d.token is None:
                    d.succ.append(nd)
                    nd.nd += 1
            nd.deps = live
            nd.ready = 0.0
        fut = {q: [] for q in self.eng}
        avail = {q: [] for q in self.eng}
        free = {q: 0.0 for q in self.eng}
        for nd in batch:
            if nd.nd == 0:
                heapq.heappush(fut[nd.q], (0.0, nd.prio, nd))
        order = []
        left = len(batch)
        while left:
            best_q, best_t = None, None
            for q in self.eng:
                f, a = fut[q], avail[q]
                while f and f[0][0] <= free[q]:
                    r, p, x = heapq.heappop(f)
                    heapq.heappush(a, (p, x))
                if a:
                    t = free[q]
                elif f:
                    t = f[0][0]
                else:
                    continue
                if best_t is None or t < best_t:
                    best_q, best_t = q, t
            q = best_q
            if avail[q]:
                p, x = heapq.heappop(avail[q])
            else:
                r, p, x = heapq.heappop(fut[q])
            x.start = max(free[q], x.ready)
            free[q] = x.start + x.cost
            x.finish = x.start + x.lat
            order.append(x)
            left -= 1
            for s_ in x.succ:
                s_.nd -= 1
                rt = x.finish + (0.0 if (s_.q == x.q) else SYNC)
                if rt > s_.ready:
                    s_.ready = rt
                if s_.nd == 0:
                    heapq.heappush(fut[s_.q], (s_.ready, s_.prio, s_))
            x.succ = None
        if os.environ.get("SCHED_DEBUG") and len(order) > 500:
            fr = {q: 0.0 for q in self.eng}
            fin = {}
            for nd in sorted(order, key=lambda x: x.prio):
                st_ = fr[nd.q]
                for d in nd.deps:
                    if id(d) in fin:
                        st_ = max(st_, fin[id(d)] + (0.0 if d.q == nd.q else 100.0))
                fr[nd.q] = st_ + nd.cost
                fin[id(nd)] = st_ + nd.lat
            print("   program-order makespan_us=%.1f" % (max(fin.values()) / 1e3))
            busy = {}
            for nd in order:
                busy[nd.q] = busy.get(nd.q, 0.0) + nd.cost
            print("SCHED batch n=%d makespan_us=%.1f busy_us=%s" % (len(order), max(x.finish for x in order) / 1e3,
                                                                   {k: round(v / 1e3, 1) for k, v in busy.items()}), flush=True)
        for nd in order:
            self._emit_node(nd)

    def barrier(self):
        self.flush()
        for q in self.eng:
            for p in self.eng:
                if p != q and self.count[p]:
                    self._wait(q, ("q", p, self.count[p]))
            for lane in self.lanes:
                if lane.count:
                    self._wait(q, ("l", lane.idx, lane.count))


def _mm_cost(N):
    return 240.0 if N >= 512 else (126.0 if N >= 256 else 75.0)


class FnList(list):
    cost = None


class Ring:
    def __init__(self, st, nc, name, shape, dt, n, nb=1):
        self.items = [(st.enter_context(nc.sbuf_tensor("%s%d" % (name, i), list(shape), dt)),
                       Buf() if nb == 1 else [Buf() for _ in range(nb)]) for i in range(n)]
        self.i = 0

    def get(self):
        it = self.items[self.i % len(self.items)]
        self.i += 1
        return it


def run_pipeline(gens, depth):
    active = []
    gens = iter(gens)
    more = True
    while True:
        if more and len(active) < depth:
            try:
                active.append(next(gens))
            except StopIteration:
                more = False
        if not active:
            break
        for g in list(active):
            try:
                next(g)
            except StopIteration:
                active.remove(g)


def build_nc(TA, TBK, TBQ, phases=(0, 1, 2, 3, 4)):
    NT = TA + TBQ
    nc = bass.Bass("TRN2", target_bir_lowering=False)

    def din(name, shape, dt=F32):
        return nc.dram_tensor(name, list(shape), dt, kind="ExternalInput").ap()

    def dscr(name, shape, dt):
        return nc.dram_tensor(name, list(shape), dt, kind="Internal").ap()

    xa = din("xa", [TA, D])
    xbk = din("xbk", [TBK, D])
    xbq = din("xbq", [TBQ, D])
    halo = din("halo", [2, D])
    mem_d = [din("mema", [MEMT, D]), din("memb", [MEMT, D])]
    cs_d = {"a": din("cs_a", [128, (TA // 128) * 24]), "bk": din("cs_bk", [128, (TBK // 128) * 24]),
            "bq": din("cs_bq", [128, (TBQ // 128) * 24])}
    w_in_d = din("w_in", [D, 3072])
    w_out_d = din("w_out", [D, D])
    wm_q_d = din("wm_q", [D, 512])
    wm_kv_d = din("wm_kv", [D, D])
    wm_o_d = din("wm_o", [512, D])
    w_ff1_d = din("w_ff1", [D, 4096])
    w_ff2_d = din("w_ff2", [4096, D])
    gcol_d = din("gcol", [128, 32])
    cwb_d = din("cwb", [128, 16])
    rep512_d = din("rep512", [128, 4 * 512])
    rep128_d = din("rep128", [128, 128])
    lamv_d = din("lamv", [128, 4 * 64])
    ident_d = din("ident", [128, 128], BF16)
    out_d = nc.dram_tensor("out", [NT, D], F32, kind="ExternalOutput").ap()

    KTA, KTB = TA // 128, TBK // 128
    qT_d = [dscr("qT_a", [4, 128, TA], BF16), dscr("qT_b", [4, 128, TBQ], BF16)]
    kT_d = [dscr("kT_a", [4, 128, TA], BF16), dscr("kT_b", [4, 128, TBK], BF16)]
    v_d = [dscr("v_a", [4, 128, KTA, 130], BF16), dscr("v_b", [4, 128, KTB, 130], BF16)]
    mixT_d = dscr("mixT", [8, 128, NT], BF16)
    x2s_d = dscr("x2s", [NT, D], F32)
    mixT_p = mixT_d.rearrange("c p t -> p c t")

    with ExitStack() as top:
        S = Sched(nc, top)
        ps = top.enter_context(nc.psum_tensor("ps", [128, 4096], F32))
        psb = ps.bitcast(BF16)
        PB = [Buf() for _ in range(8)]

        def sb(st, name, shape, dt):
            return st.enter_context(nc.sbuf_tensor(name, list(shape), dt))

        ident_t = sb(top, "ident_t", [128, 128], BF16)
        gcol_t = sb(top, "gcol_t", [128, 32], F32)
        cwb_t = sb(top, "cwb_t", [128, 16], F32)
        rep512_t = sb(top, "rep512_t", [128, 4, 512], F32)
        rep128_t = sb(top, "rep128_t", [128, 128], F32)
        lamv_t = sb(top, "lamv_t", [128, 4, 64], F32)
        nhalf = sb(top, "nhalf", [128, 8], F32)
        lam_t = sb(top, "lam_t", [128, 16], F32)
        kmT = [sb(top, "kmT%d" % j, [128, 4, MEMT], BF16) for j in range(2)]
        vm = [sb(top, "vm%d" % j, [128, 2, 4, 130], BF16) for j in range(2)]
        CONST = Buf()
        CONST_ = CONST
        st_ring = Ring(top, nc, "st", [128, 4], F32, 6)
        st8_ring = Ring(top, nc, "st8", [128, 24], F32, 4)

        block = top.enter_context(nc.Block())

        def rms(xt, xbuf, hb, hbuf):
            rms_b(xt, xbuf, hb, hbuf, rms_a(xt, xbuf, hb, hbuf))

        def rms_b(xt, xbuf, hb, hbuf, ctx):
            stt, stb = ctx
            S.emit("act", lambda: nc.scalar.activation(out=hb[:], in_=xt[:], func=AF.Copy, scale=stt[:, 2:3]),
                   reads=[xbuf, stb], writes=[hbuf], n=1136)

        def rms_a(xt, xbuf, hb, hbuf):
            stt, stb = st_ring.get()
            S.emit("act", lambda: nc.scalar.activation(out=hb[:], in_=xt[:], func=AF.Square, accum_out=stt[:, 0:1]),
                   reads=[xbuf], writes=[hbuf, stb], n=1136)
            S.emit("pool", lambda: nc.gpsimd.tensor_scalar(out=stt[:, 1:2], in0=stt[:, 0:1], scalar1=1.0 / D, scalar2=EPS,
                                                          op0=ALU.mult, op1=ALU.add), reads=[stb], writes=[stb], cost=230.0)
            S.emit("pool", lambda: nc.gpsimd.tensor_tensor(out=stt[:, 2:3], in0=stt[:, 1:2], in1=nhalf[:, 0:1], op=ALU.pow),
                   reads=[stb], writes=[stb], cost=600.0)
            return stt, stb

        def transposes(src_fn, sbufs, n, bank, dst_ap, dbufs, evac="dve"):
            srcs = [src_fn(k) for k in range(n)]
            S.emit("pe", [lambda k=k: nc.tensor.transpose(out=psb[:, bank * 1024 + k * 128: bank * 1024 + (k + 1) * 128],
                                                          in_=srcs[k], identity=ident_t[:]) for k in range(n)],
                   reads=sbufs, writes=[PB[bank]], cost=30.0 + 70.0 * n)
            src_ps = psb[:, bank * 1024: bank * 1024 + n * 128].rearrange("p (k t) -> p k t", k=n)
            if evac == "dve":
                S.emit("dve", lambda: nc.vector.tensor_copy(out=dst_ap, in_=src_ps), reads=[PB[bank]], writes=dbufs, n=128 * n)
            else:
                S.emit("act", lambda: nc.scalar.copy(out=dst_ap, in_=src_ps), reads=[PB[bank]], writes=dbufs, n=128 * n)

        def groupnorm(bank, G, gain_ap, rings):
            qsb_ring, sq_ring, qn_ring = rings
            gs = 512 // G
            qs_, qsb = qsb_ring.get()
            S.emit("act", lambda: nc.scalar.copy(out=qs_[:], in_=ps[:, bank * 512:(bank + 1) * 512]), reads=[PB[bank]], writes=[qsb], n=512)
            sq, sqb = sq_ring.get()
            S.emit("dve", lambda: nc.vector.tensor_tensor(out=sq[:], in0=qs_[:], in1=qs_[:], op=ALU.mult), reads=[qsb], writes=[sqb], n=512)
            stt, stb = st8_ring.get()
            S.emit("dve", lambda: nc.vector.tensor_reduce(out=stt[:, 0:G], in_=sq[:].rearrange("p (g d) -> p g d", g=G),
                                                          axis=AX.X, op=ALU.add), reads=[sqb], writes=[stb], n=512)
            S.emit("pool", lambda: nc.gpsimd.tensor_scalar(out=stt[:, 8:8 + G], in0=stt[:, 0:G], scalar1=1.0 / gs, scalar2=EPS,
                                                          op0=ALU.mult, op1=ALU.add), reads=[stb], writes=[stb], cost=230.0)
            S.emit("pool", lambda: nc.gpsimd.tensor_tensor(out=stt[:, 16:16 + G], in0=stt[:, 8:8 + G], in1=nhalf[:, 0:G], op=ALU.pow),
                   reads=[stb], writes=[stb], cost=400.0 + 170.0 * G)
            S.emit("dve", lambda: nc.vector.tensor_tensor(out=sq[:].rearrange("p (g d) -> p g d", g=G),
                                                          in0=qs_[:].rearrange("p (g d) -> p g d", g=G),
                                                          in1=stt[:, 16:16 + G].unsqueeze(2).broadcast_to([128, G, gs]), op=ALU.mult),
                   reads=[qsb, stb], writes=[sqb])
            qn, qnb = qn_ring.get()
            S.emit("dve", lambda: nc.vector.tensor_tensor(out=qn[:], in0=sq[:], in1=gain_ap, op=ALU.mult), reads=[sqb], writes=[qnb])
            return qn, qnb

        def load_weight(dst, w_d, KC, cols, gidx, stg_ring, engs=("dve", "pool", "act"), wbufs=None, ks=None, c0s=None):
            i = 0
            for k in (range(KC) if ks is None else ks):
                for c0 in (range(0, cols, 1024) if c0s is None else c0s):
                    cw = min(1024, cols - c0)
                    CONST = CONST_ if wbufs is None else wbufs[(k, c0 // 1024)]
                    stg, stgb = stg_ring.get()
                    S.dma("sp", stg[:, 0:cw], w_d[k * 128:(k + 1) * 128, c0:c0 + cw], writes=[stgb], nbytes=512 * cw)
                    e = engs[i % len(engs)]
                    i += 1
                    o = dst[:, k, c0:c0 + cw]
                    if gidx is None:
                        if e == "dve":
                            S.emit("dve", lambda: nc.vector.tensor_copy(out=o, in_=stg[:, 0:cw]), reads=[stgb], writes=[CONST], n=1024)
                        elif e == "pool":
                            S.emit("pool", lambda: nc.gpsimd.tensor_copy(out=o, in_=stg[:, 0:cw]), reads=[stgb], writes=[CONST], n=1024)
                        else:
                            S.emit("act", lambda: nc.scalar.copy(out=o, in_=stg[:, 0:cw]), reads=[stgb], writes=[CONST], n=1024)
                    else:
                        g = gcol_t[:, gidx * 8 + k: gidx * 8 + k + 1]
                        if e == "dve":
                            S.emit("dve", lambda: nc.vector.tensor_scalar(out=o, in0=stg[:, 0:cw], scalar1=g, scalar2=None, op0=ALU.mult),
                                   reads=[stgb], writes=[CONST], n=1024)
                        elif e == "pool":
                            S.emit("pool", lambda: nc.gpsimd.tensor_scalar(out=o, in0=stg[:, 0:cw], scalar1=g, scalar2=None, op0=ALU.mult),
                                   reads=[stgb], writes=[CONST], n=1024)
                        else:
                            S.emit("act", lambda: nc.scalar.activation(out=o, in_=stg[:, 0:cw], func=AF.Copy, scale=g),
                                   reads=[stgb], writes=[CONST], n=1024)

        def gn1(bank, G, qs_ring, sq_ring, g8_ring):
            gs = 512 // G
            qs_, qsb = qs_ring.get()
            S.emit("act", lambda: nc.scalar.copy(out=qs_[:], in_=ps[:, bank * 512:(bank + 1) * 512]), reads=[PB[bank]], writes=[qsb], n=512)
            sq, sqb = sq_ring.get()
            S.emit("act", lambda: nc.scalar.activation(out=sq[:], in_=ps[:, bank * 512:(bank + 1) * 512], func=AF.Square),
                   reads=[PB[bank]], writes=[sqb], n=512)
            stt, stb = g8_ring.get()
            S.emit("dve", lambda: nc.vector.tensor_reduce(out=stt[:, 0:G], in_=sq[:].rearrange("p (g d) -> p g d", g=G),
                                                          axis=AX.X, op=ALU.add), reads=[sqb], writes=[stb], n=512)
            S.emit("pool", lambda: nc.gpsimd.tensor_scalar(out=stt[:, 8:8 + G], in0=stt[:, 0:G], scalar1=1.0 / gs, scalar2=EPS,
                                                          op0=ALU.mult, op1=ALU.add), reads=[stb], writes=[stb], cost=230.0)
            S.emit("pool", lambda: nc.gpsimd.tensor_tensor(out=stt[:, 16:16 + G], in0=stt[:, 8:8 + G], in1=nhalf[:, 0:G], op=ALU.pow),
                   reads=[stb], writes=[stb], cost=400.0 + 170.0 * G)
            return qs_, qsb, stt, stb

        def gn2(ctx, G, gain_ap):
            qs_, qsb, stt, stb = ctx
            gs = 512 // G
            q3 = qs_[:].rearrange("p (g d) -> p g d", g=G)
            S.emit("dve", lambda: nc.vector.tensor_tensor(out=q3, in0=q3, in1=stt[:, 16:16 + G].unsqueeze(2).broadcast_to([128, G, gs]), op=ALU.mult),
                   reads=[qsb, stb], writes=[qsb], n=512)
            S.emit("pool", lambda: nc.gpsimd.tensor_tensor(out=qs_[:], in0=qs_[:], in1=gain_ap, op=ALU.mult), reads=[qsb], writes=[qsb], n=512)
            return qs_, qsb

        def mm_group(out_ap, pairs, start=True, stop=True, skip=False):
            n = len(pairs)
            fns = FnList()
            fns.cost = 30.0 + n * _mm_cost(pairs[0][1].free_size())
            for i, (l, r) in enumerate(pairs):
                st_ = start and i == 0
                sp_ = stop and i == n - 1
                if skip:
                    fns.append(lambda l=l, r=r, st_=st_, sp_=sp_: nc.tensor.matmul(out_ap, lhsT=l, rhs=r, start=st_, stop=sp_, skip_group_check=True))
                else:
                    fns.append(lambda l=l, r=r, st_=st_, sp_=sp_: nc.tensor.matmul(out_ap, lhsT=l, rhs=r, start=st_, stop=sp_))
            return fns

        @block.sync
        def _(sync):
            with ExitStack() as p0:
                for t_, d_ in ((ident_t, ident_d), (gcol_t, gcol_d), (cwb_t, cwb_d), (rep128_t, rep128_d)):
                    S.dma("sp", t_[:], d_, writes=[CONST])
                S.dma("sp", rep512_t[:].rearrange("p a b -> p (a b)"), rep512_d, writes=[CONST])
                S.dma("sp", lamv_t[:].rearrange("p a b -> p (a b)"), lamv_d, writes=[CONST])
                S.emit("pool", lambda: nc.gpsimd.memset(nhalf[:], -0.5), writes=[CONST])
                for j in range(2):
                    S.emit("pool", lambda: nc.gpsimd.memset(vm[j][:], 1.0), writes=[CONST])
                S.barrier()
                scr = sb(p0, "lscr", [128, 64], F32)
                LB = Buf()
                for i in range(2):
                    S.emit("dve", lambda: nc.vector.tensor_tensor(out=scr[:], in0=lamv_t[:, 2 * i, :], in1=lamv_t[:, 2 * i + 1, :], op=ALU.mult),
                           writes=[LB])
                    S.emit("dve", lambda: nc.vector.tensor_reduce(out=lam_t[:, i:i + 1], in_=scr[:], axis=AX.X, op=ALU.add), reads=[LB], writes=[LB])
                S.emit("act", lambda: nc.scalar.activation(out=lam_t[:, 2:4], in_=lam_t[:, 0:2], func=AF.Exp), reads=[LB], writes=[LB])
                S.emit("dve", lambda: nc.vector.tensor_tensor(out=lam_t[:, 4:5], in0=lam_t[:, 2:3], in1=lam_t[:, 3:4], op=ALU.subtract), reads=[LB], writes=[LB])
                S.emit("dve", lambda: nc.vector.tensor_scalar(out=lam_t[:, 5:6], in0=lam_t[:, 4:5], scalar1=LAM_INIT, scalar2=-1.0,
                                                             op0=ALU.add, op1=ALU.mult), reads=[LB], writes=[LB])
                nlam = lam_t[:, 5:6]

                stg_ring = Ring(p0, nc, "stg", [128, 1024], F32, 3)
                wkv = sb(p0, "wkv", [128, 8, 1024], BF16)
                load_weight(wkv, wm_kv_d, 8, 1024, 2, stg_ring)
                S.barrier()
                x_ring = Ring(p0, nc, "x0_", [128, D], F32, 2)
                hb_ring = Ring(p0, nc, "hb0_", [128, D], BF16, 2)
                hT_ring = Ring(p0, nc, "hT0_", [128, 8, 128], BF16, 2)
                rings = (Ring(p0, nc, "qs0_", [128, 512], F32, 2), Ring(p0, nc, "sq0_", [128, 512], F32, 2), Ring(p0, nc, "qn0_", [128, 512], F32, 2))
                qb_ring = Ring(p0, nc, "qb0_", [128, 512], BF16, 2)
                for j in range(2):
                    for t in range(2):
                        xt, xbuf = x_ring.get()
                        S.dma("sp", xt[:], mem_d[j][t * 128:(t + 1) * 128, :], writes=[xbuf])
                        hb, hbuf = hb_ring.get()
                        rms(xt, xbuf, hb, hbuf)
                        hT, hTb = hT_ring.get()
                        transposes(lambda k: hb[:, k * 128:(k + 1) * 128], [hbuf], 8, 0, hT[:], [hTb])
                        S.emit("pe", mm_group(ps[:, 512:1024], [(hT[:, k, :], wkv[:, k, 0:512]) for k in range(8)]), reads=[hTb], writes=[PB[1]])
                        S.emit("pe", mm_group(ps[:, 1024:1536], [(hT[:, k, :], wkv[:, k, 512:1024]) for k in range(8)]), reads=[hTb], writes=[PB[2]])
                        kn, knb = groupnorm(1, 4, rep512_t[:, 3, :], rings)
                        kb, kbb = qb_ring.get()
                        S.emit("act", lambda: nc.scalar.copy(out=kb[:], in_=kn[:]), reads=[knb], writes=[kbb])
                        transposes(lambda h: kb[:, h * 128:(h + 1) * 128], [kbb], 4, 3, kmT[j][:, :, t * 128:(t + 1) * 128], [CONST])
                        S.emit("dve", lambda: nc.vector.tensor_copy(out=vm[j][:, t, :, 0:128], in_=ps[:, 1024:1536].rearrange("p (h d) -> p h d", h=4)),
                               reads=[PB[2]], writes=[CONST])
                S.barrier()

            if 1 in phases:
                with ExitStack() as p1:
                    w_in = sb(p1, "w_in_t", [128, 8, 3072], BF16)
                    with ExitStack() as p1s:
                        stg_ring = Ring(p1s, nc, "stg1_", [128, 1024], F32, 3)
                        load_weight(w_in, w_in_d, 8, 3072, 0, stg_ring)
                        S.barrier()
                    cs_t = sb(p1, "cs_t", [128, max(TA, TBK, TBQ) // 128, 24], F32)
                    x_ring = Ring(p1, nc, "x1_", [128, D], F32, 5)
                    hb_ring = Ring(p1, nc, "hb1_", [128, D], BF16, 5)
                    hT_ring = Ring(p1, nc, "hT1_", [128, 8, 256], BF16, 3, nb=2)
                    qs_ring = Ring(p1, nc, "qs1_", [128, 512], F32, 8)
                    sq_ring = Ring(p1, nc, "sq1_", [128, 512], F32, 3)
                    g8_ring = Ring(p1, nc, "g81_", [128, 24], F32, 12)
                    qb_ring = Ring(p1, nc, "qb1_", [128, 512], BF16, 8, nb=2)
                    rt_ring = Ring(p1, nc, "rt1_", [128, 4, 64], F32, 4)
                    qkT_ring = Ring(p1, nc, "qkT1_", [128, 4, 256], BF16, 4)
                    vb_ring = Ring(p1, nc, "vb1_", [128, 4, 2, 130], BF16, 3)
                    csb_ring = Ring(p1, nc, "csb1_", [128, 256], F32, 4)
                    zts = [sb(p1, "zt%d" % i, [128, 4, 258], F32) for i in range(2)]
                    bts = [sb(p1, "bt%d" % i, [128, 4, 257], F32) for i in range(2)]
                    zc = sb(p1, "zc", [128, 4, 2], F32)
                    bc = sb(p1, "bc", [128, 4, 1], F32)
                    zh = sb(p1, "zh", [128, 4, 2], F32)
                    acc = sb(p1, "acc", [128, 256], F32)
                    fl = sb(p1, "fl", [128, 8, 4], F32)
                    yl = sb(p1, "yl", [128, 4], BF16)
                    y_ring = Ring(p1, nc, "y1_", [128, 4, 256], BF16, 3)
                    ZBs, BBs = [Buf(), Buf()], [Buf(), Buf()]
                    ZCB, ZHB, ACCB = Buf(), Buf(), Buf()
                    for it_, ib_ in vb_ring.items:
                        S.emit("pool", lambda: nc.gpsimd.memset(it_[:], 1.0), writes=[ib_])
                    S.barrier()
                    tm_banks = [1, 2, 3]
                    tm_i = [0]
                    cv_banks = [4, 5, 6]
                    cv_i = [0]

                    def nxt(lst, ctr):
                        b = lst[ctr[0] % len(lst)]
                        ctr[0] += 1
                        return b

                    def halo_z():
                        xt, xbuf = x_ring.get()
                        S.emit("pool", lambda: nc.gpsimd.memset(xt[:], 0.0), writes=[xbuf])
                        S.dma("sp", xt[0:2, :], halo, writes=[xbuf])
                        hb, hbuf = hb_ring.get()
                        rms(xt, xbuf, hb, hbuf)
                        hT, hTbs = hT_ring.get()
                        transposes(lambda k: hb[:, k * 128:(k + 1) * 128], [hbuf], 8, 0, hT[:, :, 0:128], [hTbs[0]])
                        for j in range(4):
                            b1, b2 = nxt(cv_banks, cv_i), nxt(cv_banks, cv_i)
                            S.emit("pe", mm_group(ps[:, b1 * 512:b1 * 512 + 2], [(w_in[:, k, 512 + j * 128:512 + (j + 1) * 128], hT[:, k, 0:2]) for k in range(8)]),
                                   reads=[hTbs[0]], writes=[PB[b1]])
                            S.emit("pe", mm_group(ps[:, b2 * 512:b2 * 512 + 2], [(w_in[:, k, 1024 + j * 128:1024 + (j + 1) * 128], hT[:, k, 0:2]) for k in range(8)]),
                                   reads=[hTbs[0]], writes=[PB[b2]])
                            S.emit("act", lambda: nc.scalar.copy(out=fl[:, j, 0:2], in_=ps[:, b1 * 512:b1 * 512 + 2]), reads=[PB[b1]], writes=[ZHB])
                            S.emit("dve", lambda: nc.vector.tensor_tensor(out=zh[:, j, :], in0=fl[:, j, 0:2], in1=ps[:, b2 * 512:b2 * 512 + 2], op=ALU.mult),
                                   reads=[ZHB, PB[b2]], writes=[ZHB])

                    def stream(T, x_ap, cskey, do_conv, do_q, do_k, do_v, job, mix_col0, has_halo):
                        S.dma("sp", cs_t[:, 0:T // 128, :].rearrange("p a b -> p (a b)"), cs_d[cskey], writes=[CONST])
                        S.barrier()
                        if do_conv:
                            if has_halo:
                                halo_z()
                            else:
                                S.emit("dve", lambda: nc.vector.memset(zh[:], 0.0), writes=[ZHB])
                        nblk = T // 256

                        def block_gen(blk):
                            s = blk * 256
                            xts = []
                            for t in range(2):
                                xt, xbuf = x_ring.get()
                                tok0 = s + t * 128
                                S.dma("sp", xt[:], x_ap[tok0:tok0 + 128, :], writes=[xbuf], nbytes=524288)
                                xts.append((xt, xbuf))
                            yield
                            hbs = []
                            for t in range(2):
                                hb, hbuf = hb_ring.get()
                                rms(xts[t][0], xts[t][1], hb, hbuf)
                                hbs.append((hb, hbuf))
                            yield
                            hT, hTbs = hT_ring.get()
                            for t in range(2):
                                hb, hbuf = hbs[t]
                                transposes(lambda k: hb[:, k * 128:(k + 1) * 128], [hbuf], 8, 0, hT[:, :, t * 128:(t + 1) * 128], [hTbs[t]],
                                           evac=("dve" if t == 0 else "act"))
                            yield
                            ctxs = {}
                            vb = None
                            if do_v:
                                vb, vbb = vb_ring.get()
                            for t in range(2):
                                for which in ("q", "k"):
                                    if (which == "q" and not do_q) or (which == "k" and not do_k):
                                        continue
                                    c0 = 1536 if which == "q" else 2048
                                    bank = nxt(tm_banks, tm_i)
                                    S.emit("pe", mm_group(ps[:, bank * 512:(bank + 1) * 512],
                                                          [(hT[:, k, t * 128:(t + 1) * 128], w_in[:, k, c0:c0 + 512]) for k in range(8)]),
                                           reads=[hTbs[t]], writes=[PB[bank]])
                                    ctxs[(which, t)] = gn1(bank, 8, qs_ring, sq_ring, g8_ring)
                                if do_v:
                                    bank = nxt(tm_banks, tm_i)
                                    S.emit("pe", mm_group(ps[:, bank * 512:(bank + 1) * 512],
                                                          [(hT[:, k, t * 128:(t + 1) * 128], w_in[:, k, 2560:3072]) for k in range(8)]),
                                           reads=[hTbs[t]], writes=[PB[bank]])
                                    S.emit("dve", lambda: nc.vector.tensor_copy(out=vb[:, :, t, 0:128], in_=ps[:, bank * 512:(bank + 1) * 512].rearrange("p (h d) -> p h d", h=4)),
                                           reads=[PB[bank]], writes=[vbb], n=512)
                            if do_v:
                                S.dma("sp", v_d[job].rearrange("h p k c -> p h k c")[:, :, blk * 2:blk * 2 + 2, :], vb[:], reads=[vbb])
                            if do_conv:
                                cur = blk % 2
                                zt, bt, ZB, BB = zts[cur], bts[cur], ZBs[cur], BBs[cur]
                                if blk == 0:
                                    S.emit("pool", lambda: nc.gpsimd.memset(zt[:, :, 0:1], 0.0), reads=[ZB], writes=[ZB])
                                    S.emit("pool", lambda: nc.gpsimd.tensor_copy(out=zt[:, :, 1:2], in_=zh[:, :, 0:1]), reads=[ZHB], writes=[ZB])
                                    S.emit("pool", lambda: nc.gpsimd.memset(bt[:, :, 0:1], 0.0), reads=[BB], writes=[BB])
                                else:
                                    S.emit("pool", lambda: nc.gpsimd.tensor_copy(out=zt[:, :, 0:2], in_=zc[:]), reads=[ZCB], writes=[ZB])
                                    S.emit("pool", lambda: nc.gpsimd.tensor_copy(out=bt[:, :, 0:1], in_=bc[:]), reads=[ZCB], writes=[BB])
                                for j in range(4):
                                    bC, bV, bB = nxt(cv_banks, cv_i), nxt(cv_banks, cv_i), nxt(cv_banks, cv_i)
                                    for bnk, c0 in ((bC, 512), (bV, 1024), (bB, 0)):
                                        S.emit("pe", mm_group(ps[:, bnk * 512:bnk * 512 + 256],
                                                              [(w_in[:, k, c0 + j * 128:c0 + (j + 1) * 128], hT[:, k, :]) for k in range(8)]),
                                               reads=hTbs, writes=[PB[bnk]])
                                    cs_, csb_ = csb_ring.get()
                                    S.emit("act", lambda: nc.scalar.copy(out=cs_[:], in_=ps[:, bC * 512:bC * 512 + 256]), reads=[PB[bC]], writes=[csb_])
                                    S.emit("dve", lambda: nc.vector.tensor_tensor(out=zt[:, j, 2:258], in0=cs_[:], in1=ps[:, bV * 512:bV * 512 + 256], op=ALU.mult),
                                           reads=[csb_, PB[bV]], writes=[ZB])
                                    S.emit("act", lambda: nc.scalar.copy(out=bt[:, j, 1:257], in_=ps[:, bB * 512:bB * 512 + 256]), reads=[PB[bB]], writes=[BB])
                                S.emit("pool", lambda: nc.gpsimd.tensor_copy(out=zc[:], in_=zt[:, :, 256:258]), reads=[ZB], writes=[ZCB])
                                S.emit("pool", lambda: nc.gpsimd.tensor_copy(out=bc[:], in_=bt[:, :, 256:257]), reads=[BB], writes=[ZCB])
                            yield
                            qbs = {}
                            for t in range(2):
                                tt = blk * 2 + t
                                for which in ("q", "k"):
                                    if (which, t) not in ctxs:
                                        continue
                                    qn, qnb = gn2(ctxs[(which, t)], 8, rep512_t[:, 0 if which == "q" else 1, :])
                                    qb, qbbs = qb_ring.get()
                                    q3 = qn[:].rearrange("p (g d) -> p g d", g=8)
                                    b3 = qb[:].rearrange("p (g d) -> p g d", g=8)
                                    S.emit("act", lambda: nc.scalar.copy(out=b3[:, :, 16:64], in_=q3[:, :, 16:64]), reads=[qnb], writes=[qbbs[0]], n=384)
                                    rt, rtb = rt_ring.get()
                                    rtf = rt[:].rearrange("p a b -> p (a b)")
                                    t1 = rtf[:, 0:128].rearrange("p (g d) -> p g d", g=8)
                                    t2 = rtf[:, 128:256].rearrange("p (g d) -> p g d", g=8)
                                    r1, r2 = q3[:, :, 0:8], q3[:, :, 8:16]
                                    cos4 = cs_t[:, tt, 0:8].unsqueeze(1).unsqueeze(1).broadcast_to([128, 8, 2, 8])
                                    sinb = cs_t[:, tt, 8:16].unsqueeze(1).broadcast_to([128, 8, 8])
                                    nsinb = cs_t[:, tt, 16:24].unsqueeze(1).broadcast_to([128, 8, 8])
                                    S.emit("dve", lambda: nc.vector.tensor_tensor(out=t1.rearrange("p g (h d) -> p g h d", h=2),
                                                                                  in0=q3[:, :, 0:16].rearrange("p g (h d) -> p g h d", h=2), in1=cos4, op=ALU.mult),
                                           reads=[qnb], writes=[rtb], n=128)
                                    S.emit("dve", lambda: nc.vector.tensor_tensor(out=t2[:, :, 0:8], in0=r2, in1=nsinb, op=ALU.mult), reads=[qnb], writes=[rtb], n=64)
                                    S.emit("dve", lambda: nc.vector.tensor_tensor(out=t2[:, :, 8:16], in0=r1, in1=sinb, op=ALU.mult), reads=[qnb], writes=[rtb], n=64)
                                    S.emit("dve", lambda: nc.vector.tensor_tensor(out=b3[:, :, 0:16], in0=t1, in1=t2, op=ALU.add), reads=[rtb], writes=[qbbs[1]], n=128)
                                    qbb = qbbs
                                    qbs[(which, t)] = (qb, qbb)
                            if do_conv:
                                yb, ybb = y_ring.get()
                                for j in range(4):
                                    w0, w1, w2, bb_ = (cwb_t[:, j * 4 + i:j * 4 + i + 1] for i in range(4))
                                    S.emit("dve", lambda: nc.vector.tensor_scalar(out=acc[:], in0=zt[:, j, 1:257], scalar1=w1, scalar2=bb_, op0=ALU.mult, op1=ALU.add),
                                           reads=[ZB], writes=[ACCB])
                                    S.emit("dve", lambda: nc.vector.scalar_tensor_tensor(out=acc[:], in0=zt[:, j, 0:256], scalar=w0, in1=acc[:], op0=ALU.mult, op1=ALU.add),
                                           reads=[ZB, ACCB], writes=[ACCB])
                                    S.emit("dve", lambda: nc.vector.scalar_tensor_tensor(out=acc[:], in0=zt[:, j, 2:258], scalar=w2, in1=acc[:], op0=ALU.mult, op1=ALU.add),
                                           reads=[ZB, ACCB], writes=[ACCB])
                                    S.emit("dve", lambda: nc.vector.tensor_tensor(out=yb[:, j, :], in0=bt[:, j, 0:256], in1=acc[:], op=ALU.mult),
                                           reads=[BB, ACCB], writes=[ybb])
                                jj0 = 1 if blk == 0 else 0
                                c_lo = mix_col0 + s - 1 + jj0
                                S.dma("sp", mixT_p[:, 0:4, c_lo:mix_col0 + s + 255], yb[:, :, jj0:256], reads=[ybb])
                            yield
                            for which, d_ in (("q", qT_d), ("k", kT_d)):
                                if (which, 0) not in qbs:
                                    continue
                                dstT, dstTb = qkT_ring.get()
                                for t in range(2):
                                    qb, qbb = qbs[(which, t)]
                                    transposes(lambda h: qb[:, h * 128:(h + 1) * 128], list(qbb), 4, 7, dstT[:, :, t * 128:(t + 1) * 128], [dstTb], evac="act")
                                S.dma("sp", d_[job].rearrange("h p t -> p h t")[:, :, s:s + 256], dstT[:], reads=[dstTb])

                        run_pipeline((block_gen(b) for b in range(nblk)), 6)
                        if do_conv:
                            FB = Buf()
                            cw3 = cwb_t[:].rearrange("p (j i) -> p j i", i=4)
                            f = lambda i: fl[:, i, :]
                            S.emit("dve", lambda: nc.vector.tensor_tensor(out=f(0), in0=zc[:, :, 1], in1=cw3[:, :, 1], op=ALU.mult), reads=[ZCB], writes=[FB])
                            S.emit("dve", lambda: nc.vector.tensor_tensor(out=f(1), in0=f(0), in1=cw3[:, :, 3], op=ALU.add), reads=[FB], writes=[FB])
                            S.emit("dve", lambda: nc.vector.tensor_tensor(out=f(2), in0=zc[:, :, 0], in1=cw3[:, :, 0], op=ALU.mult), reads=[ZCB, FB], writes=[FB])
                            S.emit("dve", lambda: nc.vector.tensor_tensor(out=f(3), in0=f(1), in1=f(2), op=ALU.add), reads=[FB], writes=[FB])
                            S.emit("dve", lambda: nc.vector.tensor_tensor(out=f(4), in0=zh[:, :, 1], in1=cw3[:, :, 2], op=ALU.mult), reads=[ZHB, FB], writes=[FB])
                            S.emit("dve", lambda: nc.vector.tensor_tensor(out=f(5), in0=f(3), in1=f(4), op=ALU.add), reads=[FB], writes=[FB])
                            S.emit("dve", lambda: nc.vector.tensor_tensor(out=yl[:], in0=f(5), in1=bc[:, :, 0], op=ALU.mult), reads=[FB, ZCB], writes=[FB])
                            S.dma("sp", mixT_p[:, 0:4, mix_col0 + T - 1:mix_col0 + T], yl[:].unsqueeze(2), reads=[FB], noncontig=True)
                        S.barrier()

                    stream(TA, xa, "a", True, True, True, True, 0, 0, False)
                    stream(TBK, xbk, "bk", False, False, True, True, 1, 0, False)
                    stream(TBQ, xbq, "bq", True, True, False, False, 1, TA, True)
                    S.barrier()

            if 2 in phases:
                with ExitStack() as p2:
                    S.flush()
                    S.reorder = False
                    kbuf = sb(p2, "kbuf", [128, KVCAP], BF16)
                    vbuf = sb(p2, "vbuf", [128, (KVCAP // 128) * 130], BF16)
                    vbuf3 = vbuf[:].rearrange("p (n c) -> p n c", c=130)
                    qb_ring = Ring(p2, nc, "qblk", [128, 4, 512], BF16, 3)
                    pT_ring = Ring(p2, nc, "pT", [128, 1024], BF16, 3)
                    osb = sb(p2, "osb", [128, 8, 129], F32)
                    ta = sb(p2, "ta", [128, 4, 128], F32)
                    tb = sb(p2, "tb", [128, 4, 128], F32)
                    onb = sb(p2, "onb", [128, 4, 128], BF16)
                    oT_ring = Ring(p2, nc, "oT", [128, 512], BF16, 2)
                    OSB, TAB, TBB, ONB = Buf(), Buf(), Buf(), Buf()
                    scale = 64 ** -0.5
                    SLOT = KVCAP // 2
                    passes = []
                    hpa = min(4, SLOT // TA)
                    for h0 in range(0, 4, hpa):
                        passes.append((0, list(range(h0, h0 + hpa)), TA, TA, 0))
                    hpb = min(4, SLOT // TBK)
                    for h0 in range(0, 4, hpb):
                        passes.append((1, list(range(h0, h0 + hpb)), TBK, TBQ, TA))
                    KBs = [[Buf() for _ in range(4)] for _ in range(2)]
                    VBs = [[Buf() for _ in range(4)] for _ in range(2)]

                    def load_kv(pi):
                        job, heads, Nk, Nq, col0 = passes[pi]
                        sl = pi % 2
                        KT = Nk // 128
                        for hi, h in enumerate(heads):
                            for c0 in range(0, Nk, 4096):
                                cw = min(4096, Nk - c0)
                                S.dma("sp", kbuf[:, sl * SLOT + hi * Nk + c0: sl * SLOT + hi * Nk + c0 + cw], kT_d[job][h, :, c0:c0 + cw],
                                      writes=[KBs[sl][hi]], nbytes=1048576)
                            for k0 in range(0, KT, 32):
                                kw = min(32, KT - k0)
                                S.dma("sp", vbuf3[:, sl * (SLOT // 128) + hi * KT + k0: sl * (SLOT // 128) + hi * KT + k0 + kw, :],
                                      v_d[job][h, :, k0:k0 + kw, :], writes=[VBs[sl][hi]], nbytes=1064960)

                    q0_pre = {}
                    load_kv(0)
                    if len(passes) > 1:
                        load_kv(1)
                    for pi, (job, heads, Nk, Nq, col0) in enumerate(passes):
                        nh = len(heads)
                        KT = Nk // 128
                        sl = pi % 2
                        KB, VB = KBs[sl], VBs[sl]
                        kbase = sl * SLOT
                        vbase = sl * (SLOT // 128)
                        nqb = Nq // 512
                        qblks = {}

                        def load_q(qb):
                            qt, qtb = qb_ring.get()
                            S.dma("sp", qt[:, 0:nh, :], qT_d[job].rearrange("h p t -> p h t")[:, heads[0]:heads[0] + nh, qb * 512:(qb + 1) * 512], writes=[qtb])
                            qblks[qb] = (qt, qtb)

                        its = [(qb, hi, kt) for qb in range(nqb) for hi in range(nh) for kt in range(KT)]
                        if pi in q0_pre:
                            qblks[0] = q0_pre.pop(pi)
                        else:
                            load_q(0)
                        state = {}

                        def qk(i):
                            qb, hi, kt = its[i]
                            if hi == 0 and kt == 0 and qb + 1 < nqb:
                                load_q(qb + 1)
                            qt, qtb = qblks[qb]
                            b0 = (i % 2) * 2
                            fns = []
                            for c in range(2):
                                fns.append(lambda c=c: nc.tensor.matmul(ps[:, (b0 + c) * 512:(b0 + c + 1) * 512],
                                                                        lhsT=kbuf[c * 64:(c + 1) * 64, kbase + hi * Nk + kt * 128: kbase + hi * Nk + (kt + 1) * 128],
                                                                        rhs=qt[c * 64:(c + 1) * 64, hi, :], start=True, stop=True))
                            S.emit("pe", fns, reads=[qtb, KB[hi]], writes=[PB[b0], PB[b0 + 1]], cost=330.0)

                        def ex(i):
                            b0 = (i % 2) * 2
                            pt, ptb = pT_ring.get()
                            state[i] = (pt, ptb)
                            S.emit("act", lambda: nc.scalar.activation(out=pt[:], in_=ps[:, b0 * 512:(b0 + 2) * 512], func=AF.Exp, scale=scale),
                                   reads=[PB[b0], PB[b0 + 1]], writes=[ptb], n=1024)

                        def av(i):
                            qb, hi, kt = its[i]
                            pt, ptb = state.pop(i)
                            fns = []
                            for c in range(2):
                                for qs in range(4):
                                    a = c * 4 + qs
                                    col = (4 + a // 3) * 512 + (a % 3) * 129
                                    st_ = (kt == 0 and a % 3 == 0)
                                    fns.append(lambda c=c, qs=qs, col=col, st_=st_: nc.tensor.matmul(
                                        ps[:, col:col + 129], lhsT=pt[:, c * 512 + qs * 128: c * 512 + (qs + 1) * 128],
                                        rhs=vbuf3[:, vbase + hi * KT + kt, 0:129], start=st_, stop=(kt == KT - 1), skip_group_check=True))
                            S.emit("pe", fns, reads=[ptb, VB[hi]], writes=[PB[4], PB[5], PB[6]], cost=650.0)
                            if kt == KT - 1:
                                finish(qb, hi)

                        def finish(qb, hi):
                            h = heads[hi]
                            for bnk, a0, na in ((4, 0, 3), (5, 3, 3), (6, 6, 2)):
                                S.emit("dve", lambda: nc.vector.tensor_copy(out=osb[:, a0:a0 + na, :],
                                                                            in_=ps[:, bnk * 512: bnk * 512 + na * 129].rearrange("p (a c) -> p a c", c=129)),
                                       reads=[PB[bnk]], writes=[OSB], n=387)
                            stt, stb = st8_ring.get()
                            sums = osb[:, :, 128:129].rearrange("p a c -> p (a c)")
                            S.emit("dve", lambda: nc.vector.reciprocal(out=stt[:, 0:8], in_=sums), reads=[OSB], writes=[stb])
                            S.emit("dve", lambda: nc.vector.tensor_scalar(out=stt[:, 8:12], in0=stt[:, 4:8], scalar1=nlam, scalar2=None, op0=ALU.mult),
                                   reads=[stb], writes=[stb])
                            S.emit("dve", lambda: nc.vector.tensor_tensor(out=ta[:], in0=osb[:, 0:4, 0:128], in1=stt[:, 0:4].unsqueeze(2).broadcast_to([128, 4, 128]), op=ALU.mult),
                                   reads=[OSB, stb], writes=[TAB], n=512)
                            S.emit("dve", lambda: nc.vector.tensor_tensor(out=tb[:], in0=osb[:, 4:8, 0:128], in1=stt[:, 8:12].unsqueeze(2).broadcast_to([128, 4, 128]), op=ALU.mult),
                                   reads=[OSB, stb], writes=[TBB], n=512)
                            S.emit("dve", lambda: nc.vector.tensor_tensor(out=ta[:], in0=ta[:], in1=tb[:], op=ALU.add), reads=[TAB, TBB], writes=[TAB], n=512)
                            S.emit("dve", lambda: nc.vector.tensor_tensor(out=tb[:], in0=ta[:], in1=ta[:], op=ALU.mult), reads=[TAB], writes=[TBB], n=512)
                            S.emit("dve", lambda: nc.vector.tensor_reduce(out=stt[:, 12:16], in_=tb[:], axis=AX.X, op=ALU.add), reads=[TBB], writes=[stb], n=512)
                            S.emit("pool", lambda: nc.gpsimd.tensor_scalar(out=stt[:, 16:20], in0=stt[:, 12:16], scalar1=1.0 / 128, scalar2=EPS, op0=ALU.mult, op1=ALU.add),
                                   reads=[stb], writes=[stb])
                            S.emit("pool", lambda: nc.gpsimd.tensor_tensor(out=stt[:, 20:24], in0=stt[:, 16:20], in1=nhalf[:, 0:4], op=ALU.pow), reads=[stb], writes=[stb], cost=1150.0)
                            S.emit("dve", lambda: nc.vector.tensor_tensor(out=tb[:], in0=ta[:], in1=stt[:, 20:24].unsqueeze(2).broadcast_to([128, 4, 128]), op=ALU.mult),
                                   reads=[TAB, stb], writes=[TBB], n=512)
                            S.emit("dve", lambda: nc.vector.scalar_tensor_tensor(out=onb[:], in0=tb[:], scalar=1.0 - LAM_INIT,
                                                                                 in1=rep128_t[:].unsqueeze(1).broadcast_to([128, 4, 128]), op0=ALU.mult, op1=ALU.mult),
                                   reads=[TBB], writes=[ONB], n=512)
                            def fin_b():
                                ot, otb = oT_ring.get()
                                transposes(lambda qs: onb[:, qs, :], [ONB], 4, 7, ot[:].rearrange("p (k t) -> p k t", k=4), [otb])
                                S.dma("sp", mixT_d[4 + h, :, col0 + qb * 512: col0 + (qb + 1) * 512], ot[:], reads=[otb])
                            deferred.append(fin_b)

                        n = len(its)
                        deferred = []
                        qk(0)
                        if n > 1:
                            qk(1)
                        for i in range(n):
                            ex(i)
                            if i + 2 < n:
                                qk(i + 2)
                            if deferred and its[i][2] == min(6, KT - 2):
                                deferred.pop(0)()
                            av(i)
                        while deferred:
                            deferred.pop(0)()
                        if pi + 1 < len(passes):
                            j2, h2, _, _, _ = passes[pi + 1]
                            qt2, qtb2 = qb_ring.get()
                            S.dma("sp", qt2[:, 0:len(h2), :], qT_d[j2].rearrange("h p t -> p h t")[:, h2[0]:h2[0] + len(h2), 0:512], writes=[qtb2])
                            q0_pre[pi + 1] = (qt2, qtb2)
                        if pi + 2 < len(passes):
                            load_kv(pi + 2)
                    S.barrier()

            S.flush()
            S.reorder = True
            if 3 in phases:
                with ExitStack() as p3:
                    w_out = sb(p3, "w_out_t", [128, 8, 1024], BF16)
                    wmq = sb(p3, "wmq_t", [128, 8, 512], BF16)
                    wmo = sb(p3, "wmo_t", [128, 4, 1024], BF16)
                    with ExitStack() as p3s:
                        stg_ring = Ring(p3s, nc, "stg3_", [128, 1024], F32, 3)
                        load_weight(w_out, w_out_d, 8, 1024, None, stg_ring)
                        load_weight(wmq, wm_q_d, 8, 512, 1, stg_ring)
                        load_weight(wmo, wm_o_d, 4, 1024, None, stg_ring)
                        S.barrier()
                    x_ring = Ring(p3, nc, "x3_", [128, D], F32, 14)
                    mix_ring = Ring(p3, nc, "mix3_", [128, 8, 256], BF16, 2)
                    hb_ring = Ring(p3, nc, "hb3_", [128, D], BF16, 4)
                    hT_ring = Ring(p3, nc, "hT3_", [128, 8, 256], BF16, 3, nb=2)
                    qs_ring = Ring(p3, nc, "qs3_", [128, 512], F32, 4)
                    sq_ring = Ring(p3, nc, "sq3_", [128, 512], F32, 3)
                    g8_ring = Ring(p3, nc, "g83_", [128, 24], F32, 8)
                    qb_ring = Ring(p3, nc, "qb3_", [128, 512], BF16, 4)
                    qmT_ring = Ring(p3, nc, "qmT3_", [128, 4, 256], BF16, 3)
                    pT_ring = Ring(p3, nc, "pT3_", [128, 2, 256], BF16, 6)
                    osb_ring = Ring(p3, nc, "osb3_", [128, 8, 129], F32, 2)
                    om_ring = Ring(p3, nc, "om3_", [128, 8, 128], BF16, 2)
                    omT_ring = Ring(p3, nc, "omT3_", [128, 4, 256], BF16, 3)
                    mscale = 128 ** -0.5
                    nsb = NT // 256

                    def sb_gen(sbi):
                        g0 = sbi * 256
                        job = 0 if g0 < TA else 1
                        mx, mxb = mix_ring.get()
                        S.dma("sp", mx[:], mixT_p[:, :, g0:g0 + 256], writes=[mxb])
                        xs = []
                        for t in range(2):
                            xt, xbuf = x_ring.get()
                            r0 = g0 + t * 128
                            src = xa[r0:r0 + 128, :] if r0 < TA else xbq[r0 - TA:r0 - TA + 128, :]
                            S.dma("sp", xt[:], src, writes=[xbuf], nbytes=524288)
                            xs.append((xt, xbuf))
                        yield
                        hbs = []
                        for t in range(2):
                            xt, xbuf = xs[t]
                            b0 = 2 * t
                            for half in range(2):
                                S.emit("pe", mm_group(ps[:, (b0 + half) * 512:(b0 + half + 1) * 512],
                                                      [(mx[:, k, t * 128:(t + 1) * 128], w_out[:, k, half * 512:(half + 1) * 512]) for k in range(8)]),
                                       reads=[mxb], writes=[PB[b0 + half]])
                            S.emit("dve", lambda: nc.vector.tensor_tensor(out=xt[:], in0=xt[:], in1=ps[:, b0 * 512:(b0 + 2) * 512], op=ALU.add),
                                   reads=[xbuf, PB[b0], PB[b0 + 1]], writes=[xbuf], n=1024)
                            hb, hbuf = hb_ring.get()
                            rms(xt, xbuf, hb, hbuf)
                            hbs.append((hb, hbuf))
                        yield
                        hT, hTbs = hT_ring.get()
                        for t in range(2):
                            hb, hbuf = hbs[t]
                            transposes(lambda k: hb[:, k * 128:(k + 1) * 128], [hbuf], 8, 4, hT[:, :, t * 128:(t + 1) * 128], [hTbs[t]],
                                       evac=("dve" if t == 0 else "act"))
                        ctxs = []
                        for t in range(2):
                            bank = 5 + t
                            S.emit("pe", mm_group(ps[:, bank * 512:(bank + 1) * 512], [(hT[:, k, t * 128:(t + 1) * 128], wmq[:, k, :]) for k in range(8)]),
                                   reads=[hTbs[t]], writes=[PB[bank]])
                            ctxs.append(gn1(bank, 4, qs_ring, sq_ring, g8_ring))
                        yield
                        qmT, qmTb = qmT_ring.get()
                        for t in range(2):
                            qn, qnb = gn2(ctxs[t], 4, rep512_t[:, 2, :])
                            qb, qbb = qb_ring.get()
                            S.emit("act", lambda: nc.scalar.copy(out=qb[:], in_=qn[:]), reads=[qnb], writes=[qbb], n=512)
                            transposes(lambda h: qb[:, h * 128:(h + 1) * 128], [qbb], 4, 7, qmT[:, :, t * 128:(t + 1) * 128], [qmTb], evac="act")
                        yield
                        started = set()
                        pts = {}

                        def scores(h):
                            sbank = 5 + h % 2
                            fns = []
                            for half in range(2):
                                fns.append(lambda half=half: nc.tensor.matmul(ps[:, sbank * 512 + half * 256: sbank * 512 + (half + 1) * 256],
                                                                              lhsT=kmT[job][:, h, half * 128:(half + 1) * 128], rhs=qmT[:, h, :],
                                                                              start=(half == 0), stop=True, skip_group_check=True))
                            S.emit("pe", fns, reads=[qmTb], writes=[PB[sbank]], cost=300.0)
                            pt, ptb = pT_ring.get()
                            S.emit("act", lambda: nc.scalar.activation(out=pt[:].rearrange("p a b -> p (a b)"), in_=ps[:, sbank * 512:(sbank + 1) * 512],
                                                                       func=AF.Exp, scale=mscale), reads=[PB[sbank]], writes=[ptb], n=512)
                            pts[h] = (pt, ptb)

                        def avm(h):
                            pt, ptb = pts.pop(h)
                            fns = []
                            wb = set()
                            for t in range(2):
                                a = t * 4 + h
                                bnk = a // 3
                                col = bnk * 512 + (a % 3) * 129
                                wb.add(bnk)
                                for half in range(2):
                                    st_ = bnk not in started
                                    started.add(bnk)
                                    fns.append(lambda t=t, half=half, col=col, st_=st_: nc.tensor.matmul(
                                        ps[:, col:col + 129], lhsT=pt[:, half, t * 128:(t + 1) * 128], rhs=vm[job][:, half, h, 0:129],
                                        start=st_, stop=(half == 1), skip_group_check=True))
                            S.emit("pe", fns, reads=[ptb], writes=[PB[b] for b in sorted(wb)], cost=350.0)

                        scores(0)
                        for h in range(4):
                            if h + 1 < 4:
                                scores(h + 1)
                            avm(h)
                        osb, OSB = osb_ring.get()
                        for bnk, a0, na in ((0, 0, 3), (1, 3, 3), (2, 6, 2)):
                            S.emit("dve", lambda: nc.vector.tensor_copy(out=osb[:, a0:a0 + na, :],
                                                                        in_=ps[:, bnk * 512: bnk * 512 + na * 129].rearrange("p (a c) -> p a c", c=129)),
                                   reads=[PB[bnk]], writes=[OSB], n=387)
                        yield
                        stt, stb = g8_ring.get()
                        S.emit("dve", lambda: nc.vector.reciprocal(out=stt[:, 0:8], in_=osb[:, :, 128:129].rearrange("p a c -> p (a c)")), reads=[OSB], writes=[stb])
                        om, OMB = om_ring.get()
                        S.emit("dve", lambda: nc.vector.tensor_tensor(out=om[:], in0=osb[:, :, 0:128], in1=stt[:, 0:8].unsqueeze(2).broadcast_to([128, 8, 128]), op=ALU.mult),
                               reads=[OSB, stb], writes=[OMB], n=1024)
                        omT, omTb = omT_ring.get()
                        for t in range(2):
                            transposes(lambda h: om[:, t * 4 + h, :], [OMB], 4, 7, omT[:, :, t * 128:(t + 1) * 128], [omTb], evac="act")
                        yield
                        for t in range(2):
                            xt, xbuf = xs[t]
                            b0 = 3 if t == 0 else 5
                            for half in range(2):
                                S.emit("pe", mm_group(ps[:, (b0 + half) * 512:(b0 + half + 1) * 512],
                                                      [(omT[:, k, t * 128:(t + 1) * 128], wmo[:, k, half * 512:(half + 1) * 512]) for k in range(4)]),
                                       reads=[omTb], writes=[PB[b0 + half]])
                            S.emit("dve", lambda: nc.vector.tensor_tensor(out=xt[:], in0=xt[:], in1=ps[:, b0 * 512:(b0 + 2) * 512], op=ALU.add),
                                   reads=[xbuf, PB[b0], PB[b0 + 1]], writes=[xbuf], n=1024)
                            S.dma("sp", x2s_d[g0 + t * 128: g0 + (t + 1) * 128, :], xt[:], reads=[xbuf], nbytes=524288)

                    run_pipeline((sb_gen(i) for i in range(nsb)), 7)
                    S.barrier()

            if 4 in phases:
                with ExitStack() as p4:
                    S.flush()
                    S.reorder = True
                    ff1 = sb(p4, "ff1_t", [128, 8, 4096], BF16)
                    ff2 = sb(p4, "ff2_t", [128, 32, 1024], BF16)
                    stg_ring = Ring(p4, nc, "stg4_", [128, 1024], F32, 2)
                    W1B = {(k, g): Buf() for k in range(8) for g in range(4)}
                    W2B = {(f, 0): Buf() for f in range(32)}
                    x_ring = Ring(p4, nc, "x4_", [128, D], F32, 6)
                    hb_ring = Ring(p4, nc, "hb4_", [128, D], BF16, 4)
                    hT_ring = Ring(p4, nc, "hT4_", [128, 8, 256], BF16, 2)
                    rl_ring = Ring(p4, nc, "rl4_", [128, 256], F32, 2)
                    aT_ring = Ring(p4, nc, "aT4_", [128, 256], BF16, 3)
                    nsb = NT // 256
                    pend = {}
                    src_d = x2s_d if 3 in phases else None

                    def load_sb4(sbi):
                        xs = []
                        for t in range(2):
                            xt, xbuf = x_ring.get()
                            r0 = sbi * 256 + t * 128
                            S.dma("sp", xt[:], src_d[r0:r0 + 128, :], writes=[xbuf], nbytes=524288)
                            xs.append((xt, xbuf))
                        pend[sbi] = xs

                    hTs = {}
                    prea = {}

                    def pre_a(sbi):
                        xs = pend[sbi]
                        hT, hTb = hT_ring.get()
                        hTs[sbi] = (hT, hTb)
                        prea[sbi] = []
                        for t in range(2):
                            xt, xbuf = xs[t]
                            hb, hbuf = hb_ring.get()
                            prea[sbi].append((hb, hbuf, rms_a(xt, xbuf, hb, hbuf)))

                    def pre_b(sbi):
                        xs = pend[sbi]
                        for t in range(2):
                            xt, xbuf = xs[t]
                            hb, hbuf, ctx = prea[sbi][t]
                            rms_b(xt, xbuf, hb, hbuf, ctx)

                    def pre_c(sbi):
                        hT, hTb = hTs[sbi]
                        for t in range(2):
                            hb, hbuf, ctx = prea[sbi][t]
                            transposes(lambda k: hb[:, k * 128:(k + 1) * 128], [hbuf], 8, 0, hT[:, :, t * 128:(t + 1) * 128], [hTb],
                                       evac=("dve" if t == 0 else "act"))
                        del prea[sbi]

                    load_sb4(0)
                    if nsb > 1:
                        load_sb4(1)
                    pre_a(0)
                    pre_b(0)
                    pre_c(0)
                    for g in range(4):
                        load_weight(ff1, w_ff1_d, 8, 4096, 3, stg_ring, wbufs=W1B, c0s=[g * 1024])
                        load_weight(ff2, w_ff2_d, 32, 1024, None, stg_ring, wbufs=W2B, ks=range(g * 8, g * 8 + 8))
                    for sbi in range(nsb):
                        if sbi + 2 < nsb:
                            load_sb4(sbi + 2)
                        xs = pend.pop(sbi)
                        hT, hTb = hTs.pop(sbi)
                        st4 = {}

                        def f1(f):
                            bank = 1 + f % 2
                            S.emit("pe", mm_group(ps[:, bank * 512: bank * 512 + 256], [(ff1[:, k, f * 128:(f + 1) * 128], hT[:, k, :]) for k in range(8)]),
                                   reads=[hTb] + [W1B[(k, f // 8)] for k in range(8)], writes=[PB[bank]])
                            rl, rlb = rl_ring.get()
                            S.emit("act", lambda: nc.scalar.activation(out=rl[:], in_=ps[:, bank * 512: bank * 512 + 256], func=AF.Relu), reads=[PB[bank]], writes=[rlb], n=256)
                            at, atb = aT_ring.get()
                            if f % 2 == 0:
                                S.emit("dve", lambda: nc.vector.tensor_tensor(out=at[:], in0=rl[:], in1=rl[:], op=ALU.mult), reads=[rlb], writes=[atb])
                            else:
                                S.emit("pool", lambda: nc.gpsimd.tensor_tensor(out=at[:], in0=rl[:], in1=rl[:], op=ALU.mult), reads=[rlb], writes=[atb])
                            st4[f] = (at, atb)

                        def f2(f):
                            at, atb = st4.pop(f)
                            fns = []
                            for t in range(2):
                                for half in range(2):
                                    bank = 3 + t * 2 + half
                                    fns.append(lambda t=t, half=half, bank=bank: nc.tensor.matmul(
                                        ps[:, bank * 512:(bank + 1) * 512], lhsT=at[:, t * 128:(t + 1) * 128], rhs=ff2[:, f, half * 512:(half + 1) * 512],
                                        start=(f == 0), stop=(f == 31)))
                            S.emit("pe", fns, reads=[atb, W2B[(f, 0)]], writes=[PB[3], PB[4], PB[5], PB[6]], cost=990.0)

                        f1(0)
                        for f in range(32):
                            if f + 1 < 32:
                                f1(f + 1)
                            f2(f)
                            if sbi + 1 < nsb:
                                if f == 6:
                                    pre_a(sbi + 1)
                                elif f == 12:
                                    pre_b(sbi + 1)
                                elif f == 18:
                                    pre_c(sbi + 1)
                        for t in range(2):
                            xt, xbuf = xs[t]
                            b0 = 3 + t * 2
                            S.emit("dve", lambda: nc.vector.tensor_tensor(out=xt[:], in0=xt[:], in1=ps[:, b0 * 512:(b0 + 2) * 512], op=ALU.add),
                                   reads=[xbuf, PB[b0], PB[b0 + 1]], writes=[xbuf], n=1024)
                            r0 = sbi * 256 + t * 128
                            S.dma("sp", out_d[r0:r0 + 128, :], xt[:], reads=[xbuf], nbytes=524288)
                    S.barrier()
            S.barrier()
    return nc


def _rope_table(pos):
    inv_freq = (np.float32(500000.0) ** (-np.arange(0, 16, 2, dtype=np.float32) / np.float32(16))).astype(np.float32)
    ang = (pos.astype(np.float32)[:, None] * inv_freq[None, :]).astype(np.float32)
    return np.concatenate([np.cos(ang), np.sin(ang), -np.sin(ang)], axis=1).astype(np.float32)


def _pt(tab):
    T = tab.shape[0]
    return np.ascontiguousarray(tab.reshape(T // 128, 128, 24).transpose(1, 0, 2).reshape(128, (T // 128) * 24))


def _col(v):
    return np.ascontiguousarray(np.asarray(v, np.float32).reshape(-1, 128).T)


def make_in_maps(inp, n_cores, TA, TBK, TBQ):
    f = lambda a: np.ascontiguousarray(np.asarray(a, dtype=np.float32))
    xp, xs = f(inp["x_prompt"]), f(inp["x_sample"])
    mp, ms = f(inp["mem_prompt"]), f(inp["mem_sample"])
    shared = {
        "w_in": f(inp["w_in"][0]), "w_out": f(inp["w_out"][0]), "wm_q": f(inp["wm_q"][0]), "wm_kv": f(inp["wm_kv"][0]),
        "wm_o": f(inp["wm_o"][0]), "w_ff1": f(inp["w_ff1"][0]), "w_ff2": f(inp["w_ff2"][0]),
        "gcol": np.ascontiguousarray(np.concatenate([_col(inp["g_mix"][0]), _col(inp["g_memq"][0]), _col(inp["g_memkv"][0]), _col(inp["g_mlp"][0])], axis=1)),
        "rep128": np.ascontiguousarray(np.broadcast_to(f(inp["g_subln"][0])[None, :], (128, 128))),
        "ident": np.eye(128, dtype=np.float32).astype(ml_dtypes.bfloat16),
        "cs_a": _pt(_rope_table(np.arange(TA))), "cs_bk": _pt(_rope_table(np.arange(TBK))),
    }
    cw, cb = f(inp["conv_w"][0]), f(inp["conv_b"][0])
    cwb = np.zeros((128, 4, 4), np.float32)
    for j in range(4):
        cwb[:, j, 0:3] = cw[:, j * 128:(j + 1) * 128].T
        cwb[:, j, 3] = cb[j * 128:(j + 1) * 128]
    shared["cwb"] = cwb.reshape(128, 16)
    rep = np.stack([np.tile(f(inp["q_norm"][0]), 8), np.tile(f(inp["k_norm"][0]), 8),
                    np.tile(f(inp["q_norm_mem"][0]), 4), np.tile(f(inp["k_norm_mem"][0]), 4)], axis=0)
    shared["rep512"] = np.ascontiguousarray(np.broadcast_to(rep.reshape(1, 2048), (128, 2048)))
    lv = np.stack([f(inp["lambda_q1"][0]), f(inp["lambda_k1"][0]), f(inp["lambda_q2"][0]), f(inp["lambda_k2"][0])], axis=0)
    shared["lamv"] = np.ascontiguousarray(np.broadcast_to(lv.reshape(1, 256), (128, 256)))
    nq = TBK // TBQ
    maps = []
    for c in range(n_cores):
        sb_, qi = c // nq, c % nq
        q0 = qi * TBQ
        m = dict(shared)
        m["xa"] = xp[c]
        m["xbk"] = xs[sb_]
        m["xbq"] = np.ascontiguousarray(xs[sb_][q0:q0 + TBQ])
        hl = np.zeros((2, D), np.float32)
        if q0 > 0:
            hl[0] = xs[sb_][q0 - 1]
        if q0 + TBQ < TBK:
            hl[1] = xs[sb_][q0 + TBQ]
        m["halo"] = hl
        m["mema"] = mp[c]
        m["memb"] = ms[sb_]
        m["cs_bq"] = _pt(_rope_table(np.arange(q0, q0 + TBQ)))
        maps.append(m)
    return maps


_NC_CACHE = {}


def kernel(**inputs):
    TA, TBK, TBQ = 8192, 16384, 4096
    key = (TA, TBK, TBQ)
    if key not in _NC_CACHE:
        _NC_CACHE[key] = build_nc(TA, TBK, TBQ)
    nc = _NC_CACHE[key]
    in_maps = make_in_maps(inputs, 8, TA, TBK, TBQ)
    res = run_bass_kernel_spmd(nc, in_maps, core_ids=list(range(8)))
    outs = [np.asarray(r["out"], dtype=np.float32) for r in res.results]
    y_prompt = np.stack([o[:TA] for o in outs], axis=0)
    nq = TBK // TBQ
    y_sample = np.stack([np.concatenate([outs[b * nq + q][TA:] for q in range(nq)], axis=0) for b in range(8 // nq)], axis=0)
    return (y_prompt, y_sample)
```

```python
import math
import os
import heapq
import types
from contextlib import ExitStack

import numpy as np
import ml_dtypes
import concourse.bass as bass
import concourse.mybir as mybir
from concourse.alu_op_type import AluOpType as ALU
from concourse.bass_utils import run_bass_kernel_spmd

F32 = mybir.dt.float32
BF16 = mybir.dt.bfloat16
AF = mybir.ActivationFunctionType
AX = mybir.AxisListType

D = 1024
EPS = 1e-6
LAM_INIT = 0.8 - 0.6 * math.exp(-0.3 * 0)
MEMT = 256
KVCAP = 32768


class Buf:
    __slots__ = ("w", "r")

    def __init__(self):
        self.w = None
        self.r = []


class Lane:
    def __init__(self, idx, sem):
        self.idx = idx
        self.sem = sem
        self.count = 0


class Node:
    __slots__ = ("q", "fns", "deps", "cost", "lat", "prio", "succ", "nd", "ready", "start", "finish", "token", "dma", "noncontig")

    def __init__(self, q, fns, deps, cost, lat, prio, dma=None, noncontig=False):
        self.q, self.fns, self.deps, self.cost, self.lat, self.prio = q, fns, deps, cost, lat, prio
        self.succ = []
        self.nd = 0
        self.ready = 0.0
        self.start = self.finish = 0.0
        self.token = None
        self.dma = dma
        self.noncontig = noncontig


def _freeze(fn, depth=0):
    if not isinstance(fn, types.FunctionType) or fn.__closure__ is None or depth > 2:
        return fn
    cells = []
    for c in fn.__closure__:
        try:
            v = c.cell_contents
        except ValueError:
            cells.append(c)
            continue
        if isinstance(v, types.FunctionType) and v.__closure__ is not None:
            v = _freeze(v, depth + 1)
        cells.append(types.CellType(v))
    g = types.FunctionType(fn.__code__, fn.__globals__, fn.__name__, fn.__defaults__, tuple(cells))
    g.__kwdefaults__ = fn.__kwdefaults__
    return g


_FIX = {"pe": 30.0, "act": 240.0, "dve": 280.0, "pool": 300.0, "sp": 100.0}
_PER = {"act": 0.833, "dve": 1.05, "pool": 2.1}


class Sched:
    def __init__(self, nc, st, nlanes=12):
        self.nc = nc
        self.eng = {"pe": nc.tensor, "act": nc.scalar, "dve": nc.vector, "pool": nc.gpsimd, "sp": nc.sync}
        self.sem = {k: st.enter_context(nc.semaphore("s_" + k)) for k in self.eng}
        self.count = {k: 0 for k in self.eng}
        self.waited = {k: {} for k in self.eng}
        self.lanes = [Lane(i, st.enter_context(nc.semaphore("s_l%d" % i))) for i in range(nlanes)]
        self.next_lane = 0
        self.batch = []
        self.reorder = True
        self.slack = 500.0

    def _collect(self, reads, writes):
        deps = []
        for b in reads:
            if b.w is not None:
                deps.append(b.w)
        for b in writes:
            if b.w is not None:
                deps.append(b.w)
            deps.extend(b.r)
        return deps

    def _mark(self, node, reads, writes):
        for b in reads:
            b.r.append(node)
        for b in writes:
            b.w = node
            b.r = []

    def emit(self, q, fns, reads=(), writes=(), n=256, cost=None):
        if callable(fns):
            fns = [fns]
        if cost is None and getattr(fns, "cost", None):
            cost = fns.cost
        fns = [_freeze(f) for f in fns]
        if cost is None:
            if q == "pe":
                cost = _FIX["pe"] + 180.0 * len(fns)
            else:
                cost = _FIX[q] + _PER[q] * n
        node = Node(q, fns, self._collect(reads, writes), cost, cost, len(self.batch))
        self._mark(node, reads, writes)
        self.batch.append(node)
        return node

    def dma(self, q, out, in_, reads=(), writes=(), nbytes=262144, noncontig=False):
        lat = 2000.0 + nbytes / 120.0
        node = Node(q, None, self._collect(reads, writes), _FIX["sp"], lat, len(self.batch), dma=(out, in_), noncontig=noncontig)
        self._mark(node, reads, writes)
        self.batch.append(node)
        return node

    def _wait(self, q, tok):
        kind, name, val = tok
        if kind == "q" and name == q and q == "pe":
            return
        key = (kind, name)
        if self.waited[q].get(key, 0) >= val:
            return
        self.waited[q][key] = val
        sem = self.sem[name] if kind == "q" else self.lanes[name].sem
        self.eng[q].wait_ge(sem, val)

    def _emit_node(self, node):
        q = node.q
        need = {}
        for d in node.deps:
            kind, name, val = d.token
            if need.get((kind, name), 0) < val:
                need[(kind, name)] = val
        for (kind, name), val in need.items():
            self._wait(q, (kind, name, val))
        if node.dma is not None:
            lane = self.lanes[self.next_lane]
            self.next_lane = (self.next_lane + 1) % len(self.lanes)
            if lane.count:
                self._wait(q, ("l", lane.idx, lane.count))
            out, in_ = node.dma
            if node.noncontig:
                with self.nc.allow_non_contiguous_dma(reason="single-column scatter"):
                    ins = self.eng[q].dma_start(out=out, in_=in_)
            else:
                ins = self.eng[q].dma_start(out=out, in_=in_)
            lane.count += 16
            ins.then_inc(lane.sem, 16)
            node.token = ("l", lane.idx, lane.count)
        else:
            ins = None
            for f in node.fns:
                ins = f()
            self.count[q] += 1
            ins.then_inc(self.sem[q], 1)
            node.token = ("q", q, self.count[q])
        node.fns = None
        node.dma = None

    def flush(self):
        batch = self.batch
        self.batch = []
        if not batch:
            return
        if not self.reorder:
            for nd in batch:
                nd.deps = [d for d in nd.deps]
                self._emit_node(nd)
            return
        SYNC = self.slack
        for nd in batch:
            live = []
            seen = set()
            for d in nd.deps:
                if id(d) in seen:
                    continue
                seen.add(id(d))
                live.append(d)
                if d.token is None:
                    d.succ.append(nd)
                    nd.nd += 1
            nd.deps = live
            nd.ready = 0.0
        fut = {q: [] for q in self.eng}
        avail = {q: [] for q in self.eng}
        free = {q: 0.0 for q in self.eng}
        for nd in batch:
            if nd.nd == 0:
                heapq.heappush(fut[nd.q], (0.0, nd.prio, nd))
        order = []
        left = len(batch)
        while left:
            best_q, best_t = None, None
            for q in self.eng:
                f, a = fut[q], avail[q]
                while f and f[0][0] <= free[q]:
                    r, p, x = heapq.heappop(f)
                    heapq.heappush(a, (p, x))
                if a:
                    t = free[q]
                elif f:
                    t = f[0][0]
                else:
                    continue
                if best_t is None or t < best_t:
                    best_q, best_t = q, t
            q = best_q
            if avail[q]:
                p, x = heapq.heappop(avail[q])
            else:
                r, p, x = heapq.heappop(fut[q])
            x.start = max(free[q], x.ready)
            free[q] = x.start + x.cost
            x.finish = x.start + x.lat
            order.append(x)
            left -= 1
            for s_ in x.succ:
                s_.nd -= 1
                rt = x.finish + (0.0 if (s_.q == x.q) else SYNC)
                if rt > s_.ready:
                    s_.ready = rt
                if s_.nd == 0:
                    heapq.heappush(fut[s_.q], (s_.ready, s_.prio, s_))
            x.succ = None
        if os.environ.get("SCHED_DEBUG") and len(order) > 500:
            fr = {q: 0.0 for q in self.eng}
            fin = {}
            for nd in sorted(order, key=lambda x: x.prio):
                st_ = fr[nd.q]
                for d in nd.deps:
                    if id(d) in fin:
                        st_ = max(st_, fin[id(d)] + (0.0 if d.q == nd.q else 100.0))
                fr[nd.q] = st_ + nd.cost
                fin[id(nd)] = st_ + nd.lat
            print("   program-order makespan_us=%.1f" % (max(fin.values()) / 1e3))
            busy = {}
            for nd in order:
                busy[nd.q] = busy.get(nd.q, 0.0) + nd.cost
            print("SCHED batch n=%d makespan_us=%.1f busy_us=%s" % (len(order), max(x.finish for x in order) / 1e3,
                                                                   {k: round(v / 1e3, 1) for k, v in busy.items()}), flush=True)
        for nd in order:
            self._emit_node(nd)

    def barrier(self):
        self.flush()
        for q in self.eng:
            for p in self.eng:
                if p != q and self.count[p]:
                    self._wait(q, ("q", p, self.count[p]))
            for lane in self.lanes:
                if lane.count:
                    self._wait(q, ("l", lane.idx, lane.count))


def _mm_cost(N):
    return 240.0 if N >= 512 else (126.0 if N >= 256 else 75.0)


class FnList(list):
    cost = None


class Ring:
    def __init__(self, st, nc, name, shape, dt, n, nb=1):
        self.items = [(st.enter_context(nc.sbuf_tensor("%s%d" % (name, i), list(shape), dt)),
                       Buf() if nb == 1 else [Buf() for _ in range(nb)]) for i in range(n)]
        self.i = 0

    def get(self):
        it = self.items[self.i % len(self.items)]
        self.i += 1
        return it


def run_pipeline(gens, depth):
    active = []
    gens = iter(gens)
    more = True
    while True:
        if more and len(active) < depth:
            try:
                active.append(next(gens))
            except StopIteration:
                more = False
        if not active:
            break
        for g in list(active):
            try:
                next(g)
            except StopIteration:
                active.remove(g)


def build_nc(TA, TBK, TBQ, phases=(0, 1, 2, 3, 4)):
    NT = TA + TBQ
    nc = bass.Bass("TRN2", target_bir_lowering=False)

    def din(name, shape, dt=F32):
        return nc.dram_tensor(name, list(shape), dt, kind="ExternalInput").ap()

    def dscr(name, shape, dt):
        return nc.dram_tensor(name, list(shape), dt, kind="Internal").ap()

    xa = din("xa", [TA, D])
    xbk = din("xbk", [TBK, D])
    xbq = din("xbq", [TBQ, D])
    halo = din("halo", [2, D])
    mem_d = [din("mema", [MEMT, D]), din("memb", [MEMT, D])]
    cs_d = {"a": din("cs_a", [128, (TA // 128) * 24]), "bk": din("cs_bk", [128, (TBK // 128) * 24]),
            "bq": din("cs_bq", [128, (TBQ // 128) * 24])}
    w_in_d = din("w_in", [D, 3072])
    w_out_d = din("w_out", [D, D])
    wm_q_d = din("wm_q", [D, 512])
    wm_kv_d = din("wm_kv", [D, D])
    wm_o_d = din("wm_o", [512, D])
    w_ff1_d = din("w_ff1", [D, 4096])
    w_ff2_d = din("w_ff2", [4096, D])
    gcol_d = din("gcol", [128, 32])
    cwb_d = din("cwb", [128, 16])
    rep512_d = din("rep512", [128, 4 * 512])
    rep128_d = din("rep128", [128, 128])
    lamv_d = din("lamv", [128, 4 * 64])
    ident_d = din("ident", [128, 128], BF16)
    out_d = nc.dram_tensor("out", [NT, D], F32, kind="ExternalOutput").ap()

    KTA, KTB = TA // 128, TBK // 128
    qT_d = [dscr("qT_a", [4, 128, TA], BF16), dscr("qT_b", [4, 128, TBQ], BF16)]
    kT_d = [dscr("kT_a", [4, 128, TA], BF16), dscr("kT_b", [4, 128, TBK], BF16)]
    v_d = [dscr("v_a", [4, 128, KTA, 130], BF16), dscr("v_b", [4, 128, KTB, 130], BF16)]
    mixT_d = dscr("mixT", [8, 128, NT], BF16)
    x2s_d = dscr("x2s", [NT, D], F32)
    mixT_p = mixT_d.rearrange("c p t -> p c t")

    with ExitStack() as top:
        S = Sched(nc, top)
        ps = top.enter_context(nc.psum_tensor("ps", [128, 4096], F32))
        psb = ps.bitcast(BF16)
        PB = [Buf() for _ in range(8)]

        def sb(st, name, shape, dt):
            return st.enter_context(nc.sbuf_tensor(name, list(shape), dt))

        ident_t = sb(top, "ident_t", [128, 128], BF16)
        gcol_t = sb(top, "gcol_t", [128, 32], F32)
        cwb_t = sb(top, "cwb_t", [128, 16], F32)
        rep512_t = sb(top, "rep512_t", [128, 4, 512], F32)
        rep128_t = sb(top, "rep128_t", [128, 128], F32)
        lamv_t = sb(top, "lamv_t", [128, 4, 64], F32)
        nhalf = sb(top, "nhalf", [128, 8], F32)
        lam_t = sb(top, "lam_t", [128, 16], F32)
        kmT = [sb(top, "kmT%d" % j, [128, 4, MEMT], BF16) for j in range(2)]
        vm = [sb(top, "vm%d" % j, [128, 2, 4, 130], BF16) for j in range(2)]
        CONST = Buf()
        CONST_ = CONST
        st_ring = Ring(top, nc, "st", [128, 4], F32, 6)
        st8_ring = Ring(top, nc, "st8", [128, 24], F32, 4)

        block = top.enter_context(nc.Block())

        def rms(xt, xbuf, hb, hbuf):
            rms_b(xt, xbuf, hb, hbuf, rms_a(xt, xbuf, hb, hbuf))

        def rms_b(xt, xbuf, hb, hbuf, ctx):
            stt, stb = ctx
            S.emit("act", lambda: nc.scalar.activation(out=hb[:], in_=xt[:], func=AF.Copy, scale=stt[:, 2:3]),
                   reads=[xbuf, stb], writes=[hbuf], n=1136)

        def rms_a(xt, xbuf, hb, hbuf):
            stt, stb = st_ring.get()
            S.emit("act", lambda: nc.scalar.activation(out=hb[:], in_=xt[:], func=AF.Square, accum_out=stt[:, 0:1]),
                   reads=[xbuf], writes=[hbuf, stb], n=1136)
            S.emit("pool", lambda: nc.gpsimd.tensor_scalar(out=stt[:, 1:2], in0=stt[:, 0:1], scalar1=1.0 / D, scalar2=EPS,
                                                          op0=ALU.mult, op1=ALU.add), reads=[stb], writes=[stb], cost=230.0)
            S.emit("pool", lambda: nc.gpsimd.tensor_tensor(out=stt[:, 2:3], in0=stt[:, 1:2], in1=nhalf[:, 0:1], op=ALU.pow),
                   reads=[stb], writes=[stb], cost=600.0)
            return stt, stb

        def transposes(src_fn, sbufs, n, bank, dst_ap, dbufs, evac="dve"):
            srcs = [src_fn(k) for k in range(n)]
            S.emit("pe", [lambda k=k: nc.tensor.transpose(out=psb[:, bank * 1024 + k * 128: bank * 1024 + (k + 1) * 128],
                                                          in_=srcs[k], identity=ident_t[:]) for k in range(n)],
                   reads=sbufs, writes=[PB[bank]], cost=30.0 + 70.0 * n)
            src_ps = psb[:, bank * 1024: bank * 1024 + n * 128].rearrange("p (k t) -> p k t", k=n)
            if evac == "dve":
                S.emit("dve", lambda: nc.vector.tensor_copy(out=dst_ap, in_=src_ps), reads=[PB[bank]], writes=dbufs, n=128 * n)
            else:
                S.emit("act", lambda: nc.scalar.copy(out=dst_ap, in_=src_ps), reads=[PB[bank]], writes=dbufs, n=128 * n)

        def groupnorm(bank, G, gain_ap, rings):
            qsb_ring, sq_ring, qn_ring = rings
            gs = 512 // G
            qs_, qsb = qsb_ring.get()
            S.emit("act", lambda: nc.scalar.copy(out=qs_[:], in_=ps[:, bank * 512:(bank + 1) * 512]), reads=[PB[bank]], writes=[qsb], n=512)
            sq, sqb = sq_ring.get()
            S.emit("dve", lambda: nc.vector.tensor_tensor(out=sq[:], in0=qs_[:], in1=qs_[:], op=ALU.mult), reads=[qsb], writes=[sqb], n=512)
            stt, stb = st8_ring.get()
            S.emit("dve", lambda: nc.vector.tensor_reduce(out=stt[:, 0:G], in_=sq[:].rearrange("p (g d) -> p g d", g=G),
                                                          axis=AX.X, op=ALU.add), reads=[sqb], writes=[stb], n=512)
            S.emit("pool", lambda: nc.gpsimd.tensor_scalar(out=stt[:, 8:8 + G], in0=stt[:, 0:G], scalar1=1.0 / gs, scalar2=EPS,
                                                          op0=ALU.mult, op1=ALU.add), reads=[stb], writes=[stb], cost=230.0)
            S.emit("pool", lambda: nc.gpsimd.tensor_tensor(out=stt[:, 16:16 + G], in0=stt[:, 8:8 + G], in1=nhalf[:, 0:G], op=ALU.pow),
                   reads=[stb], writes=[stb], cost=400.0 + 170.0 * G)
            S.emit("dve", lambda: nc.vector.tensor_tensor(out=sq[:].rearrange("p (g d) -> p g d", g=G),
                                                          in0=qs_[:].rearrange("p (g d) -> p g d", g=G),
                                                          in1=stt[:, 16:16 + G].unsqueeze(2).broadcast_to([128, G, gs]), op=ALU.mult),
                   reads=[qsb, stb], writes=[sqb])
            qn, qnb = qn_ring.get()
            S.emit("dve", lambda: nc.vector.tensor_tensor(out=qn[:], in0=sq[:], in1=gain_ap, op=ALU.mult), reads=[sqb], writes=[qnb])
            return qn, qnb

        def load_weight(dst, w_d, KC, cols, gidx, stg_ring, engs=("dve", "act"), wbufs=None, ks=None, c0s=None):
            i = 0
            for k in (range(KC) if ks is None else ks):
                for c0 in (range(0, cols, 1024) if c0s is None else c0s):
                    cw = min(1024, cols - c0)
                    CONST = CONST_ if wbufs is None else wbufs[(k, c0 // 1024)]
                    stg, stgb = stg_ring.get()
                    S.dma("sp", stg[:, 0:cw], w_d[k * 128:(k + 1) * 128, c0:c0 + cw], writes=[stgb], nbytes=512 * cw)
                    e = engs[i % len(engs)]
                    i += 1
                    o = dst[:, k, c0:c0 + cw]
                    if gidx is None:
                        if e == "dve":
                            S.emit("dve", lambda: nc.vector.tensor_copy(out=o, in_=stg[:, 0:cw]), reads=[stgb], writes=[CONST], n=1024)
                        elif e == "pool":
                            S.emit("pool", lambda: nc.gpsimd.tensor_copy(out=o, in_=stg[:, 0:cw]), reads=[stgb], writes=[CONST], n=1024)
                        else:
                            S.emit("act", lambda: nc.scalar.copy(out=o, in_=stg[:, 0:cw]), reads=[stgb], writes=[CONST], n=1024)
                    else:
                        g = gcol_t[:, gidx * 8 + k: gidx * 8 + k + 1]
                        if e == "dve":
                            S.emit("dve", lambda: nc.vector.tensor_scalar(out=o, in0=stg[:, 0:cw], scalar1=g, scalar2=None, op0=ALU.mult),
                                   reads=[stgb], writes=[CONST], n=1024)
                        elif e == "pool":
                            S.emit("pool", lambda: nc.gpsimd.tensor_scalar(out=o, in0=stg[:, 0:cw], scalar1=g, scalar2=None, op0=ALU.mult),
                                   reads=[stgb], writes=[CONST], n=1024)
                        else:
                            S.emit("act", lambda: nc.scalar.activation(out=o, in_=stg[:, 0:cw], func=AF.Copy, scale=g),
                                   reads=[stgb], writes=[CONST], n=1024)

        def gn1(bank, G, qs_ring, sq_ring, g8_ring):
            gs = 512 // G
            qs_, qsb = qs_ring.get()
            S.emit("act", lambda: nc.scalar.copy(out=qs_[:], in_=ps[:, bank * 512:(bank + 1) * 512]), reads=[PB[bank]], writes=[qsb], n=512)
            sq, sqb = sq_ring.get()
            S.emit("act", lambda: nc.scalar.activation(out=sq[:], in_=ps[:, bank * 512:(bank + 1) * 512], func=AF.Square),
                   reads=[PB[bank]], writes=[sqb], n=512)
            stt, stb = g8_ring.get()
            S.emit("dve", lambda: nc.vector.tensor_reduce(out=stt[:, 0:G], in_=sq[:].rearrange("p (g d) -> p g d", g=G),
                                                          axis=AX.X, op=ALU.add), reads=[sqb], writes=[stb], n=512)
            S.emit("pool", lambda: nc.gpsimd.tensor_scalar(out=stt[:, 8:8 + G], in0=stt[:, 0:G], scalar1=1.0 / gs, scalar2=EPS,
                                                          op0=ALU.mult, op1=ALU.add), reads=[stb], writes=[stb], cost=230.0)
            S.emit("pool", lambda: nc.gpsimd.tensor_tensor(out=stt[:, 16:16 + G], in0=stt[:, 8:8 + G], in1=nhalf[:, 0:G], op=ALU.pow),
                   reads=[stb], writes=[stb], cost=400.0 + 170.0 * G)
            return qs_, qsb, stt, stb

        def gn2(ctx, G, gain_ap):
            qs_, qsb, stt, stb = ctx
            gs = 512 // G
            q3 = qs_[:].rearrange("p (g d) -> p g d", g=G)
            S.emit("dve", lambda: nc.vector.tensor_tensor(out=q3, in0=q3, in1=stt[:, 16:16 + G].unsqueeze(2).broadcast_to([128, G, gs]), op=ALU.mult),
                   reads=[qsb, stb], writes=[qsb], n=512)
            S.emit("pool", lambda: nc.gpsimd.tensor_tensor(out=qs_[:], in0=qs_[:], in1=gain_ap, op=ALU.mult), reads=[qsb], writes=[qsb], n=512)
            return qs_, qsb

        def mm_group(out_ap, pairs, start=True, stop=True, skip=False):
            n = len(pairs)
            fns = FnList()
            fns.cost = 30.0 + n * _mm_cost(pairs[0][1].free_size())
            for i, (l, r) in enumerate(pairs):
                st_ = start and i == 0
                sp_ = stop and i == n - 1
                if skip:
                    fns.append(lambda l=l, r=r, st_=st_, sp_=sp_: nc.tensor.matmul(out_ap, lhsT=l, rhs=r, start=st_, stop=sp_, skip_group_check=True))
                else:
                    fns.append(lambda l=l, r=r, st_=st_, sp_=sp_: nc.tensor.matmul(out_ap, lhsT=l, rhs=r, start=st_, stop=sp_))
            return fns

        @block.sync
        def _(sync):
            with ExitStack() as p0:
                for t_, d_ in ((ident_t, ident_d), (gcol_t, gcol_d), (cwb_t, cwb_d), (rep128_t, rep128_d)):
                    S.dma("sp", t_[:], d_, writes=[CONST])
                S.dma("sp", rep512_t[:].rearrange("p a b -> p (a b)"), rep512_d, writes=[CONST])
                S.dma("sp", lamv_t[:].rearrange("p a b -> p (a b)"), lamv_d, writes=[CONST])
                S.emit("pool", lambda: nc.gpsimd.memset(nhalf[:], -0.5), writes=[CONST])
                for j in range(2):
                    S.emit("pool", lambda: nc.gpsimd.memset(vm[j][:], 1.0), writes=[CONST])
                S.barrier()
                scr = sb(p0, "lscr", [128, 64], F32)
                LB = Buf()
                for i in range(2):
                    S.emit("dve", lambda: nc.vector.tensor_tensor(out=scr[:], in0=lamv_t[:, 2 * i, :], in1=lamv_t[:, 2 * i + 1, :], op=ALU.mult),
                           writes=[LB])
                    S.emit("dve", lambda: nc.vector.tensor_reduce(out=lam_t[:, i:i + 1], in_=scr[:], axis=AX.X, op=ALU.add), reads=[LB], writes=[LB])
                S.emit("act", lambda: nc.scalar.activation(out=lam_t[:, 2:4], in_=lam_t[:, 0:2], func=AF.Exp), reads=[LB], writes=[LB])
                S.emit("dve", lambda: nc.vector.tensor_tensor(out=lam_t[:, 4:5], in0=lam_t[:, 2:3], in1=lam_t[:, 3:4], op=ALU.subtract), reads=[LB], writes=[LB])
                S.emit("dve", lambda: nc.vector.tensor_scalar(out=lam_t[:, 5:6], in0=lam_t[:, 4:5], scalar1=LAM_INIT, scalar2=-1.0,
                                                             op0=ALU.add, op1=ALU.mult), reads=[LB], writes=[LB])
                nlam = lam_t[:, 5:6]

                stg_ring = Ring(p0, nc, "stg", [128, 1024], F32, 3)
                wkv = sb(p0, "wkv", [128, 8, 1024], BF16)
                load_weight(wkv, wm_kv_d, 8, 1024, 2, stg_ring)
                S.barrier()
                x_ring = Ring(p0, nc, "x0_", [128, D], F32, 2)
                hb_ring = Ring(p0, nc, "hb0_", [128, D], BF16, 2)
                hT_ring = Ring(p0, nc, "hT0_", [128, 8, 128], BF16, 2)
                rings = (Ring(p0, nc, "qs0_", [128, 512], F32, 2), Ring(p0, nc, "sq0_", [128, 512], F32, 2), Ring(p0, nc, "qn0_", [128, 512], F32, 2))
                qb_ring = Ring(p0, nc, "qb0_", [128, 512], BF16, 2)
                for j in range(2):
                    for t in range(2):
                        xt, xbuf = x_ring.get()
                        S.dma("sp", xt[:], mem_d[j][t * 128:(t + 1) * 128, :], writes=[xbuf])
                        hb, hbuf = hb_ring.get()
                        rms(xt, xbuf, hb, hbuf)
                        hT, hTb = hT_ring.get()
                        transposes(lambda k: hb[:, k * 128:(k + 1) * 128], [hbuf], 8, 0, hT[:], [hTb])
                        S.emit("pe", mm_group(ps[:, 512:1024], [(hT[:, k, :], wkv[:, k, 0:512]) for k in range(8)]), reads=[hTb], writes=[PB[1]])
                        S.emit("pe", mm_group(ps[:, 1024:1536], [(hT[:, k, :], wkv[:, k, 512:1024]) for k in range(8)]), reads=[hTb], writes=[PB[2]])
                        kn, knb = groupnorm(1, 4, rep512_t[:, 3, :], rings)
                        kb, kbb = qb_ring.get()
                        S.emit("act", lambda: nc.scalar.copy(out=kb[:], in_=kn[:]), reads=[knb], writes=[kbb])
                        transposes(lambda h: kb[:, h * 128:(h + 1) * 128], [kbb], 4, 3, kmT[j][:, :, t * 128:(t + 1) * 128], [CONST])
                        S.emit("dve", lambda: nc.vector.tensor_copy(out=vm[j][:, t, :, 0:128], in_=ps[:, 1024:1536].rearrange("p (h d) -> p h d", h=4)),
                               reads=[PB[2]], writes=[CONST])
                S.barrier()

            if 1 in phases:
                with ExitStack() as p1:
                    w_in = sb(p1, "w_in_t", [128, 8, 3072], BF16)
                    with ExitStack() as p1s:
                        stg_ring = Ring(p1s, nc, "stg1_", [128, 1024], F32, 3)
                        load_weight(w_in, w_in_d, 8, 3072, 0, stg_ring)
                        S.barrier()
                    cs_t = sb(p1, "cs_t", [128, max(TA, TBK, TBQ) // 128, 24], F32)
                    x_ring = Ring(p1, nc, "x1_", [128, D], F32, 5)
                    hb_ring = Ring(p1, nc, "hb1_", [128, D], BF16, 5)
                    hT_ring = Ring(p1, nc, "hT1_", [128, 8, 256], BF16, 3, nb=2)
                    qs_ring = Ring(p1, nc, "qs1_", [128, 512], F32, 8)
                    sq_ring = Ring(p1, nc, "sq1_", [128, 512], F32, 3)
                    g8_ring = Ring(p1, nc, "g81_", [128, 24], F32, 12)
                    qb_ring = Ring(p1, nc, "qb1_", [128, 512], BF16, 8, nb=2)
                    rt_ring = Ring(p1, nc, "rt1_", [128, 4, 64], F32, 4)
                    qkT_ring = Ring(p1, nc, "qkT1_", [128, 4, 256], BF16, 4)
                    vb_ring = Ring(p1, nc, "vb1_", [128, 4, 2, 130], BF16, 3)
                    csb_ring = Ring(p1, nc, "csb1_", [128, 256], F32, 4)
                    zts = [sb(p1, "zt%d" % i, [128, 4, 258], F32) for i in range(2)]
                    bts = [sb(p1, "bt%d" % i, [128, 4, 257], F32) for i in range(2)]
                    zc = sb(p1, "zc", [128, 4, 2], F32)
                    bc = sb(p1, "bc", [128, 4, 1], F32)
                    zh = sb(p1, "zh", [128, 4, 2], F32)
                    acc = sb(p1, "acc", [128, 256], F32)
                    fl = sb(p1, "fl", [128, 8, 4], F32)
                    yl = sb(p1, "yl", [128, 4], BF16)
                    y_ring = Ring(p1, nc, "y1_", [128, 4, 256], BF16, 3)
                    ZBs, BBs = [Buf(), Buf()], [Buf(), Buf()]
                    ZCB, ZHB, ACCB = Buf(), Buf(), Buf()
                    for it_, ib_ in vb_ring.items:
                        S.emit("pool", lambda: nc.gpsimd.memset(it_[:], 1.0), writes=[ib_])
                    S.barrier()
                    tm_banks = [1, 2, 3]
                    tm_i = [0]
                    cv_banks = [4, 5, 6]
                    cv_i = [0]

                    def nxt(lst, ctr):
                        b = lst[ctr[0] % len(lst)]
                        ctr[0] += 1
                        return b

                    def halo_z():
                        xt, xbuf = x_ring.get()
                        S.emit("pool", lambda: nc.gpsimd.memset(xt[:], 0.0), writes=[xbuf])
                        S.dma("sp", xt[0:2, :], halo, writes=[xbuf])
                        hb, hbuf = hb_ring.get()
                        rms(xt, xbuf, hb, hbuf)
                        hT, hTbs = hT_ring.get()
                        transposes(lambda k: hb[:, k * 128:(k + 1) * 128], [hbuf], 8, 0, hT[:, :, 0:128], [hTbs[0]])
                        for j in range(4):
                            b1, b2 = nxt(cv_banks, cv_i), nxt(cv_banks, cv_i)
                            S.emit("pe", mm_group(ps[:, b1 * 512:b1 * 512 + 2], [(w_in[:, k, 512 + j * 128:512 + (j + 1) * 128], hT[:, k, 0:2]) for k in range(8)]),
                                   reads=[hTbs[0]], writes=[PB[b1]])
                            S.emit("pe", mm_group(ps[:, b2 * 512:b2 * 512 + 2], [(w_in[:, k, 1024 + j * 128:1024 + (j + 1) * 128], hT[:, k, 0:2]) for k in range(8)]),
                                   reads=[hTbs[0]], writes=[PB[b2]])
                            S.emit("act", lambda: nc.scalar.copy(out=fl[:, j, 0:2], in_=ps[:, b1 * 512:b1 * 512 + 2]), reads=[PB[b1]], writes=[ZHB])
                            S.emit("dve", lambda: nc.vector.tensor_tensor(out=zh[:, j, :], in0=fl[:, j, 0:2], in1=ps[:, b2 * 512:b2 * 512 + 2], op=ALU.mult),
                                   reads=[ZHB, PB[b2]], writes=[ZHB])

                    def stream(T, x_ap, cskey, do_conv, do_q, do_k, do_v, job, mix_col0, has_halo):
                        S.dma("sp", cs_t[:, 0:T // 128, :].rearrange("p a b -> p (a b)"), cs_d[cskey], writes=[CONST])
                        S.barrier()
                        if do_conv:
                            if has_halo:
                                halo_z()
                            else:
                                S.emit("dve", lambda: nc.vector.memset(zh[:], 0.0), writes=[ZHB])
                        nblk = T // 256

                        def block_gen(blk):
                            s = blk * 256
                            xts = []
                            for t in range(2):
                                xt, xbuf = x_ring.get()
                                tok0 = s + t * 128
                                S.dma("sp", xt[:], x_ap[tok0:tok0 + 128, :], writes=[xbuf], nbytes=524288)
                                xts.append((xt, xbuf))
                            yield
                            hbs = []
                            for t in range(2):
                                hb, hbuf = hb_ring.get()
                                rms(xts[t][0], xts[t][1], hb, hbuf)
                                hbs.append((hb, hbuf))
                            yield
                            hT, hTbs = hT_ring.get()
                            for t in range(2):
                                hb, hbuf = hbs[t]
                                transposes(lambda k: hb[:, k * 128:(k + 1) * 128], [hbuf], 8, 0, hT[:, :, t * 128:(t + 1) * 128], [hTbs[t]],
                                           evac=("dve" if t == 0 else "act"))
                            yield
                            ctxs = {}
                            vb = None
                            if do_v:
                                vb, vbb = vb_ring.get()
                            for t in range(2):
                                for which in ("q", "k"):
                                    if (which == "q" and not do_q) or (which == "k" and not do_k):
                                        continue
                                    c0 = 1536 if which == "q" else 2048
                                    bank = nxt(tm_banks, tm_i)
                                    S.emit("pe", mm_group(ps[:, bank * 512:(bank + 1) * 512],
                                                          [(hT[:, k, t * 128:(t + 1) * 128], w_in[:, k, c0:c0 + 512]) for k in range(8)]),
                                           reads=[hTbs[t]], writes=[PB[bank]])
                                    ctxs[(which, t)] = gn1(bank, 8, qs_ring, sq_ring, g8_ring)
                                if do_v:
                                    bank = nxt(tm_banks, tm_i)
                                    S.emit("pe", mm_group(ps[:, bank * 512:(bank + 1) * 512],
                                                          [(hT[:, k, t * 128:(t + 1) * 128], w_in[:, k, 2560:3072]) for k in range(8)]),
                                           reads=[hTbs[t]], writes=[PB[bank]])
                                    S.emit("dve", lambda: nc.vector.tensor_copy(out=vb[:, :, t, 0:128], in_=ps[:, bank * 512:(bank + 1) * 512].rearrange("p (h d) -> p h d", h=4)),
                                           reads=[PB[bank]], writes=[vbb], n=512)
                            if do_v:
                                S.dma("sp", v_d[job].rearrange("h p k c -> p h k c")[:, :, blk * 2:blk * 2 + 2, :], vb[:], reads=[vbb])
                            if do_conv:
                                cur = blk % 2
                                zt, bt, ZB, BB = zts[cur], bts[cur], ZBs[cur], BBs[cur]
                                if blk == 0:
                                    S.emit("pool", lambda: nc.gpsimd.memset(zt[:, :, 0:1], 0.0), reads=[ZB], writes=[ZB])
                                    S.emit("pool", lambda: nc.gpsimd.tensor_copy(out=zt[:, :, 1:2], in_=zh[:, :, 0:1]), reads=[ZHB], writes=[ZB])
                                    S.emit("pool", lambda: nc.gpsimd.memset(bt[:, :, 0:1], 0.0), reads=[BB], writes=[BB])
                                else:
                                    S.emit("pool", lambda: nc.gpsimd.tensor_copy(out=zt[:, :, 0:2], in_=zc[:]), reads=[ZCB], writes=[ZB])
                                    S.emit("pool", lambda: nc.gpsimd.tensor_copy(out=bt[:, :, 0:1], in_=bc[:]), reads=[ZCB], writes=[BB])
                                for j in range(4):
                                    bC, bV, bB = nxt(cv_banks, cv_i), nxt(cv_banks, cv_i), nxt(cv_banks, cv_i)
                                    for bnk, c0 in ((bC, 512), (bV, 1024), (bB, 0)):
                                        S.emit("pe", mm_group(ps[:, bnk * 512:bnk * 512 + 256],
                                                              [(w_in[:, k, c0 + j * 128:c0 + (j + 1) * 128], hT[:, k, :]) for k in range(8)]),
                                               reads=hTbs, writes=[PB[bnk]])
                                    cs_, csb_ = csb_ring.get()
                                    S.emit("act", lambda: nc.scalar.copy(out=cs_[:], in_=ps[:, bC * 512:bC * 512 + 256]), reads=[PB[bC]], writes=[csb_])
                                    S.emit("dve", lambda: nc.vector.tensor_tensor(out=zt[:, j, 2:258], in0=cs_[:], in1=ps[:, bV * 512:bV * 512 + 256], op=ALU.mult),
                                           reads=[csb_, PB[bV]], writes=[ZB])
                                    S.emit("act", lambda: nc.scalar.copy(out=bt[:, j, 1:257], in_=ps[:, bB * 512:bB * 512 + 256]), reads=[PB[bB]], writes=[BB])
                                S.emit("pool", lambda: nc.gpsimd.tensor_copy(out=zc[:], in_=zt[:, :, 256:258]), reads=[ZB], writes=[ZCB])
                                S.emit("pool", lambda: nc.gpsimd.tensor_copy(out=bc[:], in_=bt[:, :, 256:257]), reads=[BB], writes=[ZCB])
                            yield
                            qbs = {}
                            for t in range(2):
                                tt = blk * 2 + t
                                for which in ("q", "k"):
                                    if (which, t) not in ctxs:
                                        continue
                                    qn, qnb = gn2(ctxs[(which, t)], 8, rep512_t[:, 0 if which == "q" else 1, :])
                                    qb, qbbs = qb_ring.get()
                                    q3 = qn[:].rearrange("p (g d) -> p g d", g=8)
                                    b3 = qb[:].rearrange("p (g d) -> p g d", g=8)
                                    S.emit("act", lambda: nc.scalar.copy(out=b3[:, :, 16:64], in_=q3[:, :, 16:64]), reads=[qnb], writes=[qbbs[0]], n=384)
                                    rt, rtb = rt_ring.get()
                                    rtf = rt[:].rearrange("p a b -> p (a b)")
                                    t1 = rtf[:, 0:128].rearrange("p (g d) -> p g d", g=8)
                                    t2 = rtf[:, 128:256].rearrange("p (g d) -> p g d", g=8)
                                    r1, r2 = q3[:, :, 0:8], q3[:, :, 8:16]
                                    cos4 = cs_t[:, tt, 0:8].unsqueeze(1).unsqueeze(1).broadcast_to([128, 8, 2, 8])
                                    sinb = cs_t[:, tt, 8:16].unsqueeze(1).broadcast_to([128, 8, 8])
                                    nsinb = cs_t[:, tt, 16:24].unsqueeze(1).broadcast_to([128, 8, 8])
                                    S.emit("dve", lambda: nc.vector.tensor_tensor(out=t1.rearrange("p g (h d) -> p g h d", h=2),
                                                                                  in0=q3[:, :, 0:16].rearrange("p g (h d) -> p g h d", h=2), in1=cos4, op=ALU.mult),
                                           reads=[qnb], writes=[rtb], n=128)
                                    S.emit("dve", lambda: nc.vector.tensor_tensor(out=t2[:, :, 0:8], in0=r2, in1=nsinb, op=ALU.mult), reads=[qnb], writes=[rtb], n=64)
                                    S.emit("dve", lambda: nc.vector.tensor_tensor(out=t2[:, :, 8:16], in0=r1, in1=sinb, op=ALU.mult), reads=[qnb], writes=[rtb], n=64)
                                    S.emit("dve", lambda: nc.vector.tensor_tensor(out=b3[:, :, 0:16], in0=t1, in1=t2, op=ALU.add), reads=[rtb], writes=[qbbs[1]], n=128)
                                    qbb = qbbs
                                    qbs[(which, t)] = (qb, qbb)
                            if do_conv:
                                yb, ybb = y_ring.get()
                                for j in range(4):
                                    w0, w1, w2, bb_ = (cwb_t[:, j * 4 + i:j * 4 + i + 1] for i in range(4))
                                    S.emit("dve", lambda: nc.vector.tensor_scalar(out=acc[:], in0=zt[:, j, 1:257], scalar1=w1, scalar2=bb_, op0=ALU.mult, op1=ALU.add),
                                           reads=[ZB], writes=[ACCB])
                                    S.emit("dve", lambda: nc.vector.scalar_tensor_tensor(out=acc[:], in0=zt[:, j, 0:256], scalar=w0, in1=acc[:], op0=ALU.mult, op1=ALU.add),
                                           reads=[ZB, ACCB], writes=[ACCB])
                                    S.emit("dve", lambda: nc.vector.scalar_tensor_tensor(out=acc[:], in0=zt[:, j, 2:258], scalar=w2, in1=acc[:], op0=ALU.mult, op1=ALU.add),
                                           reads=[ZB, ACCB], writes=[ACCB])
                                    S.emit("dve", lambda: nc.vector.tensor_tensor(out=yb[:, j, :], in0=bt[:, j, 0:256], in1=acc[:], op=ALU.mult),
                                           reads=[BB, ACCB], writes=[ybb])
                                jj0 = 1 if blk == 0 else 0
                                c_lo = mix_col0 + s - 1 + jj0
                                S.dma("sp", mixT_p[:, 0:4, c_lo:mix_col0 + s + 255], yb[:, :, jj0:256], reads=[ybb])
                            yield
                            for which, d_ in (("q", qT_d), ("k", kT_d)):
                                if (which, 0) not in qbs:
                                    continue
                                dstT, dstTb = qkT_ring.get()
                                for t in range(2):
                                    qb, qbb = qbs[(which, t)]
                                    transposes(lambda h: qb[:, h * 128:(h + 1) * 128], list(qbb), 4, 7, dstT[:, :, t * 128:(t + 1) * 128], [dstTb], evac="act")
                                S.dma("sp", d_[job].rearrange("h p t -> p h t")[:, :, s:s + 256], dstT[:], reads=[dstTb])

                        run_pipeline((block_gen(b) for b in range(nblk)), 6)
                        if do_conv:
                            FB = Buf()
                            cw3 = cwb_t[:].rearrange("p (j i) -> p j i", i=4)
                            f = lambda i: fl[:, i, :]
                            S.emit("dve", lambda: nc.vector.tensor_tensor(out=f(0), in0=zc[:, :, 1], in1=cw3[:, :, 1], op=ALU.mult), reads=[ZCB], writes=[FB])
                            S.emit("dve", lambda: nc.vector.tensor_tensor(out=f(1), in0=f(0), in1=cw3[:, :, 3], op=ALU.add), reads=[FB], writes=[FB])
                            S.emit("dve", lambda: nc.vector.tensor_tensor(out=f(2), in0=zc[:, :, 0], in1=cw3[:, :, 0], op=ALU.mult), reads=[ZCB, FB], writes=[FB])
                            S.emit("dve", lambda: nc.vector.tensor_tensor(out=f(3), in0=f(1), in1=f(2), op=ALU.add), reads=[FB], writes=[FB])
                            S.emit("dve", lambda: nc.vector.tensor_tensor(out=f(4), in0=zh[:, :, 1], in1=cw3[:, :, 2], op=ALU.mult), reads=[ZHB, FB], writes=[FB])
                            S.emit("dve", lambda: nc.vector.tensor_tensor(out=f(5), in0=f(3), in1=f(4), op=ALU.add), reads=[FB], writes=[FB])
                            S.emit("dve", lambda: nc.vector.tensor_tensor(out=yl[:], in0=f(5), in1=bc[:, :, 0], op=ALU.mult), reads=[FB, ZCB], writes=[FB])
                            S.dma("sp", mixT_p[:, 0:4, mix_col0 + T - 1:mix_col0 + T], yl[:].unsqueeze(2), reads=[FB], noncontig=True)
                        S.barrier()

                    stream(TA, xa, "a", True, True, True, True, 0, 0, False)
                    stream(TBK, xbk, "bk", False, False, True, True, 1, 0, False)
                    stream(TBQ, xbq, "bq", True, True, False, False, 1, TA, True)
                    S.barrier()

            if 2 in phases:
                with ExitStack() as p2:
                    S.flush()
                    S.reorder = False
                    kbuf = sb(p2, "kbuf", [128, KVCAP], BF16)
                    vbuf = sb(p2, "vbuf", [128, (KVCAP // 128) * 130], BF16)
                    vbuf3 = vbuf[:].rearrange("p (n c) -> p n c", c=130)
                    qb_ring = Ring(p2, nc, "qblk", [128, 4, 512], BF16, 3)
                    pT_ring = Ring(p2, nc, "pT", [128, 1024], BF16, 3)
                    osb = sb(p2, "osb", [128, 8, 129], F32)
                    ta = sb(p2, "ta", [128, 4, 128], F32)
                    tb = sb(p2, "tb", [128, 4, 128], F32)
                    onb = sb(p2, "onb", [128, 4, 128], BF16)
                    oT_ring = Ring(p2, nc, "oT", [128, 512], BF16, 2)
                    OSB, TAB, TBB, ONB = Buf(), Buf(), Buf(), Buf()
                    scale = 64 ** -0.5
                    SLOT = KVCAP // 2
                    passes = []
                    hpa = min(4, SLOT // TA)
                    for h0 in range(0, 4, hpa):
                        passes.append((0, list(range(h0, h0 + hpa)), TA, TA, 0))
                    hpb = min(4, SLOT // TBK)
                    for h0 in range(0, 4, hpb):
                        passes.append((1, list(range(h0, h0 + hpb)), TBK, TBQ, TA))
                    KBs = [[Buf() for _ in range(4)] for _ in range(2)]
                    VBs = [[Buf() for _ in range(4)] for _ in range(2)]

                    def load_kv(pi):
                        job, heads, Nk, Nq, col0 = passes[pi]
                        sl = pi % 2
                        KT = Nk // 128
                        for hi, h in enumerate(heads):
                            for c0 in range(0, Nk, 4096):
                                cw = min(4096, Nk - c0)
                                S.dma("sp", kbuf[:, sl * SLOT + hi * Nk + c0: sl * SLOT + hi * Nk + c0 + cw], kT_d[job][h, :, c0:c0 + cw],
                                      writes=[KBs[sl][hi]], nbytes=1048576)
                            for k0 in range(0, KT, 32):
                                kw = min(32, KT - k0)
                                S.dma("sp", vbuf3[:, sl * (SLOT // 128) + hi * KT + k0: sl * (SLOT // 128) + hi * KT + k0 + kw, :],
                                      v_d[job][h, :, k0:k0 + kw, :], writes=[VBs[sl][hi]], nbytes=1064960)

                    q0_pre = {}
                    load_kv(0)
                    if len(passes) > 1:
                        load_kv(1)
                    for pi, (job, heads, Nk, Nq, col0) in enumerate(passes):
                        nh = len(heads)
                        KT = Nk // 128
                        sl = pi % 2
                        KB, VB = KBs[sl], VBs[sl]
                        kbase = sl * SLOT
                        vbase = sl * (SLOT // 128)
                        nqb = Nq // 512
                        qblks = {}

                        def load_q(qb):
                            qt, qtb = qb_ring.get()
                            S.dma("sp", qt[:, 0:nh, :], qT_d[job].rearrange("h p t -> p h t")[:, heads[0]:heads[0] + nh, qb * 512:(qb + 1) * 512], writes=[qtb])
                            qblks[qb] = (qt, qtb)

                        its = [(qb, hi, kt) for qb in range(nqb) for hi in range(nh) for kt in range(KT)]
                        if pi in q0_pre:
                            qblks[0] = q0_pre.pop(pi)
                        else:
                            load_q(0)
                        state = {}

                        def qk(i):
                            qb, hi, kt = its[i]
                            if hi == 0 and kt == 0 and qb + 1 < nqb:
                                load_q(qb + 1)
                            qt, qtb = qblks[qb]
                            b0 = (i % 2) * 2
                            fns = []
                            for c in range(2):
                                fns.append(lambda c=c: nc.tensor.matmul(ps[:, (b0 + c) * 512:(b0 + c + 1) * 512],
                                                                        lhsT=kbuf[c * 64:(c + 1) * 64, kbase + hi * Nk + kt * 128: kbase + hi * Nk + (kt + 1) * 128],
                                                                        rhs=qt[c * 64:(c + 1) * 64, hi, :], start=True, stop=True))
                            S.emit("pe", fns, reads=[qtb, KB[hi]], writes=[PB[b0], PB[b0 + 1]], cost=330.0)

                        def ex(i):
                            b0 = (i % 2) * 2
                            pt, ptb = pT_ring.get()
                            state[i] = (pt, ptb)
                            S.emit("act", lambda: nc.scalar.activation(out=pt[:], in_=ps[:, b0 * 512:(b0 + 2) * 512], func=AF.Exp, scale=scale),
                                   reads=[PB[b0], PB[b0 + 1]], writes=[ptb], n=1024)

                        def av(i):
                            qb, hi, kt = its[i]
                            pt, ptb = state.pop(i)
                            fns = []
                            for c in range(2):
                                for qs in range(4):
                                    a = c * 4 + qs
                                    col = (4 + a // 3) * 512 + (a % 3) * 129
                                    st_ = (kt == 0 and a % 3 == 0)
                                    fns.append(lambda c=c, qs=qs, col=col, st_=st_: nc.tensor.matmul(
                                        ps[:, col:col + 129], lhsT=pt[:, c * 512 + qs * 128: c * 512 + (qs + 1) * 128],
                                        rhs=vbuf3[:, vbase + hi * KT + kt, 0:129], start=st_, stop=(kt == KT - 1), skip_group_check=True))
                            S.emit("pe", fns, reads=[ptb, VB[hi]], writes=[PB[4], PB[5], PB[6]], cost=650.0)
                            if kt == KT - 1:
                                finish(qb, hi)

                        def finish(qb, hi):
                            h = heads[hi]
                            for bnk, a0, na in ((4, 0, 3), (5, 3, 3), (6, 6, 2)):
                                S.emit("dve", lambda: nc.vector.tensor_copy(out=osb[:, a0:a0 + na, :],
                                                                            in_=ps[:, bnk * 512: bnk * 512 + na * 129].rearrange("p (a c) -> p a c", c=129)),
                                       reads=[PB[bnk]], writes=[OSB], n=387)
                            stt, stb = st8_ring.get()
                            sums = osb[:, :, 128:129].rearrange("p a c -> p (a c)")
                            S.emit("dve", lambda: nc.vector.reciprocal(out=stt[:, 0:8], in_=sums), reads=[OSB], writes=[stb])
                            S.emit("dve", lambda: nc.vector.tensor_scalar(out=stt[:, 8:12], in0=stt[:, 4:8], scalar1=nlam, scalar2=None, op0=ALU.mult),
                                   reads=[stb], writes=[stb])
                            S.emit("dve", lambda: nc.vector.tensor_tensor(out=ta[:], in0=osb[:, 0:4, 0:128], in1=stt[:, 0:4].unsqueeze(2).broadcast_to([128, 4, 128]), op=ALU.mult),
                                   reads=[OSB, stb], writes=[TAB], n=512)
                            S.emit("dve", lambda: nc.vector.tensor_tensor(out=tb[:], in0=osb[:, 4:8, 0:128], in1=stt[:, 8:12].unsqueeze(2).broadcast_to([128, 4, 128]), op=ALU.mult),
                                   reads=[OSB, stb], writes=[TBB], n=512)
                            S.emit("dve", lambda: nc.vector.tensor_tensor(out=ta[:], in0=ta[:], in1=tb[:], op=ALU.add), reads=[TAB, TBB], writes=[TAB], n=512)
                            S.emit("dve", lambda: nc.vector.tensor_tensor(out=tb[:], in0=ta[:], in1=ta[:], op=ALU.mult), reads=[TAB], writes=[TBB], n=512)
                            S.emit("dve", lambda: nc.vector.tensor_reduce(out=stt[:, 12:16], in_=tb[:], axis=AX.X, op=ALU.add), reads=[TBB], writes=[stb], n=512)
                            S.emit("pool", lambda: nc.gpsimd.tensor_scalar(out=stt[:, 16:20], in0=stt[:, 12:16], scalar1=1.0 / 128, scalar2=EPS, op0=ALU.mult, op1=ALU.add),
                                   reads=[stb], writes=[stb])
                            S.emit("pool", lambda: nc.gpsimd.tensor_tensor(out=stt[:, 20:24], in0=stt[:, 16:20], in1=nhalf[:, 0:4], op=ALU.pow), reads=[stb], writes=[stb], cost=1150.0)
                            S.emit("dve", lambda: nc.vector.tensor_tensor(out=tb[:], in0=ta[:], in1=stt[:, 20:24].unsqueeze(2).broadcast_to([128, 4, 128]), op=ALU.mult),
                                   reads=[TAB, stb], writes=[TBB], n=512)
                            S.emit("dve", lambda: nc.vector.scalar_tensor_tensor(out=onb[:], in0=tb[:], scalar=1.0 - LAM_INIT,
                                                                                 in1=rep128_t[:].unsqueeze(1).broadcast_to([128, 4, 128]), op0=ALU.mult, op1=ALU.mult),
                                   reads=[TBB], writes=[ONB], n=512)
                            def fin_b():
                                ot, otb = oT_ring.get()
                                transposes(lambda qs: onb[:, qs, :], [ONB], 4, 7, ot[:].rearrange("p (k t) -> p k t", k=4), [otb])
                                S.dma("sp", mixT_d[4 + h, :, col0 + qb * 512: col0 + (qb + 1) * 512], ot[:], reads=[otb])
                            deferred.append(fin_b)

                        n = len(its)
                        deferred = []
                        qk(0)
                        if n > 1:
                            qk(1)
                        for i in range(n):
                            ex(i)
                            if i + 2 < n:
                                qk(i + 2)
                            if deferred and its[i][2] == min(6, KT - 2):
                                deferred.pop(0)()
                            av(i)
                        while deferred:
                            deferred.pop(0)()
                        if pi + 1 < len(passes):
                            j2, h2, _, _, _ = passes[pi + 1]
                            qt2, qtb2 = qb_ring.get()
                            S.dma("sp", qt2[:, 0:len(h2), :], qT_d[j2].rearrange("h p t -> p h t")[:, h2[0]:h2[0] + len(h2), 0:512], writes=[qtb2])
                            q0_pre[pi + 1] = (qt2, qtb2)
                        if pi + 2 < len(passes):
                            load_kv(pi + 2)
                    S.barrier()

            S.flush()
            S.reorder = True
            if 3 in phases:
                with ExitStack() as p3:
                    w_out = sb(p3, "w_out_t", [128, 8, 1024], BF16)
                    wmq = sb(p3, "wmq_t", [128, 8, 512], BF16)
                    wmo = sb(p3, "wmo_t", [128, 4, 1024], BF16)
                    with ExitStack() as p3s:
                        stg_ring = Ring(p3s, nc, "stg3_", [128, 1024], F32, 3)
                        load_weight(w_out, w_out_d, 8, 1024, None, stg_ring)
                        load_weight(wmq, wm_q_d, 8, 512, 1, stg_ring)
                        load_weight(wmo, wm_o_d, 4, 1024, None, stg_ring)
                        S.barrier()
                    x_ring = Ring(p3, nc, "x3_", [128, D], F32, 14)
                    mix_ring = Ring(p3, nc, "mix3_", [128, 8, 256], BF16, 2)
                    hb_ring = Ring(p3, nc, "hb3_", [128, D], BF16, 4)
                    hT_ring = Ring(p3, nc, "hT3_", [128, 8, 256], BF16, 3, nb=2)
                    qs_ring = Ring(p3, nc, "qs3_", [128, 512], F32, 4)
                    sq_ring = Ring(p3, nc, "sq3_", [128, 512], F32, 3)
                    g8_ring = Ring(p3, nc, "g83_", [128, 24], F32, 8)
                    qb_ring = Ring(p3, nc, "qb3_", [128, 512], BF16, 4)
                    qmT_ring = Ring(p3, nc, "qmT3_", [128, 4, 256], BF16, 3)
                    pT_ring = Ring(p3, nc, "pT3_", [128, 2, 256], BF16, 6)
                    osb_ring = Ring(p3, nc, "osb3_", [128, 8, 129], F32, 2)
                    om_ring = Ring(p3, nc, "om3_", [128, 8, 128], BF16, 2)
                    omT_ring = Ring(p3, nc, "omT3_", [128, 4, 256], BF16, 3)
                    mscale = 128 ** -0.5
                    nsb = NT // 256

                    def sb_gen(sbi):
                        g0 = sbi * 256
                        job = 0 if g0 < TA else 1
                        mx, mxb = mix_ring.get()
                        S.dma("sp", mx[:], mixT_p[:, :, g0:g0 + 256], writes=[mxb])
                        xs = []
                        for t in range(2):
                            xt, xbuf = x_ring.get()
                            r0 = g0 + t * 128
                            src = xa[r0:r0 + 128, :] if r0 < TA else xbq[r0 - TA:r0 - TA + 128, :]
                            S.dma("sp", xt[:], src, writes=[xbuf], nbytes=524288)
                            xs.append((xt, xbuf))
                        yield
                        hbs = []
                        for t in range(2):
                            xt, xbuf = xs[t]
                            b0 = 2 * t
                            for half in range(2):
                                S.emit("pe", mm_group(ps[:, (b0 + half) * 512:(b0 + half + 1) * 512],
                                                      [(mx[:, k, t * 128:(t + 1) * 128], w_out[:, k, half * 512:(half + 1) * 512]) for k in range(8)]),
                                       reads=[mxb], writes=[PB[b0 + half]])
                            S.emit("dve", lambda: nc.vector.tensor_tensor(out=xt[:], in0=xt[:], in1=ps[:, b0 * 512:(b0 + 2) * 512], op=ALU.add),
                                   reads=[xbuf, PB[b0], PB[b0 + 1]], writes=[xbuf], n=1024)
                            hb, hbuf = hb_ring.get()
                            rms(xt, xbuf, hb, hbuf)
                            hbs.append((hb, hbuf))
                        yield
                        hT, hTbs = hT_ring.get()
                        for t in range(2):
                            hb, hbuf = hbs[t]
                            transposes(lambda k: hb[:, k * 128:(k + 1) * 128], [hbuf], 8, 4, hT[:, :, t * 128:(t + 1) * 128], [hTbs[t]],
                                       evac=("dve" if t == 0 else "act"))
                        ctxs = []
                        for t in range(2):
                            bank = 5 + t
                            S.emit("pe", mm_group(ps[:, bank * 512:(bank + 1) * 512], [(hT[:, k, t * 128:(t + 1) * 128], wmq[:, k, :]) for k in range(8)]),
                                   reads=[hTbs[t]], writes=[PB[bank]])
                            ctxs.append(gn1(bank, 4, qs_ring, sq_ring, g8_ring))
                        yield
                        qmT, qmTb = qmT_ring.get()
                        for t in range(2):
                            qn, qnb = gn2(ctxs[t], 4, rep512_t[:, 2, :])
                            qb, qbb = qb_ring.get()
                            S.emit("act", lambda: nc.scalar.copy(out=qb[:], in_=qn[:]), reads=[qnb], writes=[qbb], n=512)
                            transposes(lambda h: qb[:, h * 128:(h + 1) * 128], [qbb], 4, 7, qmT[:, :, t * 128:(t + 1) * 128], [qmTb], evac="act")
                        yield
                        started = set()
                        pts = {}

                        def scores(h):
                            sbank = 5 + h % 2
                            fns = []
                            for half in range(2):
                                fns.append(lambda half=half: nc.tensor.matmul(ps[:, sbank * 512 + half * 256: sbank * 512 + (half + 1) * 256],
                                                                              lhsT=kmT[job][:, h, half * 128:(half + 1) * 128], rhs=qmT[:, h, :],
                                                                              start=(half == 0), stop=True, skip_group_check=True))
                            S.emit("pe", fns, reads=[qmTb], writes=[PB[sbank]], cost=300.0)
                            pt, ptb = pT_ring.get()
                            S.emit("act", lambda: nc.scalar.activation(out=pt[:].rearrange("p a b -> p (a b)"), in_=ps[:, sbank * 512:(sbank + 1) * 512],
                                                                       func=AF.Exp, scale=mscale), reads=[PB[sbank]], writes=[ptb], n=512)
                            pts[h] = (pt, ptb)

                        def avm(h):
                            pt, ptb = pts.pop(h)
                            fns = []
                            wb = set()
                            for t in range(2):
                                a = t * 4 + h
                                bnk = a // 3
                                col = bnk * 512 + (a % 3) * 129
                                wb.add(bnk)
                                for half in range(2):
                                    st_ = bnk not in started
                                    started.add(bnk)
                                    fns.append(lambda t=t, half=half, col=col, st_=st_: nc.tensor.matmul(
                                        ps[:, col:col + 129], lhsT=pt[:, half, t * 128:(t + 1) * 128], rhs=vm[job][:, half, h, 0:129],
                                        start=st_, stop=(half == 1), skip_group_check=True))
                            S.emit("pe", fns, reads=[ptb], writes=[PB[b] for b in sorted(wb)], cost=350.0)

                        scores(0)
                        for h in range(4):
                            if h + 1 < 4:
                                scores(h + 1)
                            avm(h)
                        osb, OSB = osb_ring.get()
                        for bnk, a0, na in ((0, 0, 3), (1, 3, 3), (2, 6, 2)):
                            S.emit("dve", lambda: nc.vector.tensor_copy(out=osb[:, a0:a0 + na, :],
                                                                        in_=ps[:, bnk * 512: bnk * 512 + na * 129].rearrange("p (a c) -> p a c", c=129)),
                                   reads=[PB[bnk]], writes=[OSB], n=387)
                        yield
                        stt, stb = g8_ring.get()
                        S.emit("dve", lambda: nc.vector.reciprocal(out=stt[:, 0:8], in_=osb[:, :, 128:129].rearrange("p a c -> p (a c)")), reads=[OSB], writes=[stb])
                        om, OMB = om_ring.get()
                        S.emit("dve", lambda: nc.vector.tensor_tensor(out=om[:], in0=osb[:, :, 0:128], in1=stt[:, 0:8].unsqueeze(2).broadcast_to([128, 8, 128]), op=ALU.mult),
                               reads=[OSB, stb], writes=[OMB], n=1024)
                        omT, omTb = omT_ring.get()
                        for t in range(2):
                            transposes(lambda h: om[:, t * 4 + h, :], [OMB], 4, 7, omT[:, :, t * 128:(t + 1) * 128], [omTb], evac="act")
                        yield
                        for t in range(2):
                            xt, xbuf = xs[t]
                            b0 = 3 if t == 0 else 5
                            for half in range(2):
                                S.emit("pe", mm_group(ps[:, (b0 + half) * 512:(b0 + half + 1) * 512],
                                                      [(omT[:, k, t * 128:(t + 1) * 128], wmo[:, k, half * 512:(half + 1) * 512]) for k in range(4)]),
                                       reads=[omTb], writes=[PB[b0 + half]])
                            S.emit("dve", lambda: nc.vector.tensor_tensor(out=xt[:], in0=xt[:], in1=ps[:, b0 * 512:(b0 + 2) * 512], op=ALU.add),
                                   reads=[xbuf, PB[b0], PB[b0 + 1]], writes=[xbuf], n=1024)
                            S.dma("sp", x2s_d[g0 + t * 128: g0 + (t + 1) * 128, :], xt[:], reads=[xbuf], nbytes=524288)

                    run_pipeline((sb_gen(i) for i in range(nsb)), 7)
                    S.barrier()

            if 4 in phases:
                with ExitStack() as p4:
                    S.flush()
                    S.reorder = True
                    ff1 = sb(p4, "ff1_t", [128, 8, 4096], BF16)
                    ff2 = sb(p4, "ff2_t", [128, 32, 1024], BF16)
                    stg_ring = Ring(p4, nc, "stg4_", [128, 1024], F32, 2)
                    W1B = {(k, g): Buf() for k in range(8) for g in range(4)}
                    W2B = {(f, 0): Buf() for f in range(32)}
                    x_ring = Ring(p4, nc, "x4_", [128, D], F32, 6)
                    hb_ring = Ring(p4, nc, "hb4_", [128, D], BF16, 4)
                    hT_ring = Ring(p4, nc, "hT4_", [128, 8, 256], BF16, 2)
                    rl_ring = Ring(p4, nc, "rl4_", [128, 256], F32, 2)
                    aT_ring = Ring(p4, nc, "aT4_", [128, 256], BF16, 3)
                    nsb = NT // 256
                    pend = {}
                    src_d = x2s_d if 3 in phases else None

                    def load_sb4(sbi):
                        xs = []
                        for t in range(2):
                            xt, xbuf = x_ring.get()
                            r0 = sbi * 256 + t * 128
                            S.dma("sp", xt[:], src_d[r0:r0 + 128, :], writes=[xbuf], nbytes=524288)
                            xs.append((xt, xbuf))
                        pend[sbi] = xs

                    hTs = {}
                    prea = {}

                    def pre_a(sbi):
                        xs = pend[sbi]
                        hT, hTb = hT_ring.get()
                        hTs[sbi] = (hT, hTb)
                        prea[sbi] = []
                        for t in range(2):
                            xt, xbuf = xs[t]
                            hb, hbuf = hb_ring.get()
                            prea[sbi].append((hb, hbuf, rms_a(xt, xbuf, hb, hbuf)))

                    def pre_b(sbi):
                        xs = pend[sbi]
                        for t in range(2):
                            xt, xbuf = xs[t]
                            hb, hbuf, ctx = prea[sbi][t]
                            rms_b(xt, xbuf, hb, hbuf, ctx)

                    def pre_c(sbi):
                        hT, hTb = hTs[sbi]
                        for t in range(2):
                            hb, hbuf, ctx = prea[sbi][t]
                            transposes(lambda k: hb[:, k * 128:(k + 1) * 128], [hbuf], 8, 0, hT[:, :, t * 128:(t + 1) * 128], [hTb],
                                       evac=("dve" if t == 0 else "act"))
                        del prea[sbi]

                    load_sb4(0)
                    if nsb > 1:
                        load_sb4(1)
                    pre_a(0)
                    pre_b(0)
                    pre_c(0)
                    for g in range(4):
                        load_weight(ff1, w_ff1_d, 8, 4096, 3, stg_ring, wbufs=W1B, c0s=[g * 1024])
                        load_weight(ff2, w_ff2_d, 32, 1024, None, stg_ring, wbufs=W2B, ks=range(g * 8, g * 8 + 8))
                    for sbi in range(nsb):
                        if sbi + 2 < nsb:
                            load_sb4(sbi + 2)
                        xs = pend.pop(sbi)
                        hT, hTb = hTs.pop(sbi)
                        st4 = {}

                        def f1(f):
                            bank = 1 + f % 2
                            S.emit("pe", mm_group(ps[:, bank * 512: bank * 512 + 256], [(ff1[:, k, f * 128:(f + 1) * 128], hT[:, k, :]) for k in range(8)]),
                                   reads=[hTb] + [W1B[(k, f // 8)] for k in range(8)], writes=[PB[bank]])
                            rl, rlb = rl_ring.get()
                            S.emit("act", lambda: nc.scalar.activation(out=rl[:], in_=ps[:, bank * 512: bank * 512 + 256], func=AF.Relu), reads=[PB[bank]], writes=[rlb], n=256)
                            at, atb = aT_ring.get()
                            if f % 2 == 0:
                                S.emit("dve", lambda: nc.vector.tensor_tensor(out=at[:], in0=rl[:], in1=rl[:], op=ALU.mult), reads=[rlb], writes=[atb])
                            else:
                                S.emit("pool", lambda: nc.gpsimd.tensor_tensor(out=at[:], in0=rl[:], in1=rl[:], op=ALU.mult), reads=[rlb], writes=[atb])
                            st4[f] = (at, atb)

                        def f2(f):
                            at, atb = st4.pop(f)
                            fns = []
                            for t in range(2):
                                for half in range(2):
                                    bank = 3 + t * 2 + half
                                    fns.append(lambda t=t, half=half, bank=bank: nc.tensor.matmul(
                                        ps[:, bank * 512:(bank + 1) * 512], lhsT=at[:, t * 128:(t + 1) * 128], rhs=ff2[:, f, half * 512:(half + 1) * 512],
                                        start=(f == 0), stop=(f == 31)))
                            S.emit("pe", fns, reads=[atb, W2B[(f, 0)]], writes=[PB[3], PB[4], PB[5], PB[6]], cost=990.0)

                        f1(0)
                        for f in range(32):
                            if f + 1 < 32:
                                f1(f + 1)
                            f2(f)
                            if sbi + 1 < nsb:
                                if f == 6:
                                    pre_a(sbi + 1)
                                elif f == 12:
                                    pre_b(sbi + 1)
                                elif f == 18:
                                    pre_c(sbi + 1)
                        for t in range(2):
                            xt, xbuf = xs[t]
                            b0 = 3 + t * 2
                            S.emit("dve", lambda: nc.vector.tensor_tensor(out=xt[:], in0=xt[:], in1=ps[:, b0 * 512:(b0 + 2) * 512], op=ALU.add),
                                   reads=[xbuf, PB[b0], PB[b0 + 1]], writes=[xbuf], n=1024)
                            r0 = sbi * 256 + t * 128
                            S.dma("sp", out_d[r0:r0 + 128, :], xt[:], reads=[xbuf], nbytes=524288)
                    S.barrier()
            S.barrier()
    return nc


def _rope_table(pos):
    inv_freq = (np.float32(500000.0) ** (-np.arange(0, 16, 2, dtype=np.float32) / np.float32(16))).astype(np.float32)
    ang = (pos.astype(np.float32)[:, None] * inv_freq[None, :]).astype(np.float32)
    return np.concatenate([np.cos(ang), np.sin(ang), -np.sin(ang)], axis=1).astype(np.float32)


def _pt(tab):
    T = tab.shape[0]
    return np.ascontiguousarray(tab.reshape(T // 128, 128, 24).transpose(1, 0, 2).reshape(128, (T // 128) * 24))


def _col(v):
    return np.ascontiguousarray(np.asarray(v, np.float32).reshape(-1, 128).T)


def make_in_maps(inp, n_cores, TA, TBK, TBQ):
    f = lambda a: np.ascontiguousarray(np.asarray(a, dtype=np.float32))
    xp, xs = f(inp["x_prompt"]), f(inp["x_sample"])
    mp, ms = f(inp["mem_prompt"]), f(inp["mem_sample"])
    shared = {
        "w_in": f(inp["w_in"][0]), "w_out": f(inp["w_out"][0]), "wm_q": f(inp["wm_q"][0]), "wm_kv": f(inp["wm_kv"][0]),
        "wm_o": f(inp["wm_o"][0]), "w_ff1": f(inp["w_ff1"][0]), "w_ff2": f(inp["w_ff2"][0]),
        "gcol": np.ascontiguousarray(np.concatenate([_col(inp["g_mix"][0]), _col(inp["g_memq"][0]), _col(inp["g_memkv"][0]), _col(inp["g_mlp"][0])], axis=1)),
        "rep128": np.ascontiguousarray(np.broadcast_to(f(inp["g_subln"][0])[None, :], (128, 128))),
        "ident": np.eye(128, dtype=np.float32).astype(ml_dtypes.bfloat16),
        "cs_a": _pt(_rope_table(np.arange(TA))), "cs_bk": _pt(_rope_table(np.arange(TBK))),
    }
    cw, cb = f(inp["conv_w"][0]), f(inp["conv_b"][0])
    cwb = np.zeros((128, 4, 4), np.float32)
    for j in range(4):
        cwb[:, j, 0:3] = cw[:, j * 128:(j + 1) * 128].T
        cwb[:, j, 3] = cb[j * 128:(j + 1) * 128]
    shared["cwb"] = cwb.reshape(128, 16)
    rep = np.stack([np.tile(f(inp["q_norm"][0]), 8), np.tile(f(inp["k_norm"][0]), 8),
                    np.tile(f(inp["q_norm_mem"][0]), 4), np.tile(f(inp["k_norm_mem"][0]), 4)], axis=0)
    shared["rep512"] = np.ascontiguousarray(np.broadcast_to(rep.reshape(1, 2048), (128, 2048)))
    lv = np.stack([f(inp["lambda_q1"][0]), f(inp["lambda_k1"][0]), f(inp["lambda_q2"][0]), f(inp["lambda_k2"][0])], axis=0)
    shared["lamv"] = np.ascontiguousarray(np.broadcast_to(lv.reshape(1, 256), (128, 256)))
    nq = TBK // TBQ
    maps = []
    for c in range(n_cores):
        sb_, qi = c // nq, c % nq
        q0 = qi * TBQ
        m = dict(shared)
        m["xa"] = xp[c]
        m["xbk"] = xs[sb_]
        m["xbq"] = np.ascontiguousarray(xs[sb_][q0:q0 + TBQ])
        hl = np.zeros((2, D), np.float32)
        if q0 > 0:
            hl[0] = xs[sb_][q0 - 1]
        if q0 + TBQ < TBK:
            hl[1] = xs[sb_][q0 + TBQ]
        m["halo"] = hl
        m["mema"] = mp[c]
        m["memb"] = ms[sb_]
        m["cs_bq"] = _pt(_rope_table(np.arange(q0, q0 + TBQ)))
        maps.append(m)
    return maps


_NC_CACHE = {}


def kernel(**inputs):
    TA, TBK, TBQ = 8192, 16384, 4096
    key = (TA, TBK, TBQ)
    if key not in _NC_CACHE:
        _NC_CACHE[key] = build_nc(TA, TBK, TBQ)
    nc = _NC_CACHE[key]
    in_maps = make_in_maps(inputs, 8, TA, TBK, TBQ)
    res = run_bass_kernel_spmd(nc, in_maps, core_ids=list(range(8)))
    outs = [np.asarray(r["out"], dtype=np.float32) for r in res.results]
    y_prompt = np.stack([o[:TA] for o in outs], axis=0)
    nq = TBK // TBQ
    y_sample = np.stack([np.concatenate([outs[b * nq + q][TA:] for q in range(nq)], axis=0) for b in range(8 // nq)], axis=0)
    return (y_prompt, y_sample)
```

```python
import math
import os
import heapq
import types
from contextlib import ExitStack

import numpy as np
import ml_dtypes
import concourse.bass as bass
import concourse.mybir as mybir
from concourse.alu_op_type import AluOpType as ALU
from concourse.bass_utils import run_bass_kernel_spmd

F32 = mybir.dt.float32
BF16 = mybir.dt.bfloat16
AF = mybir.ActivationFunctionType
AX = mybir.AxisListType

D = 1024
EPS = 1e-6
LAM_INIT = 0.8 - 0.6 * math.exp(-0.3 * 0)
MEMT = 256
KVCAP = 32768


class Buf:
    __slots__ = ("w", "r")

    def __init__(self):
        self.w = None
        self.r = []


class Lane:
    def __init__(self, idx, sem):
        self.idx = idx
        self.sem = sem
        self.count = 0


class Node:
    __slots__ = ("q", "fns", "deps", "cost", "lat", "prio", "succ", "nd", "ready", "start", "finish", "token", "dma", "noncontig")

    def __init__(self, q, fns, deps, cost, lat, prio, dma=None, noncontig=False):
        self.q, self.fns, self.deps, self.cost, self.lat, self.prio = q, fns, deps, cost, lat, prio
        self.succ = []
        self.nd = 0
        self.ready = 0.0
        self.start = self.finish = 0.0
        self.token = None
        self.dma = dma
        self.noncontig = noncontig


def _freeze(fn, depth=0):
    if not isinstance(fn, types.FunctionType) or fn.__closure__ is None or depth > 2:
        return fn
    cells = []
    for c in fn.__closure__:
        try:
            v = c.cell_contents
        except ValueError:
            cells.append(c)
            continue
        if isinstance(v, types.FunctionType) and v.__closure__ is not None:
            v = _freeze(v, depth + 1)
        cells.append(types.CellType(v))
    g = types.FunctionType(fn.__code__, fn.__globals__, fn.__name__, fn.__defaults__, tuple(cells))
    g.__kwdefaults__ = fn.__kwdefaults__
    return g


_FIX = {"pe": 30.0, "act": 240.0, "dve": 280.0, "pool": 300.0, "sp": 100.0}
_PER = {"act": 0.833, "dve": 1.05, "pool": 2.1}


class Sched:
    def __init__(self, nc, st, nlanes=12):
        self.nc = nc
        self.eng = {"pe": nc.tensor, "act": nc.scalar, "dve": nc.vector, "pool": nc.gpsimd, "sp": nc.sync}
        self.sem = {k: st.enter_context(nc.semaphore("s_" + k)) for k in self.eng}
        self.count = {k: 0 for k in self.eng}
        self.waited = {k: {} for k in self.eng}
        self.lanes = [Lane(i, st.enter_context(nc.semaphore("s_l%d" % i))) for i in range(nlanes)]
        self.next_lane = 0
        self.batch = []
        self.reorder = True
        self.slack = 500.0

    def _collect(self, reads, writes):
        deps = []
        for b in reads:
            if b.w is not None:
                deps.append(b.w)
        for b in writes:
            if b.w is not None:
                deps.append(b.w)
            deps.extend(b.r)
        return deps

    def _mark(self, node, reads, writes):
        for b in reads:
            b.r.append(node)
        for b in writes:
            b.w = node
            b.r = []

    def emit(self, q, fns, reads=(), writes=(), n=256, cost=None):
        if callable(fns):
            fns = [fns]
        if cost is None and getattr(fns, "cost", None):
            cost = fns.cost
        fns = [_freeze(f) for f in fns]
        if cost is None:
            if q == "pe":
                cost = _FIX["pe"] + 180.0 * len(fns)
            else:
                cost = _FIX[q] + _PER[q] * n
        node = Node(q, fns, self._collect(reads, writes), cost, cost, len(self.batch))
        self._mark(node, reads, writes)
        self.batch.append(node)
        return node

    def dma(self, q, out, in_, reads=(), writes=(), nbytes=262144, noncontig=False):
        lat = 2000.0 + nbytes / 120.0
        node = Node(q, None, self._collect(reads, writes), _FIX["sp"], lat, len(self.batch), dma=(out, in_), noncontig=noncontig)
        self._mark(node, reads, writes)
        self.batch.append(node)
        return node

    def _wait(self, q, tok):
        kind, name, val = tok
        if kind == "q" and name == q and q == "pe":
            return
        key = (kind, name)
        if self.waited[q].get(key, 0) >= val:
            return
        self.waited[q][key] = val
        sem = self.sem[name] if kind == "q" else self.lanes[name].sem
        self.eng[q].wait_ge(sem, val)

    def _emit_node(self, node):
        q = node.q
        need = {}
        for d in node.deps:
            kind, name, val = d.token
            if need.get((kind, name), 0) < val:
                need[(kind, name)] = val
        for (kind, name), val in need.items():
            self._wait(q, (kind, name, val))
        if node.dma is not None:
            lane = self.lanes[self.next_lane]
            self.next_lane = (self.next_lane + 1) % len(self.lanes)
            if lane.count:
                self._wait(q, ("l", lane.idx, lane.count))
            out, in_ = node.dma
            if node.noncontig:
                with self.nc.allow_non_contiguous_dma(reason="single-column scatter"):
                    ins = self.eng[q].dma_start(out=out, in_=in_)
            else:
                ins = self.eng[q].dma_start(out=out, in_=in_)
            lane.count += 16
            ins.then_inc(lane.sem, 16)
            node.token = ("l", lane.idx, lane.count)
        else:
            ins = None
            for f in node.fns:
                ins = f()
            self.count[q] += 1
            ins.then_inc(self.sem[q], 1)
            node.token = ("q", q, self.count[q])
        node.fns = None
        node.dma = None

    def flush(self):
        batch = self.batch
        self.batch = []
        if not batch:
            return
        if not self.reorder:
            for nd in batch:
                nd.deps = [d for d in nd.deps]
                self._emit_node(nd)
            return
        SYNC = self.slack
        for nd in batch:
            live = []
            seen = set()
            for d in nd.deps:
                if id(d) in seen:
                    continue
                seen.add(id(d))
                live.append(d)
                if d.token is None:
                    d.succ.append(nd)
                    nd.nd += 1
            nd.deps = live
            nd.ready = 0.0
        fut = {q: [] for q in self.eng}
        avail = {q: [] for q in self.eng}
        free = {q: 0.0 for q in self.eng}
        for nd in batch:
            if nd.nd == 0:
                heapq.heappush(fut[nd.q], (0.0, nd.prio, nd))
        order = []
        left = len(batch)
        while left:
            best_q, best_t = None, None
            for q in self.eng:
                f, a = fut[q], avail[q]
                while f and f[0][0] <= free[q]:
                    r, p, x = heapq.heappop(f)
                    heapq.heappush(a, (p, x))
                if a:
                    t = free[q]
                elif f:
                    t = f[0][0]
                else:
                    continue
                if best_t is None or t < best_t:
                    best_q, best_t = q, t
            q = best_q
            if avail[q]:
                p, x = heapq.heappop(avail[q])
            else:
                r, p, x = heapq.heappop(fut[q])
            x.start = max(free[q], x.ready)
            free[q] = x.start + x.cost
            x.finish = x.start + x.lat
            order.append(x)
            left -= 1
            for s_ in x.succ:
                s_.nd -= 1
                rt = x.finish + (0.0 if (s_.q == x.q) else SYNC)
                if rt > s_.ready:
                    s_.ready = rt
                if s_.nd == 0:
                    heapq.heappush(fut[s_.q], (s_.ready, s_.prio, s_))
            x.succ = None
        if os.environ.get("SCHED_DEBUG") and len(order) > 500:
            fr = {q: 0.0 for q in self.eng}
            fin = {}
            for nd in sorted(order, key=lambda x: x.prio):
                st_ = fr[nd.q]
                for d in nd.deps:
                    if id(d) in fin:
                        st_ = max(st_, fin[id(d)] + (0.0 if d.q == nd.q else 100.0))
                fr[nd.q] = st_ + nd.cost
                fin[id(nd)] = st_ + nd.lat
            print("   program-order makespan_us=%.1f" % (max(fin.values()) / 1e3))
            busy = {}
            for nd in order:
                busy[nd.q] = busy.get(nd.q, 0.0) + nd.cost
            print("SCHED batch n=%d makespan_us=%.1f busy_us=%s" % (len(order), max(x.finish for x in order) / 1e3,
                                                                   {k: round(v / 1e3, 1) for k, v in busy.items()}), flush=True)
        for nd in order:
            self._emit_node(nd)

    def barrier(self):
        self.flush()
        for q in self.eng:
            for p in self.eng:
                if p != q and self.count[p]:
                    self._wait(q, ("q", p, self.count[p]))
            for lane in self.lanes:
                if lane.count:
                    self._wait(q, ("l", lane.idx, lane.count))


def _mm_cost(N):
    return 240.0 if N >= 512 else (126.0 if N >= 256 else 75.0)


class FnList(list):
    cost = None


class Ring:
    def __init__(self, st, nc, name, shape, dt, n, nb=1):
        self.items = [(st.enter_context(nc.sbuf_tensor("%s%d" % (name, i), list(shape), dt)),
                       Buf() if nb == 1 else [Buf() for _ in range(nb)]) for i in range(n)]
        self.i = 0

    def get(self):
        it = self.items[self.i % len(self.items)]
        self.i += 1
        return it


def run_pipeline(gens, depth):
    active = []
    gens = iter(gens)
    more = True
    while True:
        if more and len(active) < depth:
            try:
                active.append(next(gens))
            except StopIteration:
                more = False
        if not active:
            break
        for g in list(active):
            try:
                next(g)
            except StopIteration:
                active.remove(g)


def build_nc(TA, TBK, TBQ, phases=(0, 1, 2, 3, 4)):
    NT = TA + TBQ
    nc = bass.Bass("TRN2", target_bir_lowering=False)

    def din(name, shape, dt=F32):
        return nc.dram_tensor(name, list(shape), dt, kind="ExternalInput").ap()

    def dscr(name, shape, dt):
        return nc.dram_tensor(name, list(shape), dt, kind="Internal").ap()

    xa = din("xa", [TA, D])
    xbk = din("xbk", [TBK, D])
    xbq = din("xbq", [TBQ, D])
    halo = din("halo", [2, D])
    mem_d = [din("mema", [MEMT, D]), din("memb", [MEMT, D])]
    cs_d = {"a": din("cs_a", [128, (TA // 128) * 24]), "bk": din("cs_bk", [128, (TBK // 128) * 24]),
            "bq": din("cs_bq", [128, (TBQ // 128) * 24])}
    w_in_d = din("w_in", [D, 3072])
    w_out_d = din("w_out", [D, D])
    wm_q_d = din("wm_q", [D, 512])
    wm_kv_d = din("wm_kv", [D, D])
    wm_o_d = din("wm_o", [512, D])
    w_ff1_d = din("w_ff1", [D, 4096])
    w_ff2_d = din("w_ff2", [4096, D])
    gcol_d = din("gcol", [128, 32])
    cwb_d = din("cwb", [128, 16])
    rep512_d = din("rep512", [128, 4 * 512])
    rep128_d = din("rep128", [128, 128])
    lamv_d = din("lamv", [128, 4 * 64])
    ident_d = din("ident", [128, 128], BF16)
    out_d = nc.dram_tensor("out", [NT, D], F32, kind="ExternalOutput").ap()

    KTA, KTB = TA // 128, TBK // 128
    qT_d = [dscr("qT_a", [4, 128, TA], BF16), dscr("qT_b", [4, 128, TBQ], BF16)]
    kT_d = [dscr("kT_a", [4, 128, TA], BF16), dscr("kT_b", [4, 128, TBK], BF16)]
    v_d = [dscr("v_a", [4, 128, KTA, 130], BF16), dscr("v_b", [4, 128, KTB, 130], BF16)]
    mixT_d = dscr("mixT", [8, 128, NT], BF16)
    x2s_d = dscr("x2s", [NT, D], F32)
    mixT_p = mixT_d.rearrange("c p t -> p c t")

    with ExitStack() as top:
        S = Sched(nc, top)
        ps = top.enter_context(nc.psum_tensor("ps", [128, 4096], F32))
        psb = ps.bitcast(BF16)
        PB = [Buf() for _ in range(8)]

        def sb(st, name, shape, dt):
            return st.enter_context(nc.sbuf_tensor(name, list(shape), dt))

        ident_t = sb(top, "ident_t", [128, 128], BF16)
        gcol_t = sb(top, "gcol_t", [128, 32], F32)
        cwb_t = sb(top, "cwb_t", [128, 16], F32)
        rep512_t = sb(top, "rep512_t", [128, 4, 512], F32)
        rep128_t = sb(top, "rep128_t", [128, 128], F32)
        lamv_t = sb(top, "lamv_t", [128, 4, 64], F32)
        nhalf = sb(top, "nhalf", [128, 8], F32)
        lam_t = sb(top, "lam_t", [128, 16], F32)
        kmT = [sb(top, "kmT%d" % j, [128, 4, MEMT], BF16) for j in range(2)]
        vm = [sb(top, "vm%d" % j, [128, 2, 4, 130], BF16) for j in range(2)]
        CONST = Buf()
        CONST_ = CONST
        st_ring = Ring(top, nc, "st", [128, 4], F32, 6)
        st8_ring = Ring(top, nc, "st8", [128, 24], F32, 4)

        block = top.enter_context(nc.Block())

        def rms(xt, xbuf, hb, hbuf):
            rms_b(xt, xbuf, hb, hbuf, rms_a(xt, xbuf, hb, hbuf))

        def rms_b(xt, xbuf, hb, hbuf, ctx):
            stt, stb = ctx
            S.emit("act", lambda: nc.scalar.activation(out=hb[:], in_=xt[:], func=AF.Copy, scale=stt[:, 2:3]),
                   reads=[xbuf, stb], writes=[hbuf], n=1136)

        def rms_a(xt, xbuf, hb, hbuf):
            stt, stb = st_ring.get()
            S.emit("act", lambda: nc.scalar.activation(out=hb[:], in_=xt[:], func=AF.Square, accum_out=stt[:, 0:1]),
                   reads=[xbuf], writes=[hbuf, stb], n=1136)
            S.emit("pool", lambda: nc.gpsimd.tensor_scalar(out=stt[:, 1:2], in0=stt[:, 0:1], scalar1=1.0 / D, scalar2=EPS,
                                                          op0=ALU.mult, op1=ALU.add), reads=[stb], writes=[stb], cost=230.0)
            S.emit("pool", lambda: nc.gpsimd.tensor_tensor(out=stt[:, 2:3], in0=stt[:, 1:2], in1=nhalf[:, 0:1], op=ALU.pow),
                   reads=[stb], writes=[stb], cost=600.0)
            return stt, stb

        def transposes(src_fn, sbufs, n, bank, dst_ap, dbufs, evac="dve"):
            srcs = [src_fn(k) for k in range(n)]
            S.emit("pe", [lambda k=k: nc.tensor.transpose(out=psb[:, bank * 1024 + k * 128: bank * 1024 + (k + 1) * 128],
                                                          in_=srcs[k], identity=ident_t[:]) for k in range(n)],
                   reads=sbufs, writes=[PB[bank]], cost=30.0 + 70.0 * n)
            src_ps = psb[:, bank * 1024: bank * 1024 + n * 128].rearrange("p (k t) -> p k t", k=n)
            if evac == "dve":
                S.emit("dve", lambda: nc.vector.tensor_copy(out=dst_ap, in_=src_ps), reads=[PB[bank]], writes=dbufs, n=128 * n)
            else:
                S.emit("act", lambda: nc.scalar.copy(out=dst_ap, in_=src_ps), reads=[PB[bank]], writes=dbufs, n=128 * n)

        def groupnorm(bank, G, gain_ap, rings):
            qsb_ring, sq_ring, qn_ring = rings
            gs = 512 // G
            qs_, qsb = qsb_ring.get()
            S.emit("act", lambda: nc.scalar.copy(out=qs_[:], in_=ps[:, bank * 512:(bank + 1) * 512]), reads=[PB[bank]], writes=[qsb], n=512)
            sq, sqb = sq_ring.get()
            S.emit("dve", lambda: nc.vector.tensor_tensor(out=sq[:], in0=qs_[:], in1=qs_[:], op=ALU.mult), reads=[qsb], writes=[sqb], n=512)
            stt, stb = st8_ring.get()
            S.emit("dve", lambda: nc.vector.tensor_reduce(out=stt[:, 0:G], in_=sq[:].rearrange("p (g d) -> p g d", g=G),
                                                          axis=AX.X, op=ALU.add), reads=[sqb], writes=[stb], n=512)
            S.emit("pool", lambda: nc.gpsimd.tensor_scalar(out=stt[:, 8:8 + G], in0=stt[:, 0:G], scalar1=1.0 / gs, scalar2=EPS,
                                                          op0=ALU.mult, op1=ALU.add), reads=[stb], writes=[stb], cost=230.0)
            S.emit("pool", lambda: nc.gpsimd.tensor_tensor(out=stt[:, 16:16 + G], in0=stt[:, 8:8 + G], in1=nhalf[:, 0:G], op=ALU.pow),
                   reads=[stb], writes=[stb], cost=400.0 + 170.0 * G)
            S.emit("dve", lambda: nc.vector.tensor_tensor(out=sq[:].rearrange("p (g d) -> p g d", g=G),
                                                          in0=qs_[:].rearrange("p (g d) -> p g d", g=G),
                                                          in1=stt[:, 16:16 + G].unsqueeze(2).broadcast_to([128, G, gs]), op=ALU.mult),
                   reads=[qsb, stb], writes=[sqb])
            qn, qnb = qn_ring.get()
            S.emit("dve", lambda: nc.vector.tensor_tensor(out=qn[:], in0=sq[:], in1=gain_ap, op=ALU.mult), reads=[sqb], writes=[qnb])
            return qn, qnb

        def load_weight(dst, w_d, KC, cols, gidx, stg_ring, engs=("dve", "act"), wbufs=None, ks=None, c0s=None):
            i = 0
            for k in (range(KC) if ks is None else ks):
                for c0 in (range(0, cols, 1024) if c0s is None else c0s):
                    cw = min(1024, cols - c0)
                    CONST = Buf() if wbufs is None else wbufs[(k, c0 // 1024)]
                    stg, stgb = stg_ring.get()
                    S.dma("sp", stg[:, 0:cw], w_d[k * 128:(k + 1) * 128, c0:c0 + cw], writes=[stgb], nbytes=512 * cw)
                    e = engs[i % len(engs)]
                    i += 1
                    o = dst[:, k, c0:c0 + cw]
                    if gidx is None:
                        if e == "dve":
                            S.emit("dve", lambda: nc.vector.tensor_copy(out=o, in_=stg[:, 0:cw]), reads=[stgb], writes=[CONST], n=1024)
                        elif e == "pool":
                            S.emit("pool", lambda: nc.gpsimd.tensor_copy(out=o, in_=stg[:, 0:cw]), reads=[stgb], writes=[CONST], n=1024)
                        else:
                            S.emit("act", lambda: nc.scalar.copy(out=o, in_=stg[:, 0:cw]), reads=[stgb], writes=[CONST], n=1024)
                    else:
                        g = gcol_t[:, gidx * 8 + k: gidx * 8 + k + 1]
                        if e == "dve":
                            S.emit("dve", lambda: nc.vector.tensor_scalar(out=o, in0=stg[:, 0:cw], scalar1=g, scalar2=None, op0=ALU.mult),
                                   reads=[stgb], writes=[CONST], n=1024)
                        elif e == "pool":
                            S.emit("pool", lambda: nc.gpsimd.tensor_scalar(out=o, in0=stg[:, 0:cw], scalar1=g, scalar2=None, op0=ALU.mult),
                                   reads=[stgb], writes=[CONST], n=1024)
                        else:
                            S.emit("act", lambda: nc.scalar.activation(out=o, in_=stg[:, 0:cw], func=AF.Copy, scale=g),
                                   reads=[stgb], writes=[CONST], n=1024)

        def gn1(bank, G, qs_ring, sq_ring, g8_ring):
            gs = 512 // G
            qs_, qsb = qs_ring.get()
            S.emit("act", lambda: nc.scalar.copy(out=qs_[:], in_=ps[:, bank * 512:(bank + 1) * 512]), reads=[PB[bank]], writes=[qsb], n=512)
            sq, sqb = sq_ring.get()
            S.emit("act", lambda: nc.scalar.activation(out=sq[:], in_=ps[:, bank * 512:(bank + 1) * 512], func=AF.Square),
                   reads=[PB[bank]], writes=[sqb], n=512)
            stt, stb = g8_ring.get()
            S.emit("dve", lambda: nc.vector.tensor_reduce(out=stt[:, 0:G], in_=sq[:].rearrange("p (g d) -> p g d", g=G),
                                                          axis=AX.X, op=ALU.add), reads=[sqb], writes=[stb], n=512)
            S.emit("pool", lambda: nc.gpsimd.tensor_scalar(out=stt[:, 8:8 + G], in0=stt[:, 0:G], scalar1=1.0 / gs, scalar2=EPS,
                                                          op0=ALU.mult, op1=ALU.add), reads=[stb], writes=[stb], cost=230.0)
            S.emit("pool", lambda: nc.gpsimd.tensor_tensor(out=stt[:, 16:16 + G], in0=stt[:, 8:8 + G], in1=nhalf[:, 0:G], op=ALU.pow),
                   reads=[stb], writes=[stb], cost=400.0 + 170.0 * G)
            return qs_, qsb, stt, stb

        def gn2(ctx, G, gain_ap):
            qs_, qsb, stt, stb = ctx
            gs = 512 // G
            q3 = qs_[:].rearrange("p (g d) -> p g d", g=G)
            S.emit("dve", lambda: nc.vector.tensor_tensor(out=q3, in0=q3, in1=stt[:, 16:16 + G].unsqueeze(2).broadcast_to([128, G, gs]), op=ALU.mult),
                   reads=[qsb, stb], writes=[qsb], n=512)
            S.emit("pool", lambda: nc.gpsimd.tensor_tensor(out=qs_[:], in0=qs_[:], in1=gain_ap, op=ALU.mult), reads=[qsb], writes=[qsb], n=512)
            return qs_, qsb

        def mm_group(out_ap, pairs, start=True, stop=True, skip=False):
            n = len(pairs)
            fns = FnList()
            fns.cost = 30.0 + n * _mm_cost(pairs[0][1].free_size())
            for i, (l, r) in enumerate(pairs):
                st_ = start and i == 0
                sp_ = stop and i == n - 1
                if skip:
                    fns.append(lambda l=l, r=r, st_=st_, sp_=sp_: nc.tensor.matmul(out_ap, lhsT=l, rhs=r, start=st_, stop=sp_, skip_group_check=True))
                else:
                    fns.append(lambda l=l, r=r, st_=st_, sp_=sp_: nc.tensor.matmul(out_ap, lhsT=l, rhs=r, start=st_, stop=sp_))
            return fns

        @block.sync
        def _(sync):
            pw = ExitStack()
            w_in = sb(pw, "w_in_t", [128, 8, 3072], BF16)
            with ExitStack() as p0:
                for t_, d_ in ((ident_t, ident_d), (gcol_t, gcol_d), (cwb_t, cwb_d), (rep128_t, rep128_d)):
                    S.dma("sp", t_[:], d_, writes=[CONST])
                S.dma("sp", rep512_t[:].rearrange("p a b -> p (a b)"), rep512_d, writes=[CONST])
                S.dma("sp", lamv_t[:].rearrange("p a b -> p (a b)"), lamv_d, writes=[CONST])
                S.emit("pool", lambda: nc.gpsimd.memset(nhalf[:], -0.5), writes=[CONST])
                for j in range(2):
                    S.emit("pool", lambda: nc.gpsimd.memset(vm[j][:], 1.0), writes=[CONST])
                S.barrier()
                scr = sb(p0, "lscr", [128, 64], F32)
                LB = Buf()
                for i in range(2):
                    S.emit("dve", lambda: nc.vector.tensor_tensor(out=scr[:], in0=lamv_t[:, 2 * i, :], in1=lamv_t[:, 2 * i + 1, :], op=ALU.mult),
                           writes=[LB])
                    S.emit("dve", lambda: nc.vector.tensor_reduce(out=lam_t[:, i:i + 1], in_=scr[:], axis=AX.X, op=ALU.add), reads=[LB], writes=[LB])
                S.emit("act", lambda: nc.scalar.activation(out=lam_t[:, 2:4], in_=lam_t[:, 0:2], func=AF.Exp), reads=[LB], writes=[LB])
                S.emit("dve", lambda: nc.vector.tensor_tensor(out=lam_t[:, 4:5], in0=lam_t[:, 2:3], in1=lam_t[:, 3:4], op=ALU.subtract), reads=[LB], writes=[LB])
                S.emit("dve", lambda: nc.vector.tensor_scalar(out=lam_t[:, 5:6], in0=lam_t[:, 4:5], scalar1=LAM_INIT, scalar2=-1.0,
                                                             op0=ALU.add, op1=ALU.mult), reads=[LB], writes=[LB])
                nlam = lam_t[:, 5:6]

                stg_ring = Ring(p0, nc, "stg", [128, 1024], F32, 3)
                wkv = sb(p0, "wkv", [128, 8, 1024], BF16)
                load_weight(wkv, wm_kv_d, 8, 1024, 2, stg_ring)
                S.barrier()
                load_weight(w_in, w_in_d, 8, 3072, 0, stg_ring)
                x_ring = Ring(p0, nc, "x0_", [128, D], F32, 2)
                hb_ring = Ring(p0, nc, "hb0_", [128, D], BF16, 2)
                hT_ring = Ring(p0, nc, "hT0_", [128, 8, 128], BF16, 2)
                rings = (Ring(p0, nc, "qs0_", [128, 512], F32, 2), Ring(p0, nc, "sq0_", [128, 512], F32, 2), Ring(p0, nc, "qn0_", [128, 512], F32, 2))
                qb_ring = Ring(p0, nc, "qb0_", [128, 512], BF16, 2)
                for j in range(2):
                    for t in range(2):
                        xt, xbuf = x_ring.get()
                        S.dma("sp", xt[:], mem_d[j][t * 128:(t + 1) * 128, :], writes=[xbuf])
                        hb, hbuf = hb_ring.get()
                        rms(xt, xbuf, hb, hbuf)
                        hT, hTb = hT_ring.get()
                        transposes(lambda k: hb[:, k * 128:(k + 1) * 128], [hbuf], 8, 0, hT[:], [hTb])
                        S.emit("pe", mm_group(ps[:, 512:1024], [(hT[:, k, :], wkv[:, k, 0:512]) for k in range(8)]), reads=[hTb], writes=[PB[1]])
                        S.emit("pe", mm_group(ps[:, 1024:1536], [(hT[:, k, :], wkv[:, k, 512:1024]) for k in range(8)]), reads=[hTb], writes=[PB[2]])
                        kn, knb = groupnorm(1, 4, rep512_t[:, 3, :], rings)
                        kb, kbb = qb_ring.get()
                        S.emit("act", lambda: nc.scalar.copy(out=kb[:], in_=kn[:]), reads=[knb], writes=[kbb])
                        transposes(lambda h: kb[:, h * 128:(h + 1) * 128], [kbb], 4, 3, kmT[j][:, :, t * 128:(t + 1) * 128], [CONST])
                        S.emit("dve", lambda: nc.vector.tensor_copy(out=vm[j][:, t, :, 0:128], in_=ps[:, 1024:1536].rearrange("p (h d) -> p h d", h=4)),
                               reads=[PB[2]], writes=[CONST])
                S.barrier()

            if 1 in phases:
                with ExitStack() as p1:
                    cs_t = sb(p1, "cs_t", [128, max(TA, TBK, TBQ) // 128, 24], F32)
                    x_ring = Ring(p1, nc, "x1_", [128, D], F32, 5)
                    hb_ring = Ring(p1, nc, "hb1_", [128, D], BF16, 5)
                    hT_ring = Ring(p1, nc, "hT1_", [128, 8, 256], BF16, 3, nb=2)
                    qs_ring = Ring(p1, nc, "qs1_", [128, 512], F32, 8)
                    sq_ring = Ring(p1, nc, "sq1_", [128, 512], F32, 3)
                    g8_ring = Ring(p1, nc, "g81_", [128, 24], F32, 12)
                    qb_ring = Ring(p1, nc, "qb1_", [128, 512], BF16, 8, nb=2)
                    rt_ring = Ring(p1, nc, "rt1_", [128, 4, 64], F32, 4)
                    qkT_ring = Ring(p1, nc, "qkT1_", [128, 4, 256], BF16, 4)
                    vb_ring = Ring(p1, nc, "vb1_", [128, 4, 2, 130], BF16, 3)
                    csb_ring = Ring(p1, nc, "csb1_", [128, 256], F32, 4)
                    zts = [sb(p1, "zt%d" % i, [128, 4, 258], F32) for i in range(2)]
                    bts = [sb(p1, "bt%d" % i, [128, 4, 257], F32) for i in range(2)]
                    zc = sb(p1, "zc", [128, 4, 2], F32)
                    bc = sb(p1, "bc", [128, 4, 1], F32)
                    zh = sb(p1, "zh", [128, 4, 2], F32)
                    acc = sb(p1, "acc", [128, 256], F32)
                    fl = sb(p1, "fl", [128, 8, 4], F32)
                    yl = sb(p1, "yl", [128, 4], BF16)
                    y_ring = Ring(p1, nc, "y1_", [128, 4, 256], BF16, 3)
                    ZBs, BBs = [Buf(), Buf()], [Buf(), Buf()]
                    ZCB, ZHB, ACCB = Buf(), Buf(), Buf()
                    for it_, ib_ in vb_ring.items:
                        S.emit("pool", lambda: nc.gpsimd.memset(it_[:], 1.0), writes=[ib_])
                    S.barrier()
                    tm_banks = [1, 2, 3]
                    tm_i = [0]
                    cv_banks = [4, 5, 6]
                    cv_i = [0]

                    def nxt(lst, ctr):
                        b = lst[ctr[0] % len(lst)]
                        ctr[0] += 1
                        return b

                    def halo_z():
                        xt, xbuf = x_ring.get()
                        S.emit("pool", lambda: nc.gpsimd.memset(xt[:], 0.0), writes=[xbuf])
                        S.dma("sp", xt[0:2, :], halo, writes=[xbuf])
                        hb, hbuf = hb_ring.get()
                        rms(xt, xbuf, hb, hbuf)
                        hT, hTbs = hT_ring.get()
                        transposes(lambda k: hb[:, k * 128:(k + 1) * 128], [hbuf], 8, 0, hT[:, :, 0:128], [hTbs[0]])
                        for j in range(4):
                            b1, b2 = nxt(cv_banks, cv_i), nxt(cv_banks, cv_i)
                            S.emit("pe", mm_group(ps[:, b1 * 512:b1 * 512 + 2], [(w_in[:, k, 512 + j * 128:512 + (j + 1) * 128], hT[:, k, 0:2]) for k in range(8)]),
                                   reads=[hTbs[0]], writes=[PB[b1]])
                            S.emit("pe", mm_group(ps[:, b2 * 512:b2 * 512 + 2], [(w_in[:, k, 1024 + j * 128:1024 + (j + 1) * 128], hT[:, k, 0:2]) for k in range(8)]),
                                   reads=[hTbs[0]], writes=[PB[b2]])
                            S.emit("act", lambda: nc.scalar.copy(out=fl[:, j, 0:2], in_=ps[:, b1 * 512:b1 * 512 + 2]), reads=[PB[b1]], writes=[ZHB])
                            S.emit("dve", lambda: nc.vector.tensor_tensor(out=zh[:, j, :], in0=fl[:, j, 0:2], in1=ps[:, b2 * 512:b2 * 512 + 2], op=ALU.mult),
                                   reads=[ZHB, PB[b2]], writes=[ZHB])

                    def stream(T, x_ap, cskey, do_conv, do_q, do_k, do_v, job, mix_col0, has_halo):
                        S.dma("sp", cs_t[:, 0:T // 128, :].rearrange("p a b -> p (a b)"), cs_d[cskey], writes=[CONST])
                        S.barrier()
                        if do_conv:
                            if has_halo:
                                halo_z()
                            else:
                                S.emit("dve", lambda: nc.vector.memset(zh[:], 0.0), writes=[ZHB])
                        nblk = T // 256

                        def block_gen(blk):
                            s = blk * 256
                            xts = []
                            for t in range(2):
                                xt, xbuf = x_ring.get()
                                tok0 = s + t * 128
                                S.dma("sp", xt[:], x_ap[tok0:tok0 + 128, :], writes=[xbuf], nbytes=524288)
                                xts.append((xt, xbuf))
                            yield
                            hbs = []
                            for t in range(2):
                                hb, hbuf = hb_ring.get()
                                rms(xts[t][0], xts[t][1], hb, hbuf)
                                hbs.append((hb, hbuf))
                            yield
                            hT, hTbs = hT_ring.get()
                            for t in range(2):
                                hb, hbuf = hbs[t]
                                transposes(lambda k: hb[:, k * 128:(k + 1) * 128], [hbuf], 8, 0, hT[:, :, t * 128:(t + 1) * 128], [hTbs[t]],
                                           evac=("dve" if t == 0 else "act"))
                            yield
                            ctxs = {}
                            vb = None
                            if do_v:
                                vb, vbb = vb_ring.get()
                            for t in range(2):
                                for which in ("q", "k"):
                                    if (which == "q" and not do_q) or (which == "k" and not do_k):
                                        continue
                                    c0 = 1536 if which == "q" else 2048
                                    bank = nxt(tm_banks, tm_i)
                                    S.emit("pe", mm_group(ps[:, bank * 512:(bank + 1) * 512],
                                                          [(hT[:, k, t * 128:(t + 1) * 128], w_in[:, k, c0:c0 + 512]) for k in range(8)]),
                                           reads=[hTbs[t]], writes=[PB[bank]])
                                    ctxs[(which, t)] = gn1(bank, 8, qs_ring, sq_ring, g8_ring)
                                if do_v:
                                    bank = nxt(tm_banks, tm_i)
                                    S.emit("pe", mm_group(ps[:, bank * 512:(bank + 1) * 512],
                                                          [(hT[:, k, t * 128:(t + 1) * 128], w_in[:, k, 2560:3072]) for k in range(8)]),
                                           reads=[hTbs[t]], writes=[PB[bank]])
                                    S.emit("dve", lambda: nc.vector.tensor_copy(out=vb[:, :, t, 0:128], in_=ps[:, bank * 512:(bank + 1) * 512].rearrange("p (h d) -> p h d", h=4)),
                                           reads=[PB[bank]], writes=[vbb], n=512)
                            if do_v:
                                S.dma("sp", v_d[job].rearrange("h p k c -> p h k c")[:, :, blk * 2:blk * 2 + 2, :], vb[:], reads=[vbb])
                            if do_conv:
                                cur = blk % 2
                                zt, bt, ZB, BB = zts[cur], bts[cur], ZBs[cur], BBs[cur]
                                if blk == 0:
                                    S.emit("pool", lambda: nc.gpsimd.memset(zt[:, :, 0:1], 0.0), reads=[ZB], writes=[ZB])
                                    S.emit("pool", lambda: nc.gpsimd.tensor_copy(out=zt[:, :, 1:2], in_=zh[:, :, 0:1]), reads=[ZHB], writes=[ZB])
                                    S.emit("pool", lambda: nc.gpsimd.memset(bt[:, :, 0:1], 0.0), reads=[BB], writes=[BB])
                                else:
                                    S.emit("pool", lambda: nc.gpsimd.tensor_copy(out=zt[:, :, 0:2], in_=zc[:]), reads=[ZCB], writes=[ZB])
                                    S.emit("pool", lambda: nc.gpsimd.tensor_copy(out=bt[:, :, 0:1], in_=bc[:]), reads=[ZCB], writes=[BB])
                                for j in range(4):
                                    bC, bV, bB = nxt(cv_banks, cv_i), nxt(cv_banks, cv_i), nxt(cv_banks, cv_i)
                                    for bnk, c0 in ((bC, 512), (bV, 1024), (bB, 0)):
                                        S.emit("pe", mm_group(ps[:, bnk * 512:bnk * 512 + 256],
                                                              [(w_in[:, k, c0 + j * 128:c0 + (j + 1) * 128], hT[:, k, :]) for k in range(8)]),
                                               reads=hTbs, writes=[PB[bnk]])
                                    cs_, csb_ = csb_ring.get()
                                    S.emit("act", lambda: nc.scalar.copy(out=cs_[:], in_=ps[:, bC * 512:bC * 512 + 256]), reads=[PB[bC]], writes=[csb_])
                                    S.emit("dve", lambda: nc.vector.tensor_tensor(out=zt[:, j, 2:258], in0=cs_[:], in1=ps[:, bV * 512:bV * 512 + 256], op=ALU.mult),
                                           reads=[csb_, PB[bV]], writes=[ZB])
                                    S.emit("act", lambda: nc.scalar.copy(out=bt[:, j, 1:257], in_=ps[:, bB * 512:bB * 512 + 256]), reads=[PB[bB]], writes=[BB])
                                S.emit("pool", lambda: nc.gpsimd.tensor_copy(out=zc[:], in_=zt[:, :, 256:258]), reads=[ZB], writes=[ZCB])
                                S.emit("pool", lambda: nc.gpsimd.tensor_copy(out=bc[:], in_=bt[:, :, 256:257]), reads=[BB], writes=[ZCB])
                            yield
                            qbs = {}
                            for t in range(2):
                                tt = blk * 2 + t
                                for which in ("q", "k"):
                                    if (which, t) not in ctxs:
                                        continue
                                    qn, qnb = gn2(ctxs[(which, t)], 8, rep512_t[:, 0 if which == "q" else 1, :])
                                    qb, qbbs = qb_ring.get()
                                    q3 = qn[:].rearrange("p (g d) -> p g d", g=8)
                                    b3 = qb[:].rearrange("p (g d) -> p g d", g=8)
                                    S.emit("act", lambda: nc.scalar.copy(out=b3[:, :, 16:64], in_=q3[:, :, 16:64]), reads=[qnb], writes=[qbbs[0]], n=384)
                                    rt, rtb = rt_ring.get()
                                    rtf = rt[:].rearrange("p a b -> p (a b)")
                                    t1 = rtf[:, 0:128].rearrange("p (g d) -> p g d", g=8)
                                    t2 = rtf[:, 128:256].rearrange("p (g d) -> p g d", g=8)
                                    r1, r2 = q3[:, :, 0:8], q3[:, :, 8:16]
                                    cos4 = cs_t[:, tt, 0:8].unsqueeze(1).unsqueeze(1).broadcast_to([128, 8, 2, 8])
                                    sinb = cs_t[:, tt, 8:16].unsqueeze(1).broadcast_to([128, 8, 8])
                                    nsinb = cs_t[:, tt, 16:24].unsqueeze(1).broadcast_to([128, 8, 8])
                                    S.emit("dve", lambda: nc.vector.tensor_tensor(out=t1.rearrange("p g (h d) -> p g h d", h=2),
                                                                                  in0=q3[:, :, 0:16].rearrange("p g (h d) -> p g h d", h=2), in1=cos4, op=ALU.mult),
                                           reads=[qnb], writes=[rtb], n=128)
                                    S.emit("dve", lambda: nc.vector.tensor_tensor(out=t2[:, :, 0:8], in0=r2, in1=nsinb, op=ALU.mult), reads=[qnb], writes=[rtb], n=64)
                                    S.emit("dve", lambda: nc.vector.tensor_tensor(out=t2[:, :, 8:16], in0=r1, in1=sinb, op=ALU.mult), reads=[qnb], writes=[rtb], n=64)
                                    S.emit("dve", lambda: nc.vector.tensor_tensor(out=b3[:, :, 0:16], in0=t1, in1=t2, op=ALU.add), reads=[rtb], writes=[qbbs[1]], n=128)
                                    qbb = qbbs
                                    qbs[(which, t)] = (qb, qbb)
                            if do_conv:
                                yb, ybb = y_ring.get()
                                for j in range(4):
                                    w0, w1, w2, bb_ = (cwb_t[:, j * 4 + i:j * 4 + i + 1] for i in range(4))
                                    S.emit("dve", lambda: nc.vector.tensor_scalar(out=acc[:], in0=zt[:, j, 1:257], scalar1=w1, scalar2=bb_, op0=ALU.mult, op1=ALU.add),
                                           reads=[ZB], writes=[ACCB])
                                    S.emit("dve", lambda: nc.vector.scalar_tensor_tensor(out=acc[:], in0=zt[:, j, 0:256], scalar=w0, in1=acc[:], op0=ALU.mult, op1=ALU.add),
                                           reads=[ZB, ACCB], writes=[ACCB])
                                    S.emit("dve", lambda: nc.vector.scalar_tensor_tensor(out=acc[:], in0=zt[:, j, 2:258], scalar=w2, in1=acc[:], op0=ALU.mult, op1=ALU.add),
                                           reads=[ZB, ACCB], writes=[ACCB])
                                    S.emit("dve", lambda: nc.vector.tensor_tensor(out=yb[:, j, :], in0=bt[:, j, 0:256], in1=acc[:], op=ALU.mult),
                                           reads=[BB, ACCB], writes=[ybb])
                                jj0 = 1 if blk == 0 else 0
                                c_lo = mix_col0 + s - 1 + jj0
                                S.dma("sp", mixT_p[:, 0:4, c_lo:mix_col0 + s + 255], yb[:, :, jj0:256], reads=[ybb])
                            yield
                            for which, d_ in (("q", qT_d), ("k", kT_d)):
                                if (which, 0) not in qbs:
                                    continue
                                dstT, dstTb = qkT_ring.get()
                                for t in range(2):
                                    qb, qbb = qbs[(which, t)]
                                    transposes(lambda h: qb[:, h * 128:(h + 1) * 128], list(qbb), 4, 7, dstT[:, :, t * 128:(t + 1) * 128], [dstTb], evac="act")
                                S.dma("sp", d_[job].rearrange("h p t -> p h t")[:, :, s:s + 256], dstT[:], reads=[dstTb])

                        run_pipeline((block_gen(b) for b in range(nblk)), 6)
                        if do_conv:
                            FB = Buf()
                            cw3 = cwb_t[:].rearrange("p (j i) -> p j i", i=4)
                            f = lambda i: fl[:, i, :]
                            S.emit("dve", lambda: nc.vector.tensor_tensor(out=f(0), in0=zc[:, :, 1], in1=cw3[:, :, 1], op=ALU.mult), reads=[ZCB], writes=[FB])
                            S.emit("dve", lambda: nc.vector.tensor_tensor(out=f(1), in0=f(0), in1=cw3[:, :, 3], op=ALU.add), reads=[FB], writes=[FB])
                            S.emit("dve", lambda: nc.vector.tensor_tensor(out=f(2), in0=zc[:, :, 0], in1=cw3[:, :, 0], op=ALU.mult), reads=[ZCB, FB], writes=[FB])
                            S.emit("dve", lambda: nc.vector.tensor_tensor(out=f(3), in0=f(1), in1=f(2), op=ALU.add), reads=[FB], writes=[FB])
                            S.emit("dve", lambda: nc.vector.tensor_tensor(out=f(4), in0=zh[:, :, 1], in1=cw3[:, :, 2], op=ALU.mult), reads=[ZHB, FB], writes=[FB])
                            S.emit("dve", lambda: nc.vector.tensor_tensor(out=f(5), in0=f(3), in1=f(4), op=ALU.add), reads=[FB], writes=[FB])
                            S.emit("dve", lambda: nc.vector.tensor_tensor(out=yl[:], in0=f(5), in1=bc[:, :, 0], op=ALU.mult), reads=[FB, ZCB], writes=[FB])
                            S.dma("sp", mixT_p[:, 0:4, mix_col0 + T - 1:mix_col0 + T], yl[:].unsqueeze(2), reads=[FB], noncontig=True)
                        S.barrier()

                    stream(TA, xa, "a", True, True, True, True, 0, 0, False)
                    stream(TBK, xbk, "bk", False, False, True, True, 1, 0, False)
                    stream(TBQ, xbq, "bq", True, True, False, False, 1, TA, True)
                    S.barrier()

            pw.close()
            if 2 in phases:
                with ExitStack() as p2:
                    S.flush()
                    S.reorder = False
                    kbuf = sb(p2, "kbuf", [128, KVCAP], BF16)
                    vbuf = sb(p2, "vbuf", [128, (KVCAP // 128) * 130], BF16)
                    vbuf3 = vbuf[:].rearrange("p (n c) -> p n c", c=130)
                    qb_ring = Ring(p2, nc, "qblk", [128, 4, 512], BF16, 3)
                    pT_ring = Ring(p2, nc, "pT", [128, 1024], BF16, 3)
                    osb = sb(p2, "osb", [128, 8, 129], F32)
                    ta = sb(p2, "ta", [128, 4, 128], F32)
                    tb = sb(p2, "tb", [128, 4, 128], F32)
                    onb = sb(p2, "onb", [128, 4, 128], BF16)
                    oT_ring = Ring(p2, nc, "oT", [128, 512], BF16, 2)
                    OSB, TAB, TBB, ONB = Buf(), Buf(), Buf(), Buf()
                    scale = 64 ** -0.5
                    SLOT = KVCAP // 2
                    passes = []
                    hpa = min(4, SLOT // TA)
                    for h0 in range(0, 4, hpa):
                        passes.append((0, list(range(h0, h0 + hpa)), TA, TA, 0))
                    hpb = min(4, SLOT // TBK)
                    for h0 in range(0, 4, hpb):
                        passes.append((1, list(range(h0, h0 + hpb)), TBK, TBQ, TA))
                    KBs = [[Buf() for _ in range(4)] for _ in range(2)]
                    VBs = [[Buf() for _ in range(4)] for _ in range(2)]

                    def load_kv(pi):
                        job, heads, Nk, Nq, col0 = passes[pi]
                        sl = pi % 2
                        KT = Nk // 128
                        for hi, h in enumerate(heads):
                            for c0 in range(0, Nk, 4096):
                                cw = min(4096, Nk - c0)
                                S.dma("sp", kbuf[:, sl * SLOT + hi * Nk + c0: sl * SLOT + hi * Nk + c0 + cw], kT_d[job][h, :, c0:c0 + cw],
                                      writes=[KBs[sl][hi]], nbytes=1048576)
                            for k0 in range(0, KT, 32):
                                kw = min(32, KT - k0)
                                S.dma("sp", vbuf3[:, sl * (SLOT // 128) + hi * KT + k0: sl * (SLOT // 128) + hi * KT + k0 + kw, :],
                                      v_d[job][h, :, k0:k0 + kw, :], writes=[VBs[sl][hi]], nbytes=1064960)

                    q0_pre = {}
                    load_kv(0)
                    if len(passes) > 1:
                        load_kv(1)
                    for pi, (job, heads, Nk, Nq, col0) in enumerate(passes):
                        nh = len(heads)
                        KT = Nk // 128
                        sl = pi % 2
                        KB, VB = KBs[sl], VBs[sl]
                        kbase = sl * SLOT
                        vbase = sl * (SLOT // 128)
                        nqb = Nq // 512
                        qblks = {}

                        def load_q(qb):
                            qt, qtb = qb_ring.get()
                            S.dma("sp", qt[:, 0:nh, :], qT_d[job].rearrange("h p t -> p h t")[:, heads[0]:heads[0] + nh, qb * 512:(qb + 1) * 512], writes=[qtb])
                            qblks[qb] = (qt, qtb)

                        its = [(qb, hi, kt) for qb in range(nqb) for hi in range(nh) for kt in range(KT)]
                        if pi in q0_pre:
                            qblks[0] = q0_pre.pop(pi)
                        else:
                            load_q(0)
                        state = {}

                        def qk(i):
                            qb, hi, kt = its[i]
                            if hi == 0 and kt == 0 and qb + 1 < nqb:
                                load_q(qb + 1)
                            qt, qtb = qblks[qb]
                            b0 = (i % 2) * 2
                            fns = []
                            for c in range(2):
                                fns.append(lambda c=c: nc.tensor.matmul(ps[:, (b0 + c) * 512:(b0 + c + 1) * 512],
                                                                        lhsT=kbuf[c * 64:(c + 1) * 64, kbase + hi * Nk + kt * 128: kbase + hi * Nk + (kt + 1) * 128],
                                                                        rhs=qt[c * 64:(c + 1) * 64, hi, :], start=True, stop=True))
                            S.emit("pe", fns, reads=[qtb, KB[hi]], writes=[PB[b0], PB[b0 + 1]], cost=330.0)

                        def ex(i):
                            b0 = (i % 2) * 2
                            pt, ptb = pT_ring.get()
                            state[i] = (pt, ptb)
                            S.emit("act", lambda: nc.scalar.activation(out=pt[:], in_=ps[:, b0 * 512:(b0 + 2) * 512], func=AF.Exp, scale=scale),
                                   reads=[PB[b0], PB[b0 + 1]], writes=[ptb], n=1024)

                        def av(i):
                            qb, hi, kt = its[i]
                            pt, ptb = state.pop(i)
                            fns = []
                            for c in range(2):
                                for qs in range(4):
                                    a = c * 4 + qs
                                    col = (4 + a // 3) * 512 + (a % 3) * 129
                                    st_ = (kt == 0 and a % 3 == 0)
                                    fns.append(lambda c=c, qs=qs, col=col, st_=st_: nc.tensor.matmul(
                                        ps[:, col:col + 129], lhsT=pt[:, c * 512 + qs * 128: c * 512 + (qs + 1) * 128],
                                        rhs=vbuf3[:, vbase + hi * KT + kt, 0:129], start=st_, stop=(kt == KT - 1), skip_group_check=True))
                            S.emit("pe", fns, reads=[ptb, VB[hi]], writes=[PB[4], PB[5], PB[6]], cost=650.0)
                            if kt == KT - 1:
                                finish(qb, hi)

                        def finish(qb, hi):
                            h = heads[hi]
                            for bnk, a0, na in ((4, 0, 3), (5, 3, 3), (6, 6, 2)):
                                S.emit("dve", lambda: nc.vector.tensor_copy(out=osb[:, a0:a0 + na, :],
                                                                            in_=ps[:, bnk * 512: bnk * 512 + na * 129].rearrange("p (a c) -> p a c", c=129)),
                                       reads=[PB[bnk]], writes=[OSB], n=387)
                            stt, stb = st8_ring.get()
                            sums = osb[:, :, 128:129].rearrange("p a c -> p (a c)")
                            S.emit("dve", lambda: nc.vector.reciprocal(out=stt[:, 0:8], in_=sums), reads=[OSB], writes=[stb])
                            S.emit("dve", lambda: nc.vector.tensor_scalar(out=stt[:, 8:12], in0=stt[:, 4:8], scalar1=nlam, scalar2=None, op0=ALU.mult),
                                   reads=[stb], writes=[stb])
                            S.emit("dve", lambda: nc.vector.tensor_tensor(out=ta[:], in0=osb[:, 0:4, 0:128], in1=stt[:, 0:4].unsqueeze(2).broadcast_to([128, 4, 128]), op=ALU.mult),
                                   reads=[OSB, stb], writes=[TAB], n=512)
                            S.emit("dve", lambda: nc.vector.tensor_tensor(out=tb[:], in0=osb[:, 4:8, 0:128], in1=stt[:, 8:12].unsqueeze(2).broadcast_to([128, 4, 128]), op=ALU.mult),
                                   reads=[OSB, stb], writes=[TBB], n=512)
                            S.emit("dve", lambda: nc.vector.tensor_tensor(out=ta[:], in0=ta[:], in1=tb[:], op=ALU.add), reads=[TAB, TBB], writes=[TAB], n=512)
                            S.emit("dve", lambda: nc.vector.tensor_tensor(out=tb[:], in0=ta[:], in1=ta[:], op=ALU.mult), reads=[TAB], writes=[TBB], n=512)
                            S.emit("dve", lambda: nc.vector.tensor_reduce(out=stt[:, 12:16], in_=tb[:], axis=AX.X, op=ALU.add), reads=[TBB], writes=[stb], n=512)
                            S.emit("pool", lambda: nc.gpsimd.tensor_scalar(out=stt[:, 16:20], in0=stt[:, 12:16], scalar1=1.0 / 128, scalar2=EPS, op0=ALU.mult, op1=ALU.add),
                                   reads=[stb], writes=[stb])
                            S.emit("pool", lambda: nc.gpsimd.tensor_tensor(out=stt[:, 20:24], in0=stt[:, 16:20], in1=nhalf[:, 0:4], op=ALU.pow), reads=[stb], writes=[stb], cost=1150.0)
                            S.emit("dve", lambda: nc.vector.tensor_tensor(out=tb[:], in0=ta[:], in1=stt[:, 20:24].unsqueeze(2).broadcast_to([128, 4, 128]), op=ALU.mult),
                                   reads=[TAB, stb], writes=[TBB], n=512)
                            S.emit("dve", lambda: nc.vector.scalar_tensor_tensor(out=onb[:], in0=tb[:], scalar=1.0 - LAM_INIT,
                                                                                 in1=rep128_t[:].unsqueeze(1).broadcast_to([128, 4, 128]), op0=ALU.mult, op1=ALU.mult),
                                   reads=[TBB], writes=[ONB], n=512)
                            def fin_b():
                                ot, otb = oT_ring.get()
                                transposes(lambda qs: onb[:, qs, :], [ONB], 4, 7, ot[:].rearrange("p (k t) -> p k t", k=4), [otb])
                                S.dma("sp", mixT_d[4 + h, :, col0 + qb * 512: col0 + (qb + 1) * 512], ot[:], reads=[otb])
                            deferred.append(fin_b)

                        n = len(its)
                        deferred = []
                        qk(0)
                        if n > 1:
                            qk(1)
                        for i in range(n):
                            ex(i)
                            if i + 2 < n:
                                qk(i + 2)
                            if deferred and its[i][2] == min(6, KT - 2):
                                deferred.pop(0)()
                            av(i)
                        while deferred:
                            deferred.pop(0)()
                        if pi + 1 < len(passes):
                            j2, h2, _, _, _ = passes[pi + 1]
                            qt2, qtb2 = qb_ring.get()
                            S.dma("sp", qt2[:, 0:len(h2), :], qT_d[j2].rearrange("h p t -> p h t")[:, h2[0]:h2[0] + len(h2), 0:512], writes=[qtb2])
                            q0_pre[pi + 1] = (qt2, qtb2)
                        if pi + 2 < len(passes):
                            load_kv(pi + 2)
                    S.barrier()

            S.flush()
            S.reorder = True
            if 3 in phases:
                with ExitStack() as p3:
                    w_out = sb(p3, "w_out_t", [128, 8, 1024], BF16)
                    wmq = sb(p3, "wmq_t", [128, 8, 512], BF16)
                    wmo = sb(p3, "wmo_t", [128, 4, 1024], BF16)
                    with ExitStack() as p3s:
                        stg_ring = Ring(p3s, nc, "stg3_", [128, 1024], F32, 3)
                        load_weight(w_out, w_out_d, 8, 1024, None, stg_ring)
                        load_weight(wmq, wm_q_d, 8, 512, 1, stg_ring)
                        load_weight(wmo, wm_o_d, 4, 1024, None, stg_ring)
                        S.barrier()
                    x_ring = Ring(p3, nc, "x3_", [128, D], F32, 14)
                    mix_ring = Ring(p3, nc, "mix3_", [128, 8, 256], BF16, 2)
                    hb_ring = Ring(p3, nc, "hb3_", [128, D], BF16, 4)
                    hT_ring = Ring(p3, nc, "hT3_", [128, 8, 256], BF16, 3, nb=2)
                    qs_ring = Ring(p3, nc, "qs3_", [128, 512], F32, 4)
                    sq_ring = Ring(p3, nc, "sq3_", [128, 512], F32, 3)
                    g8_ring = Ring(p3, nc, "g83_", [128, 24], F32, 8)
                    qb_ring = Ring(p3, nc, "qb3_", [128, 512], BF16, 4)
                    qmT_ring = Ring(p3, nc, "qmT3_", [128, 4, 256], BF16, 3)
                    pT_ring = Ring(p3, nc, "pT3_", [128, 2, 256], BF16, 6)
                    osb_ring = Ring(p3, nc, "osb3_", [128, 8, 129], F32, 2)
                    om_ring = Ring(p3, nc, "om3_", [128, 8, 128], BF16, 2)
                    omT_ring = Ring(p3, nc, "omT3_", [128, 4, 256], BF16, 3)
                    mscale = 128 ** -0.5
                    nsb = NT // 256

                    def sb_gen(sbi):
                        g0 = sbi * 256
                        job = 0 if g0 < TA else 1
                        mx, mxb = mix_ring.get()
                        S.dma("sp", mx[:], mixT_p[:, :, g0:g0 + 256], writes=[mxb])
                        xs = []
                        for t in range(2):
                            xt, xbuf = x_ring.get()
                            r0 = g0 + t * 128
                            src = xa[r0:r0 + 128, :] if r0 < TA else xbq[r0 - TA:r0 - TA + 128, :]
                            S.dma("sp", xt[:], src, writes=[xbuf], nbytes=524288)
                            xs.append((xt, xbuf))
                        yield
                        hbs = []
                        for t in range(2):
                            xt, xbuf = xs[t]
                            b0 = 2 * t
                            for half in range(2):
                                S.emit("pe", mm_group(ps[:, (b0 + half) * 512:(b0 + half + 1) * 512],
                                                      [(mx[:, k, t * 128:(t + 1) * 128], w_out[:, k, half * 512:(half + 1) * 512]) for k in range(8)]),
                                       reads=[mxb], writes=[PB[b0 + half]])
                            S.emit("dve", lambda: nc.vector.tensor_tensor(out=xt[:], in0=xt[:], in1=ps[:, b0 * 512:(b0 + 2) * 512], op=ALU.add),
                                   reads=[xbuf, PB[b0], PB[b0 + 1]], writes=[xbuf], n=1024)
                            hb, hbuf = hb_ring.get()
                            rms(xt, xbuf, hb, hbuf)
                            hbs.append((hb, hbuf))
                        yield
                        hT, hTbs = hT_ring.get()
                        for t in range(2):
                            hb, hbuf = hbs[t]
                            transposes(lambda k: hb[:, k * 128:(k + 1) * 128], [hbuf], 8, 4, hT[:, :, t * 128:(t + 1) * 128], [hTbs[t]],
                                       evac=("dve" if t == 0 else "act"))
                        ctxs = []
                        for t in range(2):
                            bank = 5 + t
                            S.emit("pe", mm_group(ps[:, bank * 512:(bank + 1) * 512], [(hT[:, k, t * 128:(t + 1) * 128], wmq[:, k, :]) for k in range(8)]),
                                   reads=[hTbs[t]], writes=[PB[bank]])
                            ctxs.append(gn1(bank, 4, qs_ring, sq_ring, g8_ring))
                        yield
                        qmT, qmTb = qmT_ring.get()
                        for t in range(2):
                            qn, qnb = gn2(ctxs[t], 4, rep512_t[:, 2, :])
                            qb, qbb = qb_ring.get()
                            S.emit("act", lambda: nc.scalar.copy(out=qb[:], in_=qn[:]), reads=[qnb], writes=[qbb], n=512)
                            transposes(lambda h: qb[:, h * 128:(h + 1) * 128], [qbb], 4, 7, qmT[:, :, t * 128:(t + 1) * 128], [qmTb], evac="act")
                        yield
                        started = set()
                        pts = {}

                        def scores(h):
                            sbank = 5 + h % 2
                            fns = []
                            for half in range(2):
                                fns.append(lambda half=half: nc.tensor.matmul(ps[:, sbank * 512 + half * 256: sbank * 512 + (half + 1) * 256],
                                                                              lhsT=kmT[job][:, h, half * 128:(half + 1) * 128], rhs=qmT[:, h, :],
                                                                              start=(half == 0), stop=True, skip_group_check=True))
                            S.emit("pe", fns, reads=[qmTb], writes=[PB[sbank]], cost=300.0)
                            pt, ptb = pT_ring.get()
                            S.emit("act", lambda: nc.scalar.activation(out=pt[:].rearrange("p a b -> p (a b)"), in_=ps[:, sbank * 512:(sbank + 1) * 512],
                                                                       func=AF.Exp, scale=mscale), reads=[PB[sbank]], writes=[ptb], n=512)
                            pts[h] = (pt, ptb)

                        def avm(h):
                            pt, ptb = pts.pop(h)
                            fns = []
                            wb = set()
                            for t in range(2):
                                a = t * 4 + h
                                bnk = a // 3
                                col = bnk * 512 + (a % 3) * 129
                                wb.add(bnk)
                                for half in range(2):
                                    st_ = bnk not in started
                                    started.add(bnk)
                                    fns.append(lambda t=t, half=half, col=col, st_=st_: nc.tensor.matmul(
                                        ps[:, col:col + 129], lhsT=pt[:, half, t * 128:(t + 1) * 128], rhs=vm[job][:, half, h, 0:129],
                                        start=st_, stop=(half == 1), skip_group_check=True))
                            S.emit("pe", fns, reads=[ptb], writes=[PB[b] for b in sorted(wb)], cost=350.0)

                        scores(0)
                        for h in range(4):
                            if h + 1 < 4:
                                scores(h + 1)
                            avm(h)
                        osb, OSB = osb_ring.get()
                        for bnk, a0, na in ((0, 0, 3), (1, 3, 3), (2, 6, 2)):
                            S.emit("dve", lambda: nc.vector.tensor_copy(out=osb[:, a0:a0 + na, :],
                                                                        in_=ps[:, bnk * 512: bnk * 512 + na * 129].rearrange("p (a c) -> p a c", c=129)),
                                   reads=[PB[bnk]], writes=[OSB], n=387)
                        yield
                        stt, stb = g8_ring.get()
                        S.emit("dve", lambda: nc.vector.reciprocal(out=stt[:, 0:8], in_=osb[:, :, 128:129].rearrange("p a c -> p (a c)")), reads=[OSB], writes=[stb])
                        om, OMB = om_ring.get()
                        S.emit("dve", lambda: nc.vector.tensor_tensor(out=om[:], in0=osb[:, :, 0:128], in1=stt[:, 0:8].unsqueeze(2).broadcast_to([128, 8, 128]), op=ALU.mult),
                               reads=[OSB, stb], writes=[OMB], n=1024)
                        omT, omTb = omT_ring.get()
                        for t in range(2):
                            transposes(lambda h: om[:, t * 4 + h, :], [OMB], 4, 7, omT[:, :, t * 128:(t + 1) * 128], [omTb], evac="act")
                        yield
                        for t in range(2):
                            xt, xbuf = xs[t]
                            b0 = 3 if t == 0 else 5
                            for half in range(2):
                                S.emit("pe", mm_group(ps[:, (b0 + half) * 512:(b0 + half + 1) * 512],
                                                      [(omT[:, k, t * 128:(t + 1) * 128], wmo[:, k, half * 512:(half + 1) * 512]) for k in range(4)]),
                                       reads=[omTb], writes=[PB[b0 + half]])
                            S.emit("dve", lambda: nc.vector.tensor_tensor(out=xt[:], in0=xt[:], in1=ps[:, b0 * 512:(b0 + 2) * 512], op=ALU.add),
                                   reads=[xbuf, PB[b0], PB[b0 + 1]], writes=[xbuf], n=1024)
                            S.dma("sp", x2s_d[g0 + t * 128: g0 + (t + 1) * 128, :], xt[:], reads=[xbuf], nbytes=524288)

                    run_pipeline((sb_gen(i) for i in range(nsb)), 7)
                    S.barrier()

            if 4 in phases:
                with ExitStack() as p4:
                    S.flush()
                    S.reorder = True
                    ff1 = sb(p4, "ff1_t", [128, 8, 4096], BF16)
                    ff2 = sb(p4, "ff2_t", [128, 32, 1024], BF16)
                    stg_ring = Ring(p4, nc, "stg4_", [128, 1024], F32, 2)
                    W1B = {(k, g): Buf() for k in range(8) for g in range(4)}
                    W2B = {(f, 0): Buf() for f in range(32)}
                    x_ring = Ring(p4, nc, "x4_", [128, D], F32, 6)
                    hb_ring = Ring(p4, nc, "hb4_", [128, D], BF16, 4)
                    hT_ring = Ring(p4, nc, "hT4_", [128, 8, 256], BF16, 2)
                    rl_ring = Ring(p4, nc, "rl4_", [128, 256], F32, 2)
                    aT_ring = Ring(p4, nc, "aT4_", [128, 256], BF16, 3)
                    nsb = NT // 256
                    pend = {}
                    src_d = x2s_d if 3 in phases else None

                    def load_sb4(sbi):
                        xs = []
                        for t in range(2):
                            xt, xbuf = x_ring.get()
                            r0 = sbi * 256 + t * 128
                            S.dma("sp", xt[:], src_d[r0:r0 + 128, :], writes=[xbuf], nbytes=524288)
                            xs.append((xt, xbuf))
                        pend[sbi] = xs

                    hTs = {}
                    prea = {}

                    def pre_a(sbi):
                        xs = pend[sbi]
                        hT, hTb = hT_ring.get()
                        hTs[sbi] = (hT, hTb)
                        prea[sbi] = []
                        for t in range(2):
                            xt, xbuf = xs[t]
                            hb, hbuf = hb_ring.get()
                            prea[sbi].append((hb, hbuf, rms_a(xt, xbuf, hb, hbuf)))

                    def pre_b(sbi):
                        xs = pend[sbi]
                        for t in range(2):
                            xt, xbuf = xs[t]
                            hb, hbuf, ctx = prea[sbi][t]
                            rms_b(xt, xbuf, hb, hbuf, ctx)

                    def pre_c(sbi):
                        hT, hTb = hTs[sbi]
                        for t in range(2):
                            hb, hbuf, ctx = prea[sbi][t]
                            transposes(lambda k: hb[:, k * 128:(k + 1) * 128], [hbuf], 8, 0, hT[:, :, t * 128:(t + 1) * 128], [hTb],
                                       evac=("dve" if t == 0 else "act"))
                        del prea[sbi]

                    load_sb4(0)
                    if nsb > 1:
                        load_sb4(1)
                    pre_a(0)
                    pre_b(0)
                    pre_c(0)
                    for g in range(4):
                        load_weight(ff1, w_ff1_d, 8, 4096, 3, stg_ring, wbufs=W1B, c0s=[g * 1024])
                        load_weight(ff2, w_ff2_d, 32, 1024, None, stg_ring, wbufs=W2B, ks=range(g * 8, g * 8 + 8))
                    for sbi in range(nsb):
                        if sbi + 2 < nsb:
                            load_sb4(sbi + 2)
                        xs = pend.pop(sbi)
                        hT, hTb = hTs.pop(sbi)
                        st4 = {}

                        def f1(f):
                            bank = 1 + f % 2
                            S.emit("pe", mm_group(ps[:, bank * 512: bank * 512 + 256], [(ff1[:, k, f * 128:(f + 1) * 128], hT[:, k, :]) for k in range(8)]),
                                   reads=[hTb] + [W1B[(k, f // 8)] for k in range(8)], writes=[PB[bank]])
                            rl, rlb = rl_ring.get()
                            S.emit("act", lambda: nc.scalar.activation(out=rl[:], in_=ps[:, bank * 512: bank * 512 + 256], func=AF.Relu), reads=[PB[bank]], writes=[rlb], n=256)
                            at, atb = aT_ring.get()
                            if f % 2 == 0:
                                S.emit("dve", lambda: nc.vector.tensor_tensor(out=at[:], in0=rl[:], in1=rl[:], op=ALU.mult), reads=[rlb], writes=[atb])
                            else:
                                S.emit("pool", lambda: nc.gpsimd.tensor_tensor(out=at[:], in0=rl[:], in1=rl[:], op=ALU.mult), reads=[rlb], writes=[atb])
                            st4[f] = (at, atb)

                        def f2(f):
                            at, atb = st4.pop(f)
                            fns = []
                            for t in range(2):
                                for half in range(2):
                                    bank = 3 + t * 2 + half
                                    fns.append(lambda t=t, half=half, bank=bank: nc.tensor.matmul(
                                        ps[:, bank * 512:(bank + 1) * 512], lhsT=at[:, t * 128:(t + 1) * 128], rhs=ff2[:, f, half * 512:(half + 1) * 512],
                                        start=(f == 0), stop=(f == 31)))
                            S.emit("pe", fns, reads=[atb, W2B[(f, 0)]], writes=[PB[3], PB[4], PB[5], PB[6]], cost=990.0)

                        f1(0)
                        for f in range(32):
                            if f + 1 < 32:
                                f1(f + 1)
                            f2(f)
                            if sbi + 1 < nsb:
                                if f == 6:
                                    pre_a(sbi + 1)
                                elif f == 12:
                                    pre_b(sbi + 1)
                                elif f == 18:
                                    pre_c(sbi + 1)
                        for t in range(2):
                            xt, xbuf = xs[t]
                            b0 = 3 + t * 2
                            S.emit("dve", lambda: nc.vector.tensor_tensor(out=xt[:], in0=xt[:], in1=ps[:, b0 * 512:(b0 + 2) * 512], op=ALU.add),
                                   reads=[xbuf, PB[b0], PB[b0 + 1]], writes=[xbuf], n=1024)
                            r0 = sbi * 256 + t * 128
                            S.dma("sp", out_d[r0:r0 + 128, :], xt[:], reads=[xbuf], nbytes=524288)
                    S.barrier()
            S.barrier()
    return nc


def _rope_table(pos):
    inv_freq = (np.float32(500000.0) ** (-np.arange(0, 16, 2, dtype=np.float32) / np.float32(16))).astype(np.float32)
    ang = (pos.astype(np.float32)[:, None] * inv_freq[None, :]).astype(np.float32)
    return np.concatenate([np.cos(ang), np.sin(ang), -np.sin(ang)], axis=1).astype(np.float32)


def _pt(tab):
    T = tab.shape[0]
    return np.ascontiguousarray(tab.reshape(T // 128, 128, 24).transpose(1, 0, 2).reshape(128, (T // 128) * 24))


def _col(v):
    return np.ascontiguousarray(np.asarray(v, np.float32).reshape(-1, 128).T)


def make_in_maps(inp, n_cores, TA, TBK, TBQ):
    f = lambda a: np.ascontiguousarray(np.asarray(a, dtype=np.float32))
    xp, xs = f(inp["x_prompt"]), f(inp["x_sample"])
    mp, ms = f(inp["mem_prompt"]), f(inp["mem_sample"])
    shared = {
        "w_in": f(inp["w_in"][0]), "w_out": f(inp["w_out"][0]), "wm_q": f(inp["wm_q"][0]), "wm_kv": f(inp["wm_kv"][0]),
        "wm_o": f(inp["wm_o"][0]), "w_ff1": f(inp["w_ff1"][0]), "w_ff2": f(inp["w_ff2"][0]),
        "gcol": np.ascontiguousarray(np.concatenate([_col(inp["g_mix"][0]), _col(inp["g_memq"][0]), _col(inp["g_memkv"][0]), _col(inp["g_mlp"][0])], axis=1)),
        "rep128": np.ascontiguousarray(np.broadcast_to(f(inp["g_subln"][0])[None, :], (128, 128))),
        "ident": np.eye(128, dtype=np.float32).astype(ml_dtypes.bfloat16),
        "cs_a": _pt(_rope_table(np.arange(TA))), "cs_bk": _pt(_rope_table(np.arange(TBK))),
    }
    cw, cb = f(inp["conv_w"][0]), f(inp["conv_b"][0])
    cwb = np.zeros((128, 4, 4), np.float32)
    for j in range(4):
        cwb[:, j, 0:3] = cw[:, j * 128:(j + 1) * 128].T
        cwb[:, j, 3] = cb[j * 128:(j + 1) * 128]
    shared["cwb"] = cwb.reshape(128, 16)
    rep = np.stack([np.tile(f(inp["q_norm"][0]), 8), np.tile(f(inp["k_norm"][0]), 8),
                    np.tile(f(inp["q_norm_mem"][0]), 4), np.tile(f(inp["k_norm_mem"][0]), 4)], axis=0)
    shared["rep512"] = np.ascontiguousarray(np.broadcast_to(rep.reshape(1, 2048), (128, 2048)))
    lv = np.stack([f(inp["lambda_q1"][0]), f(inp["lambda_k1"][0]), f(inp["lambda_q2"][0]), f(inp["lambda_k2"][0])], axis=0)
    shared["lamv"] = np.ascontiguousarray(np.broadcast_to(lv.reshape(1, 256), (128, 256)))
    nq = TBK // TBQ
    maps = []
    for c in range(n_cores):
        sb_, qi = c // nq, c % nq
        q0 = qi * TBQ
        m = dict(shared)
        m["xa"] = xp[c]
        m["xbk"] = xs[sb_]
        m["xbq"] = np.ascontiguousarray(xs[sb_][q0:q0 + TBQ])
        hl = np.zeros((2, D), np.float32)
        if q0 > 0:
            hl[0] = xs[sb_][q0 - 1]
        if q0 + TBQ < TBK:
            hl[1] = xs[sb_][q0 + TBQ]
        m["halo"] = hl
        m["mema"] = mp[c]
        m["memb"] = ms[sb_]
        m["cs_bq"] = _pt(_rope_table(np.arange(q0, q0 + TBQ)))
        maps.append(m)
    return maps


_NC_CACHE = {}


def kernel(**inputs):
    TA, TBK, TBQ = 8192, 16384, 4096
    key = (TA, TBK, TBQ)
    if key not in _NC_CACHE:
        _NC_CACHE[key] = build_nc(TA, TBK, TBQ)
    nc = _NC_CACHE[key]
    in_maps = make_in_maps(inputs, 8, TA, TBK, TBQ)
    res = run_bass_kernel_spmd(nc, in_maps, core_ids=list(range(8)))
    outs = [np.asarray(r["out"], dtype=np.float32) for r in res.results]
    y_prompt = np.stack([o[:TA] for o in outs], axis=0)
    nq = TBK // TBQ
    y_sample = np.stack([np.concatenate([outs[b * nq + q][TA:] for q in range(nq)], axis=0) for b in range(8 // nq)], axis=0)
    return (y_prompt, y_sample)
```
